# Optimizing a Trainium2 kernel written in Bass

```python
import math
import jax, jax.numpy as jnp
from jax import lax
import numpy as np

D_MODEL = 4096
BATCH = 4
SEQ = 4096
DEPTH = 1

CHUNK = 64
Q_BLOCK = 128
MIX_WIDTH = D_MODEL
DA_WIDTH = MIX_WIDTH // 2
DA_HEAD_DIM = 128
DA_HEADS = DA_WIDTH // (2 * DA_HEAD_DIM)
DA_QK = DA_HEADS * 2 * DA_HEAD_DIM
ROT_DIM = DA_HEAD_DIM // 4
ROPE_THETA = 500000.0
RW_WIDTH = MIX_WIDTH - DA_WIDTH
RW_HEAD_DIM = 64
RW_HEADS = RW_WIDTH // RW_HEAD_DIM
RW_DECAY_LORA = max(32, int(round(1.8 * math.sqrt(RW_WIDTH) / 32)) * 32)
RW_A_LORA = max(32, int(round(1.8 * math.sqrt(RW_WIDTH) / 32)) * 32)
RW_GATE_LORA = max(32, int(round(0.6 * RW_WIDTH ** 0.8 / 32)) * 32)
RW_PROJ = 3 * RW_WIDTH + RW_DECAY_LORA + RW_A_LORA + RW_GATE_LORA
IN_PROJ = 2 * DA_QK + DA_WIDTH + RW_PROJ
RW_LN_EPS = 64e-5
D_FF = int(round(8 * D_MODEL / 3 / 256)) * 256
PLE_DIM = 256
RMS_EPS = 1e-6

kernel_name = "hybrid_diffattn_rwkv7_macaron_block"


def rms_norm(x, g, eps=RMS_EPS):
    xf = x.astype(jnp.float32)
    y = xf * lax.rsqrt(jnp.mean(xf * xf, axis=-1, keepdims=True) + eps)
    return (y * g.astype(jnp.float32)).astype(x.dtype)


def swiglu(h, w_gate, w_up, w_down):
    return (jax.nn.silu(h @ w_gate) * (h @ w_up)) @ w_down


def partial_rotary(t, pos):
    inv = ROPE_THETA ** (-jnp.arange(0, ROT_DIM, 2, dtype=jnp.float32) / ROT_DIM)
    ang = pos.astype(jnp.float32)[:, None] * inv[None, :]
    cos = jnp.cos(ang).astype(t.dtype)
    sin = jnp.sin(ang).astype(t.dtype)
    half = ROT_DIM // 2
    t1, t2, rest = t[..., :half], t[..., half:ROT_DIM], t[..., ROT_DIM:]
    return jnp.concatenate([t1 * cos - t2 * sin, t2 * cos + t1 * sin, rest], axis=-1)


def diff_attention(zq, zk, zv, lam, lambda_init, subln_g):
    B, T = zq.shape[0], zq.shape[1]
    H, d = DA_HEADS, DA_HEAD_DIM
    pos = jnp.arange(T)
    q = partial_rotary(zq.reshape(B, T, H, 2, d).transpose(0, 2, 3, 1, 4), pos)
    k = partial_rotary(zk.reshape(B, T, H, 2, d).transpose(0, 2, 3, 1, 4), pos)
    v = zv.reshape(B, T, H, 2 * d).transpose(0, 2, 1, 3)
    nb = T // Q_BLOCK
    qb = q.reshape(B, H, 2, nb, Q_BLOCK, d).transpose(3, 0, 1, 2, 4, 5)
    key_chunk = pos // CHUNK
    scale = d ** -0.5

    def block(args):
        i, qi = args
        s = jnp.einsum('bhcqd,bhckd->bhcqk', qi, k).astype(jnp.float32) * scale
        q_chunk = (i * Q_BLOCK + jnp.arange(Q_BLOCK)) // CHUNK
        mask = key_chunk[None, :] <= q_chunk[:, None]
        s = jnp.where(mask[None, None, None], s, -jnp.inf)
        probs = jax.nn.softmax(s, axis=-1)
        attn = probs[:, :, 0] - lam * probs[:, :, 1]
        return jnp.einsum('bhqk,bhkv->bhqv', attn.astype(v.dtype), v)

    o = lax.map(block, (jnp.arange(nb), qb))
    o = o.transpose(1, 0, 3, 2, 4).reshape(B, T, H, 2 * d)
    o = rms_norm(o, subln_g) * (1.0 - lambda_init)
    return o.reshape(B, T, DA_WIDTH)


def rwkv7_scan(r, w, k, v, a, b):
    B, T, H, N = r.shape

    def step(S, inp):
        rt, wt, kt, vt, at, bt = inp
        sa = jnp.einsum('bhij,bhj->bhi', S, at)
        S = S * wt[:, :, None, :] + sa[..., None] * bt[:, :, None, :] + vt[..., None] * kt[:, :, None, :]
        return S, jnp.einsum('bhij,bhj->bhi', S, rt)

    seq = tuple(jnp.moveaxis(t.astype(jnp.float32), 1, 0) for t in (r, w, k, v, a, b))
    S0 = jnp.zeros((B, H, N, N), jnp.float32)
    _, y = lax.scan(step, S0, seq)
    return jnp.moveaxis(y, 0, 1)


def rwkv7_mix(z, mu, w0, w2, a0, a2, g2, k_k, k_a, r_k, ln_w, ln_b):
    B, T = z.shape[0], z.shape[1]
    z_prev = jnp.pad(z[:, :-1], ((0, 0), (1, 0), (0, 0)))
    z = z + (z_prev - z) * mu
    splits = [RW_WIDTH, 2 * RW_WIDTH, 3 * RW_WIDTH, 3 * RW_WIDTH + RW_DECAY_LORA,
              3 * RW_WIDTH + RW_DECAY_LORA + RW_A_LORA]
    r, k, v, wl, al, gl = jnp.split(z, splits, axis=-1)
    w = -jax.nn.softplus(-(w0 + jnp.tanh(wl) @ w2)) - 0.5
    decay = jnp.exp(-jnp.exp(w.astype(jnp.float32)))
    a = jax.nn.sigmoid(a0 + al @ a2)
    g = jax.nn.sigmoid(gl) @ g2

    def heads(t):
        return t.reshape(B, T, RW_HEADS, RW_HEAD_DIM)

    kk = heads(k * k_k).astype(jnp.float32)
    kk = kk / jnp.maximum(jnp.sqrt(jnp.sum(kk * kk, axis=-1, keepdims=True)), 1e-12)
    k = k * (1.0 + (a - 1.0) * k_a)
    y = rwkv7_scan(heads(r), heads(decay), heads(k), heads(v), -kk, kk * heads(a).astype(jnp.float32))
    mean = jnp.mean(y, axis=-1, keepdims=True)
    var = jnp.mean(jnp.square(y - mean), axis=-1, keepdims=True)
    y = ((y - mean) * lax.rsqrt(var + RW_LN_EPS)).reshape(B, T, RW_WIDTH)
    y = y * ln_w.astype(jnp.float32) + ln_b.astype(jnp.float32)
    bonus = jnp.sum((heads(r) * heads(k) * r_k).astype(jnp.float32), axis=-1, keepdims=True) * heads(v).astype(jnp.float32)
    y = y + bonus.reshape(B, T, RW_WIDTH)
    return (y * g.astype(jnp.float32)).astype(z.dtype)


def setup_inputs(seed: int = 0) -> dict:
    key = jax.random.key(seed)
    ks = jax.random.split(key, 40)
    L = DEPTH
    f32 = jnp.float32

    def nrm(k, shape, scale):
        return jax.random.normal(k, shape, f32) * scale

    def gain(k, n):
        return 1.0 + 0.05 * jax.random.normal(k, (L, n), f32)

    return {
        "x": jax.random.normal(ks[0], (BATCH, SEQ, D_MODEL), f32),
        "p": jax.random.normal(ks[1], (DEPTH, BATCH, SEQ, PLE_DIM), f32),
        "ffn1_pre_g": gain(ks[2], D_MODEL),
        "ffn1_w_gate": nrm(ks[3], (L, D_MODEL, D_FF), D_MODEL ** -0.5),
        "ffn1_w_up": nrm(ks[4], (L, D_MODEL, D_FF), D_MODEL ** -0.5),
        "ffn1_w_down": nrm(ks[5], (L, D_FF, D_MODEL), D_FF ** -0.5),
        "ffn1_post_g": gain(ks[6], D_MODEL),
        "mix_pre_g": gain(ks[7], D_MODEL),
        "w_in": nrm(ks[8], (L, D_MODEL, IN_PROJ), D_MODEL ** -0.5),
        "diff_lambda_q1": nrm(ks[9], (L, DA_HEAD_DIM), 0.1),
        "diff_lambda_k1": nrm(ks[10], (L, DA_HEAD_DIM), 0.1),
        "diff_lambda_q2": nrm(ks[11], (L, DA_HEAD_DIM), 0.1),
        "diff_lambda_k2": nrm(ks[12], (L, DA_HEAD_DIM), 0.1),
        "diff_subln_g": gain(ks[13], 2 * DA_HEAD_DIM),
        "rwkv_mu": jax.random.uniform(ks[14], (L, RW_PROJ), f32),
        "rwkv_w0": jax.random.uniform(ks[15], (L, RW_WIDTH), f32, minval=-6.0, maxval=0.5),
        "rwkv_w2": nrm(ks[16], (L, RW_DECAY_LORA, RW_WIDTH), 0.5 * RW_DECAY_LORA ** -0.5),
        "rwkv_a0": nrm(ks[17], (L, RW_WIDTH), 0.5),
        "rwkv_a2": nrm(ks[18], (L, RW_A_LORA, RW_WIDTH), 0.5 * RW_A_LORA ** -0.5),
        "rwkv_g2": nrm(ks[19], (L, RW_GATE_LORA, RW_WIDTH), RW_GATE_LORA ** -0.5),
        "rwkv_k_k": 0.85 + 0.05 * jax.random.normal(ks[20], (L, RW_WIDTH), f32),
        "rwkv_k_a": 1.0 + 0.05 * jax.random.normal(ks[21], (L, RW_WIDTH), f32),
        "rwkv_r_k": nrm(ks[22], (L, RW_HEADS, RW_HEAD_DIM), 0.1),
        "rwkv_ln_w": gain(ks[23], RW_WIDTH),
        "rwkv_ln_b": nrm(ks[24], (L, RW_WIDTH), 0.02),
        "w_out": nrm(ks[25], (L, MIX_WIDTH, D_MODEL), MIX_WIDTH ** -0.5),
        "mix_post_g": gain(ks[26], D_MODEL),
        "ffn2_pre_g": gain(ks[27], D_MODEL),
        "ffn2_w_gate": nrm(ks[28], (L, D_MODEL, D_FF), D_MODEL ** -0.5),
        "ffn2_w_up": nrm(ks[29], (L, D_MODEL, D_FF), D_MODEL ** -0.5),
        "ffn2_w_down": nrm(ks[30], (L, D_FF, D_MODEL), D_FF ** -0.5),
        "ffn2_post_g": gain(ks[31], D_MODEL),
        "ple_pre_g": gain(ks[32], D_MODEL),
        "ple_w_gate": nrm(ks[33], (L, D_MODEL, D_MODEL), D_MODEL ** -0.5),
        "ple_w_proj": nrm(ks[34], (L, PLE_DIM, D_MODEL), PLE_DIM ** -0.5),
        "ple_post_g": gain(ks[35], D_MODEL),
    }


def reference(x, p, ffn1_pre_g, ffn1_w_gate, ffn1_w_up, ffn1_w_down, ffn1_post_g,
              mix_pre_g, w_in, diff_lambda_q1, diff_lambda_k1, diff_lambda_q2, diff_lambda_k2,
              diff_subln_g, rwkv_mu, rwkv_w0, rwkv_w2, rwkv_a0, rwkv_a2, rwkv_g2, rwkv_k_k,
              rwkv_k_a, rwkv_r_k, rwkv_ln_w, rwkv_ln_b, w_out, mix_post_g,
              ffn2_pre_g, ffn2_w_gate, ffn2_w_up, ffn2_w_down, ffn2_post_g,
              ple_pre_g, ple_w_gate, ple_w_proj, ple_post_g):
    for i in range(DEPTH):
        f = swiglu(rms_norm(x, ffn1_pre_g[i]), ffn1_w_gate[i], ffn1_w_up[i], ffn1_w_down[i])
        x = x + 0.5 * rms_norm(f, ffn1_post_g[i])

        h = rms_norm(x, mix_pre_g[i])
        z = h @ w_in[i]
        zq, zk, zv, zr = jnp.split(z, [DA_QK, 2 * DA_QK, 2 * DA_QK + DA_WIDTH], axis=-1)

        lambda_init = 0.8 - 0.6 * math.exp(-0.3 * i)
        lam = (jnp.exp(jnp.sum((diff_lambda_q1[i] * diff_lambda_k1[i]).astype(jnp.float32)))
               - jnp.exp(jnp.sum((diff_lambda_q2[i] * diff_lambda_k2[i]).astype(jnp.float32)))
               + lambda_init)
        o_diff = diff_attention(zq, zk, zv, lam, lambda_init, diff_subln_g[i])
        o_rwkv = rwkv7_mix(zr, rwkv_mu[i], rwkv_w0[i], rwkv_w2[i], rwkv_a0[i], rwkv_a2[i],
                           rwkv_g2[i], rwkv_k_k[i], rwkv_k_a[i], rwkv_r_k[i],
                           rwkv_ln_w[i], rwkv_ln_b[i])
        mixed = jnp.concatenate([o_diff, o_rwkv], axis=-1) @ w_out[i]
        x = x + rms_norm(mixed, mix_post_g[i])

        f = swiglu(rms_norm(x, ffn2_pre_g[i]), ffn2_w_gate[i], ffn2_w_up[i], ffn2_w_down[i])
        x = x + 0.5 * rms_norm(f, ffn2_post_g[i])

        gate = jax.nn.sigmoid(rms_norm(x, ple_pre_g[i]) @ ple_w_gate[i])
        x = x + rms_norm((p[i] @ ple_w_proj[i]) * gate, ple_post_g[i])
    return x
```

```python
import math
import os
DBG = os.environ.get('KDBG', '')
from contextlib import ExitStack

import numpy as np
import concourse.bass as bass
import concourse.mybir as mybir
from concourse.bass_utils import run_bass_kernel_spmd

F32 = mybir.dt.float32
BF16 = mybir.dt.bfloat16
AF = mybir.ActivationFunctionType
ALU = mybir.AluOpType
AX = mybir.AxisListType

D = 4096
DFF = 11008
NFC = DFF // 128
KC = D // 128
TT = 512
RMS_EPS = 1e-6
NCORES = 8


class Buf:
    __slots__ = ("name", "w", "r")

    def __init__(self, name=""):
        self.name = name
        self.w = {}
        self.r = {}


class _Eng:
    def __init__(self, name):
        self.name = name
        self.count = 0
        self.prog = []
        self.waited = {}


class Cx:
    DMA_K = 8

    def __init__(self, nc):
        self.nc = nc
        self.eng = {n: _Eng(n) for n in ("pe", "dve", "act", "pool", "sp")}
        self.dman = {"sp": 0, "pool": 0, "act": 0}
        self.semkeys = set()
        self.nops = 0

    def op(self, en, fn, reads=(), writes=(), sig=True, dma=False, pwrites=()):
        e = self.eng[en]
        waits = {}

        def need(tok):
            k, v = tok
            if k == "pe" and en == "pe":
                return
            if waits.get(k, 0) < v:
                waits[k] = v

        for b in reads:
            for tok in b.w.items():
                need(tok)
        for b in writes:
            for tok in b.w.items():
                need(tok)
            for tok in b.r.items():
                need(tok)
        for b in pwrites:
            for tok in b.r.items():
                need(tok)
        if dma:
            i = self.dman[en]
            self.dman[en] = i + 1
            k = "dma_%s_%d" % (en, i % self.DMA_K)
            v = 16 * (i // self.DMA_K + 1)
            if i >= self.DMA_K:
                need((k, v - 16))
            tok = (k, v)
            inc = (k, 16)
        elif sig:
            e.count += 1
            tok = (en, e.count)
            inc = (en, 1)
        else:
            tok = (en, e.count + 1)
            inc = None
        wl = []
        for k, v in waits.items():
            if e.waited.get(k, 0) >= v:
                continue
            e.waited[k] = v
            wl.append((k, v))
            self.semkeys.add(k)
        if inc is not None:
            self.semkeys.add(inc[0])
        e.prog.append((wl, fn, inc))
        self.nops += 1
        for b in writes:
            b.w = {tok[0]: tok[1]}
            b.r = {}
        for b in pwrites:
            if b.w.get(tok[0], 0) < tok[1]:
                b.w[tok[0]] = tok[1]
        for b in reads:
            if b in writes:
                continue
            if b.r.get(tok[0], 0) < tok[1]:
                b.r[tok[0]] = tok[1]
        return tok

    def pe(self, fn, reads=(), writes=(), sig=True, pwrites=()):
        return self.op("pe", fn, reads, writes, sig, pwrites=pwrites)

    def dve(self, fn, reads=(), writes=(), pwrites=()):
        return self.op("dve", fn, reads, writes, pwrites=pwrites)

    def act(self, fn, reads=(), writes=(), pwrites=()):
        return self.op("act", fn, reads, writes, pwrites=pwrites)

    def pool(self, fn, reads=(), writes=(), pwrites=()):
        return self.op("pool", fn, reads, writes, pwrites=pwrites)

    def dma(self, q, out, in_, reads=(), writes=(), pwrites=(), **kw):
        return self.op(q, lambda e: e.dma_start(out=out, in_=in_, **kw), reads, writes, dma=True, pwrites=pwrites)

    def cc(self, fn, reads=(), writes=()):
        e = self.eng["pool"]
        self.ncc = getattr(self, "ncc", 0) + 1
        k = "cc_%d" % self.ncc
        waits = {}
        for b in reads:
            for kk, v in b.w.items():
                waits[kk] = max(waits.get(kk, 0), v)
        for b in writes:
            for kk, v in list(b.w.items()) + list(b.r.items()):
                waits[kk] = max(waits.get(kk, 0), v)
        wl = []
        for kk, v in waits.items():
            if e.waited.get(kk, 0) >= v:
                continue
            e.waited[kk] = v
            wl.append((kk, v))
            self.semkeys.add(kk)
        self.semkeys.add(k)
        e.prog.append((wl, fn, (k, 1)))
        for b in writes:
            b.w = {k: 1}
            b.r = {}
        for b in reads:
            b.r[k] = 1

    def barrier(self):
        toks = {}
        for n, e in self.eng.items():
            if n != "sp" and e.count > 0:
                toks[n] = e.count
        for q, n in self.dman.items():
            for i in range(min(n, self.DMA_K)):
                last = ((n - 1 - i) // self.DMA_K) * self.DMA_K + i
                toks["dma_%s_%d" % (q, last % self.DMA_K)] = 16 * (last // self.DMA_K + 1)
        for i in range(getattr(self, "ncc", 0)):
            toks["cc_%d" % (i + 1)] = 1
        for n, e in self.eng.items():
            wl = []
            for k, v in toks.items():
                if e.waited.get(k, 0) < v:
                    e.waited[k] = v
                    wl.append((k, v))
                    self.semkeys.add(k)
            e.prog.append((wl, None, None))

    def final_wait(self, en, bufs):
        e = self.eng[en]
        wl = []
        for b in bufs:
            for k, v in b.w.items():
                if e.waited.get(k, 0) < v:
                    e.waited[k] = v
                    wl.append((k, v))
        e.prog.append((wl, None, None))

    def emit(self, es):
        nc = self.nc
        sems = {k: es.enter_context(nc.semaphore("s_" + k)) for k in sorted(self.semkeys)}
        block = es.enter_context(nc.Block())

        def replay(en):
            def f(engine):
                if en == "pool" and getattr(self, "want_pid", False):
                    self.pid = engine.partition_id()
                for wl, fn, inc in self.eng[en].prog:
                    for k, v in wl:
                        engine.wait_ge(sems[k], v)
                    if fn is None:
                        continue
                    ins = fn(engine)
                    if inc is not None:
                        ins.then_inc(sems[inc[0]], inc[1])
            return f

        block.tensor(replay("pe"))
        block.vector(replay("dve"))
        block.scalar(replay("act"))
        block.gpsimd(replay("pool"))
        block.sync(replay("sp"))


class Ring:
    def __init__(self, tiles, name):
        self.tiles = tiles
        self.bufs = [Buf("%s%d" % (name, i)) for i in range(len(tiles))]
        self.i = 0

    def next(self):
        j = self.i % len(self.tiles)
        self.i += 1
        return self.tiles[j], self.bufs[j]


class Res:
    pass


def alloc_common(nc, es, cx):
    R = Res()
    R.nc = nc
    sb = lambda name, shape, dt: es.enter_context(nc.sbuf_tensor("sb_" + name, shape, dt))
    R.ident = sb("ident", [128, 128], F32)
    R.identb = Buf("ident")
    R.hT = sb("hT", [128, KC, TT], BF16)
    R.hTb = [Buf("hT%d" % i) for i in range(4)]
    R.big = sb("big", [128, NFC * TT // 2], F32)
    R.aT = R.big[:].bitcast(BF16).rearrange("p (k c) -> p k c", c=TT)
    R.aTb = [Buf("aT%d" % i) for i in range(NFC)]
    R.wring = Ring([sb("wr%d" % i, [128, 4096], BF16) for i in range(5)], "wr")
    R.qring = Ring([sb("qr%d" % i, [128, 512], F32) for i in range(10)], "qr")
    R.stage = Ring([sb("stg%d" % i, [128, 512], F32) for i in range(2)], "stg")
    R.tmp = Ring([sb("tmp%d" % i, [128, 512], F32) for i in range(2)], "tmp")
    R.junk = sb("junk", [128, 512], BF16)
    R.junkb = Buf("junk")
    R.gcol = sb("gcol", [128, 8, KC], F32)
    R.gcolb = Buf("gcol")
    R.small = sb("small", [128, 64], F32)
    R.ssq = sb("ssq", [128, 4, 8], F32)
    R.ssqb = [Buf("ssq%d" % i) for i in range(4)]
    R.st = Ring([sb("st%d" % i, [128, 8], F32) for i in range(4)], "st")
    R.banks = [es.enter_context(nc.psum_tensor("pb%d" % i, [128, 512], F32)) for i in range(8)]
    R.bankb = [Buf("bank%d" % i) for i in range(8)]
    return R


GIDX = {"ffn1_pre_g": 0, "mix_pre_g": 1, "ffn2_pre_g": 2, "ple_pre_g": 3}


def load_consts(cx, R, dr):
    cx.dma("sp", R.ident[:], dr["ident"], writes=[R.identb])
    for name, gi in GIDX.items():
        if name in dr:
            t, tb = R.qring.next()
            cx.dma("sp", t[0:KC, 0:128], dr[name].rearrange("o (kc p) -> (o kc) p", p=128), writes=[tb])
            cx.pe(lambda e, t=t: e.transpose(out=R.banks[0][:, 0:KC], in_=t[0:KC, 0:128], identity=R.ident[0:KC, 0:KC]),
                  reads=[tb, R.identb], writes=[R.bankb[0]])
            cx.dve(lambda e, gi=gi: e.tensor_copy(out=R.gcol[:, gi, :], in_=R.banks[0][:, 0:KC]),
                   reads=[R.bankb[0]], pwrites=[R.gcolb])


def rstd_from_ss(cx, R, ss_ap, ssb, n, coef=1.0):
    t, tb = R.st.next()
    cx.act(lambda e: e.activation(out=t[:, 0:1], in_=ss_ap, func=AF.Sqrt, bias=float(RMS_EPS), scale=1.0 / n),
           reads=[ssb], writes=[tb])
    cx.dve(lambda e: e.reciprocal(out=t[:, 1:2], in_=t[:, 0:1]), reads=[tb], writes=[tb])
    if coef != 1.0:
        cx.dve(lambda e: e.tensor_scalar(out=t[:, 1:2], in0=t[:, 1:2], scalar1=float(coef), scalar2=None,
                                         op0=ALU.mult), reads=[tb], writes=[tb])
    return t[:, 1:2], tb


def norm_stage(cx, R, src, srcb, row0, gi):
    for ts in range(4):
        r0 = row0 + ts * 128
        ss, ssb = R.st.next()
        for q in range(8):
            xt, xb = R.qring.next()
            cx.dma("sp", xt[:], src[r0:r0 + 128, q * 512:(q + 1) * 512], reads=[srcb], writes=[xb])
            cx.act(lambda e, xt=xt, q=q, ss=ss: e.activation(out=R.junk[:], in_=xt[:], func=AF.Square,
                                                          accum_out=ss[:, q:q + 1]),
                   reads=[xb], writes=[R.junkb], pwrites=[ssb])
        rs, rsb = R.st.next()
        cx.dve(lambda e, ss=ss, rs=rs: e.tensor_reduce(out=rs[:, 4:5], in_=ss[:, 0:8], axis=AX.X, op=ALU.add),
               reads=[ssb], writes=[rsb])
        rstd, rb = rstd_from_ss(cx, R, rs[:, 4:5], rsb, D)
        for q in range(8):
            xt, xb = R.qring.next()
            cx.dma("sp", xt[:], src[r0:r0 + 128, q * 512:(q + 1) * 512], reads=[srcb], writes=[xb])
            if True:
                cx.dve(lambda e, xt=xt, rstd=rstd: e.tensor_scalar(out=xt[:], in0=xt[:], scalar1=rstd, scalar2=None,
                                                                op0=ALU.mult), reads=[xb, rb], writes=[xb])
            else:
                cx.act(lambda e, xt=xt, rstd=rstd: e.activation(out=xt[:], in_=xt[:], func=AF.Copy, scale=rstd),
                       reads=[xb, rb], writes=[xb])
            bi = (ts * 8 + q) % 2
            bank, bb = R.banks[bi], R.bankb[bi]
            for j in range(4):
                cx.pe(lambda e, bank=bank, j=j, xt=xt: e.transpose(
                    out=bank[:, j * 128:(j + 1) * 128], in_=xt[:, j * 128:(j + 1) * 128], identity=R.ident[:]),
                    reads=[xb, R.identb], writes=[bb], sig=(j == 3))
            for j in range(4):
                kc = q * 4 + j
                if True:
                    cx.dve(lambda e, bank=bank, j=j, kc=kc, ts=ts: e.tensor_scalar(
                        out=R.hT[:, kc, ts * 128:(ts + 1) * 128], in0=bank[:, j * 128:(j + 1) * 128],
                        scalar1=R.gcol[:, gi, kc:kc + 1], scalar2=None, op0=ALU.mult),
                        reads=[bb, R.gcolb], pwrites=[R.hTb[ts]])
                else:
                    cx.act(lambda e, bank=bank, j=j, kc=kc, ts=ts: e.activation(
                        out=R.hT[:, kc, ts * 128:(ts + 1) * 128], in_=bank[:, j * 128:(j + 1) * 128],
                        func=AF.Copy, scale=R.gcol[:, gi, kc:kc + 1]),
                        reads=[bb, R.gcolb], pwrites=[R.hTb[ts]])


def gateup_stage(cx, R, Wg, Wu, wbuf=None, wbuf2=None):
    Wgv = Wg.rearrange("(kc p) c -> p kc c", p=128)
    Wuv = Wu.rearrange("(kc p) c -> p kc c", p=128)
    nblk = NFC // 2
    for blk in range(nblk):
        c0 = blk * 256
        gb = [(blk % 2) * 4 + 0, (blk % 2) * 4 + 1]
        ub = [(blk % 2) * 4 + 2, (blk % 2) * 4 + 3]
        for half in range(2):
            pieces = []
            for Wv in (Wgv, Wuv):
                wt, wb = R.wring.next()
                wv = wt[:].rearrange("p (k c) -> p k c", c=256)
                cx.dma("pool", wv, Wv[:, half * 16:(half + 1) * 16, c0:c0 + 256],
                       reads=[b_ for b_ in (wbuf, wbuf2) if b_ is not None], writes=[wb])
                pieces.append((wv, wb))
            for (wv, wb), banks in zip(pieces, (gb, ub)):
                for j in range(2):
                    bank, bb = R.banks[banks[j]], R.bankb[banks[j]]
                    for kcl in range(16):
                        kc = half * 16 + kcl
                        cx.pe(lambda e, bank=bank, wv=wv, kcl=kcl, j=j, kc=kc: e.matmul(
                            bank[:], wv[:, kcl, j * 128:(j + 1) * 128], R.hT[:, kc, :],
                            start=(kc == 0), stop=(kc == KC - 1)),
                            reads=[wb] + R.hTb, writes=[bb], sig=(kcl == 15))
        for j in range(2):
            fc = blk * 2 + j
            t, tb = R.tmp.next()
            gbank, ubank = R.banks[gb[j]], R.banks[ub[j]]
            cx.act(lambda e, t=t, gbank=gbank: e.activation(out=t[:], in_=gbank[:], func=AF.Silu),
                   reads=[R.bankb[gb[j]]], writes=[tb])
            cx.dve(lambda e, t=t, fc=fc, ubank=ubank: e.tensor_tensor(out=R.aT[:, fc, :], in0=t[:], in1=ubank[:],
                                                                   op=ALU.mult),
                   reads=[tb, R.bankb[ub[j]]], writes=[R.aTb[fc]])


def lin_tok_stage(cx, R, actT, actb, nK, W, ncols, evac, cbs=None, par=0, wbuf=None, rowmap=None):
    Wv = W.rearrange("(k p) c -> p k c", p=128)
    G = 8
    ncb = ncols // 512
    for cb in (range(ncb) if cbs is None else cbs):
        banks = [((cb + par) % 2) * 4 + ts for ts in range(4)]
        for k0 in range(0, nK, G):
            g = min(G, nK - k0)
            wt, wb = R.wring.next()
            wv = wt[:].rearrange("p (k c) -> p k c", c=512)
            rk0 = k0 if rowmap is None else rowmap(k0)
            cx.dma("pool", wv[:, 0:g, :], Wv[:, rk0:rk0 + g, cb * 512:(cb + 1) * 512],
                   reads=([wbuf] if wbuf is not None else []), writes=[wb])
            for kl in range(g):
                k = k0 + kl
                for ts in range(4):
                    bank = R.banks[banks[ts]]
                    cx.pe(lambda e, bank=bank, k=k, kl=kl, ts=ts, wv=wv: e.matmul(
                        bank[:], actT[:, k, ts * 128:(ts + 1) * 128], wv[:, kl, :],
                        start=(k == 0), stop=(k == nK - 1)),
                        reads=[wb, (actb(k, ts) if callable(actb) else actb[k])], writes=[R.bankb[banks[ts]]],
                        sig=(k == nK - 1 or kl == g - 1))
        for ts in range(4):
            evac(cb, ts, R.banks[banks[ts]], R.bankb[banks[ts]])


def make_f_evac(cx, R, fscr, fb):
    def evac(cb, ts, bank, bb):
        s, sb_ = R.stage.next()
        cx.act(lambda e: e.activation(out=s[:], in_=bank[:], func=AF.Copy), reads=[bb], writes=[sb_])
        cx.act(lambda e: e.activation(out=R.junk[:], in_=s[:], func=AF.Square,
                                      accum_out=R.ssq[:, ts, cb:cb + 1]),
               reads=[sb_], writes=[R.junkb], pwrites=[R.ssqb[ts]])
        cx.dma("sp", fscr[ts * 128:(ts + 1) * 128, cb * 512:(cb + 1) * 512], s[:], reads=[sb_], pwrites=[fb])
    return evac


def finalize_stage(cx, R, fscr, fb, gpost, src, srcb, srow0, dst, dstb, drow0, coef, ncb=8):
    steps = [(ts, q) for ts in range(4) for q in range(8)]
    rcs = {}
    loaded = {}

    def load(i):
        ts, q = steps[i]
        cs = slice(q * 512, (q + 1) * 512)
        ft, fbq = R.qring.next()
        cx.dma("sp", ft[:], fscr[ts * 128:(ts + 1) * 128, cs], reads=[fb], writes=[fbq])
        xt, xb = R.qring.next()
        cx.dma("sp", xt[:], src[srow0 + ts * 128:srow0 + (ts + 1) * 128, cs], reads=[srcb], writes=[xb])
        gt, gb_ = R.qring.next()
        cx.dma("sp", gt[:], gpost[0:1, cs].partition_broadcast(128), writes=[gb_])
        loaded[i] = (ft, fbq, xt, xb, gt, gb_)

    def compute(i):
        ts, q = steps[i]
        cs = slice(q * 512, (q + 1) * 512)
        if ts not in rcs:
            ss, ssb = R.st.next()
            cx.dve(lambda e, ss=ss, ts=ts: e.tensor_reduce(out=ss[:, 0:1], in_=R.ssq[:, ts, 0:ncb], axis=AX.X,
                                                        op=ALU.add), reads=[R.ssqb[ts]], writes=[ssb])
            rcs[ts] = rstd_from_ss(cx, R, ss[:, 0:1], ssb, D, coef)
        rc, rb = rcs[ts]
        ft, fbq, xt, xb, gt, gb_ = loaded.pop(i)
        cx.dve(lambda e: e.scalar_tensor_tensor(out=ft[:], in0=ft[:], scalar=rc, in1=gt[:], op0=ALU.mult,
                                                op1=ALU.mult), reads=[fbq, gb_, rb], writes=[fbq])
        cx.pool(lambda e: e.tensor_tensor(out=ft[:], in0=ft[:], in1=xt[:], op=ALU.add),
                reads=[fbq, xb], writes=[fbq])
        cx.dma("sp", dst[drow0 + ts * 128:drow0 + (ts + 1) * 128, cs], ft[:], reads=[fbq], pwrites=[dstb])

    load(0)
    load(1)
    for i in range(len(steps)):
        if i + 2 < len(steps):
            load(i + 2)
        compute(i)


def ffn_block(cx, R, dr, pfx, src, srcb, dst, dstb, row0, fscr, fb, upto=9, wbuf=None, wbufs=None):
    if upto >= 1:
        norm_stage(cx, R, src, srcb, row0, GIDX[pfx + "_pre_g"])
    if upto >= 2:
        gateup_stage(cx, R, dr[pfx + "_w_gate"], dr[pfx + "_w_up"], wbuf,
                     wbuf2=(wbufs or {}).get(pfx + "_w_gate"))
    if upto >= 3:
        lin_tok_stage(cx, R, R.aT, R.aTb, NFC, dr[pfx + "_w_down"], D, make_f_evac(cx, R, fscr, fb), wbuf=wbuf)
    if upto >= 4:
        finalize_stage(cx, R, fscr, fb, dr[pfx + "_post_g"], src, srcb, row0, dst, dstb, row0, 0.5)


def build_test_ffn(ntok, upto=9):
    nc = bass.Bass("TRN2", target_bir_lowering=False)
    dr = {}
    dr["x"] = nc.dram_tensor("x", [ntok, D], F32, kind="ExternalInput").ap()
    dr["ident"] = nc.dram_tensor("ident", [128, 128], F32, kind="ExternalInput").ap()
    dr["ffn1_pre_g"] = nc.dram_tensor("ffn1_pre_g", [1, D], F32, kind="ExternalInput").ap()
    dr["ffn1_post_g"] = nc.dram_tensor("ffn1_post_g", [1, D], F32, kind="ExternalInput").ap()
    if upto >= 2:
        dr["ffn1_w_gate"] = nc.dram_tensor("ffn1_w_gate", [D, DFF], F32, kind="ExternalInput").ap()
        dr["ffn1_w_up"] = nc.dram_tensor("ffn1_w_up", [D, DFF], F32, kind="ExternalInput").ap()
        dr["ffn1_w_down"] = nc.dram_tensor("ffn1_w_down", [DFF, D], F32, kind="ExternalInput").ap()
    y = nc.dram_tensor("y", [ntok, D], F32, kind="ExternalOutput").ap()
    fscr = nc.dram_tensor("fscr", [TT, D], F32, kind="Internal").ap()
    cx = Cx(nc)
    with ExitStack() as es:
        R = alloc_common(nc, es, cx)
        load_consts(cx, R, dr)
        xb, yb, fb = Buf("x"), Buf("y"), Buf("f")
        for tt in range(ntok // TT):
            ffn_block(cx, R, dr, "ffn1", dr["x"], xb, y, yb, tt * TT, fscr, fb, upto)
        if upto < 4:
            t, tb = R.stage.next()
            cx.dve(lambda e: e.tensor_copy(out=t[:], in_=(R.hT[:, 0, :] if upto < 2 else R.aT[:, 0, :])),
                   reads=R.hTb + R.aTb + [R.gcolb], writes=[tb])
            cx.dma("sp", y[0:128, 0:512], t[:], reads=[tb], pwrites=[yb])
        cx.final_wait("sp", [yb, fb])
        for en in ("pe", "dve", "act", "pool"):
            cx.final_wait("sp", [])
        cx.emit(es)
    return nc, cx


def ple_block(cx, R, dr, src, srcb, dst, dstb, row0, fscr, fb, wbuf=None):
    norm_stage(cx, R, src, srcb, row0, GIDX["ple_pre_g"])
    for ts in range(4):
        pt, pb = R.qring.next()
        cx.dma("sp", pt[:, 0:256], dr["p"][row0 + ts * 128:row0 + (ts + 1) * 128, :], writes=[pb])
        bank, bb = R.banks[ts % 2], R.bankb[ts % 2]
        for j in range(2):
            cx.pe(lambda e, bank=bank, j=j, pt=pt: e.transpose(out=bank[:, j * 128:(j + 1) * 128],
                                                             in_=pt[:, j * 128:(j + 1) * 128], identity=R.ident[:]),
                  reads=[pb, R.identb], writes=[bb], sig=(j == 1))
        cx.dve(lambda e, bank=bank, ts=ts: e.tensor_copy(
            out=R.pT[:, :, ts * 128:(ts + 1) * 128], in_=bank[:, 0:256].rearrange("p (j t) -> p j t", j=2)),
            reads=[bb], pwrites=[R.pTb[0], R.pTb[1]])
    gst = {}

    def gate_evac(cb, ts, bank, bb):
        cx.act(lambda e: e.activation(out=R.gstage[ts][:], in_=bank[:], func=AF.Sigmoid),
               reads=[bb], writes=[R.gstageb[ts]])

    def pp_evac(cb, ts, bank, bb):
        s_, sb_ = R.stage.next()
        cx.dve(lambda e: e.tensor_tensor(out=s_[:], in0=R.gstage[ts][:], in1=bank[:], op=ALU.mult),
               reads=[bb, R.gstageb[ts]], writes=[sb_])
        cx.act(lambda e: e.activation(out=R.junk[:], in_=s_[:], func=AF.Square, accum_out=R.ssq[:, ts, cb:cb + 1]),
               reads=[sb_], writes=[R.junkb], pwrites=[R.ssqb[ts]])
        cx.dma("sp", fscr[ts * 128:(ts + 1) * 128, cb * 512:(cb + 1) * 512], s_[:], reads=[sb_], pwrites=[fb])

    for cb in range(8):
        lin_tok_stage(cx, R, R.hT, (lambda k, ts: R.hTb[ts]), KC, dr["ple_w_gate"], D, gate_evac, cbs=[cb], par=0, wbuf=wbuf)
        lin_tok_stage(cx, R, R.pT, R.pTb, 2, dr["ple_w_proj"], D, pp_evac, cbs=[cb], par=1, wbuf=wbuf)
    finalize_stage(cx, R, fscr, fb, dr["ple_post_g"], src, srcb, row0, dst, dstb, row0, 1.0)


WNAMES = ["ffn1_w_gate", "ffn1_w_up", "ffn1_w_down", "ffn2_w_gate", "ffn2_w_up", "ffn2_w_down",
          "ple_w_gate", "ple_w_proj"]
WSHAPE = {"ffn1_w_gate": (D, DFF), "ffn1_w_up": (D, DFF), "ffn1_w_down": (DFF, D),
          "ffn2_w_gate": (D, DFF), "ffn2_w_up": (D, DFF), "ffn2_w_down": (DFF, D),
          "ple_w_gate": (D, D), "ple_w_proj": (256, D)}
GNAMES = ["ffn1_pre_g", "ffn1_post_g", "mix_pre_g", "ffn2_pre_g", "ffn2_post_g", "ple_pre_g", "ple_post_g"]
NTOK = 2048


def build_full(ntok=NTOK, ncores=NCORES, ag=True):
    nc = bass.Bass("TRN2", target_bir_lowering=False)
    dr = {}
    dr["x"] = nc.dram_tensor("x", [ntok, D], F32, kind="ExternalInput").ap()
    dr["p"] = nc.dram_tensor("p", [ntok, 256], F32, kind="ExternalInput").ap()
    dr["ident"] = nc.dram_tensor("ident", [128, 128], F32, kind="ExternalInput").ap()
    for g in GNAMES:
        dr[g] = nc.dram_tensor(g, [1, D], F32, kind="ExternalInput").ap()
    cx = Cx(nc)
    wbufs = {}
    shards = {}
    for w in WNAMES:
        r, c = WSHAPE[w]
        if ag:
            shards[w] = nc.dram_tensor(w + "_sh", [r // ncores, c], F32, kind="ExternalInput").ap()
        else:
            dr[w] = nc.dram_tensor(w, [r, c], F32, kind="ExternalInput").ap()
            wbufs[w] = None
    out = nc.dram_tensor("out", [ntok, D], F32, kind="ExternalOutput").ap()
    x1 = nc.dram_tensor("x1", [ntok, D], F32, kind="Internal").ap()
    x3 = nc.dram_tensor("x3", [ntok, D], F32, kind="Internal").ap()
    fscr = nc.dram_tensor("fscr", [TT, D], F32, kind="Internal").ap()
    with ExitStack() as es:
        R = alloc_common(nc, es, cx)
        R.pT = es.enter_context(nc.sbuf_tensor("sb_pT", [128, 2, TT], BF16))
        R.pTb = [Buf("pT0"), Buf("pT1")]
        R.gstage = [es.enter_context(nc.sbuf_tensor("sb_gst%d" % i, [128, 512], F32)) for i in range(4)]
        R.gstageb = [Buf("gst%d" % i) for i in range(4)]
        load_consts(cx, R, dr)
        if ag:
            shared = nc.dram_tensor("wshared", [D * DFF], F32, kind="Internal", addr_space="Shared").ap()
            shb = Buf("wshared")
        for w in (WNAMES if ag else []):
            r, c = WSHAPE[w]
            bounce = nc.dram_tensor(w + "_bn", [r // ncores, c], F32, kind="Internal").ap()
            full = nc.dram_tensor(w, [r, c], F32, kind="Internal").ap()
            shv = shared[0:r * c].rearrange("(r c) -> r c", c=c)
            bb, wb = Buf(w + "_bn"), Buf(w)
            cx.dma("pool", bounce, shards[w], writes=[bb])
            cx.cc(lambda e, bounce=bounce, shv=shv: e.collective_compute(
                "AllGather", ALU.bypass, replica_groups=[list(range(ncores))], ins=[bounce], outs=[shv]),
                reads=[bb], writes=[shb])
            cx.dma("sp", full, shv, reads=[shb], writes=[wb])
            dr[w] = full
            wbufs[w] = wb
        xb, x1b, x3b, ob, fb = Buf("x"), Buf("x1"), Buf("x3"), Buf("out"), Buf("f")
        ntile = ntok // TT
        for tt in range(ntile):
            ffn_block(cx, R, dr, "ffn1", dr["x"], xb, x1, x1b, tt * TT, fscr, fb, wbuf=wbufs["ffn1_w_down"], wbufs=wbufs)
        for tt in range(ntile):
            ffn_block(cx, R, dr, "ffn2", x1, x1b, x3, x3b, tt * TT, fscr, fb, wbuf=wbufs["ffn2_w_down"], wbufs=wbufs)
        for tt in range(ntile):
            ple_block(cx, R, dr, x3, x3b, out, ob, tt * TT, fscr, fb, wbuf=wbufs["ple_w_proj"])
        cx.final_wait("sp", [ob])
        cx.emit(es)
    return nc, cx


NQK = 1024
ZROW = {"q": 0, "k": 1024, "r": 2048, "rk": 3072, "rv": 4096, "wl": 5120, "al": 5216, "gl": 5312}
NZ = 5568
WCOL = {"q": 0, "k": 1024, "v": 2048, "r": 3072, "rk": 4096, "rv": 5120, "wl": 6144, "al": 6240, "gl": 6336}
NWC = 6592
ATT_SCALE = 128 ** -0.5
LAMBDA_INIT = 0.8 - 0.6 * math.exp(-0.3 * 0)


class Carver:
    def __init__(self, big, limit=22016):
        self.big, self.o, self.limit = big, 0, limit

    def take(self, nwords, dt=F32):
        a = self.o
        self.o += nwords
        assert self.o <= self.limit, self.o
        v = self.big[:, a:a + nwords]
        return v.bitcast(BF16) if dt == BF16 else v


def proj_stage(cx, R, dr, T, WCOL=WCOL):
    Wm = dr["w_in_mine"]
    Wv = Wm.rearrange("(kc p) c -> p kc c", p=128)
    TL = dr["h2T_seq"].shape[2]
    hseq = dr["h2T_seq"]
    blocks = []
    for g in ("q", "k", "r", "rk", "rv"):
        for b in range(4):
            blocks.append((WCOL[g] + b * 256, ZROW[g] + b * 256, 256, 128))
    blocks.append((WCOL["wl"], ZROW["wl"], 192, 96))
    blocks.append((WCOL["gl"], ZROW["gl"], 256, 128))
    for ti in range(T // TT):
        rk, c0t = (ti * TT) // TL, (ti * TT) % TL
        cx.dma("sp", R.hT[:], hseq[rk, :, c0t:c0t + TT].rearrange("(kc p) t -> p kc t", p=128),
               reads=[dr["_h2T_seq_b"]], writes=R.hTb)
        for bi, (wc0, zr0, ncol, cw) in enumerate(blocks):
            banks = [(bi % 2) * 2, (bi % 2) * 2 + 1]
            for half in range(2):
                wt, wb = R.wring.next()
                wv = wt[:].rearrange("p (k c) -> p k c", c=256)
                cx.dma("pool", wv[:, :, 0:ncol], Wv[:, half * 16:(half + 1) * 16, wc0:wc0 + ncol], writes=[wb])
                for j in range(2):
                    bank, bb = R.banks[banks[j]], R.bankb[banks[j]]
                    for kcl in range(16):
                        kc = half * 16 + kcl
                        cx.pe(lambda e, bank=bank, wv=wv, kcl=kcl, j=j, kc=kc, cw=cw: e.matmul(
                            bank[0:cw, :], wv[:, kcl, j * cw:(j + 1) * cw], R.hT[:, kc, :],
                            start=(kc == 0), stop=(kc == KC - 1)),
                            reads=[wb] + R.hTb, writes=[bb], sig=(kcl == 15))
            for j in range(2):
                bank, bb = R.banks[banks[j]], R.bankb[banks[j]]
                st, stb = R.stage.next()
                if j == 0:
                    cx.act(lambda e, st=st, bank=bank, cw=cw: e.activation(out=st[0:cw, :], in_=bank[0:cw, :], func=AF.Copy),
                           reads=[bb], writes=[stb])
                else:
                    cx.dve(lambda e, st=st, bank=bank, cw=cw: e.tensor_copy(out=st[0:cw, :], in_=bank[0:cw, :]),
                           reads=[bb], writes=[stb])
                cx.dma("sp", dr["zT"][zr0 + j * cw:zr0 + (j + 1) * cw, ti * TT:(ti + 1) * TT], st[0:cw, :],
                       reads=[stb], pwrites=[dr["_zT_b"]])

        def v_evac(cb, ts, bank, bb, ti=ti):
            st, stb = R.tmp.next()
            stv = st[:].bitcast(BF16)[:, 0:512]
            cx.act(lambda e: e.activation(out=stv, in_=bank[:], func=AF.Copy), reads=[bb], writes=[stb])
            cx.dma("sp", dr["vat"][ti * TT + ts * 128:ti * TT + (ts + 1) * 128, cb * 512:(cb + 1) * 512], stv,
                   reads=[stb], pwrites=[dr["_vat_b"]])
        lin_tok_stage(cx, R, R.hT, (lambda k, ts: R.hTb[ts]), KC, Wm[:, WCOL["v"]:WCOL["v"] + 1024], 1024, v_evac)


def attn_setup(cx, R, dr):
    A = Res()
    cv = Carver(R.big)
    T = dr["zT"].shape[1]
    A.T = T
    A.qr = cv.take(T, BF16).rearrange("p (c t) -> p c t", c=2)
    A.kr = cv.take(T, BF16).rearrange("p (c t) -> p c t", c=2)
    A.V = cv.take(T, BF16).rearrange("p (t v) -> p t v", v=256)
    A.masks = cv.take(1024, BF16).rearrange("p (m q) -> p m q", m=4)
    A.ones_bf = cv.take(64, BF16)
    A.ones_f = cv.take(128)
    A.perm = cv.take(128)
    A.P = Ring([cv.take(256, BF16) for _ in range(3)], "P")
    A.od = [cv.take(512) for _ in range(2)]
    A.odb = [Buf("od0"), Buf("od1")]
    A.sq = [cv.take(512) for _ in range(2)]
    A.sqb = [Buf("sq0"), Buf("sq1")]
    A.fin = Ring([cv.take(256, BF16) for _ in range(2)], "fin")
    A.qrb, A.krb, A.Vb, A.cb = Buf("qr"), Buf("kr"), Buf("V"), Buf("aconst")
    cx.dma("sp", A.masks, dr["amask"].rearrange("m p q -> p m q"), writes=[A.cb])
    cx.dma("sp", A.perm, dr["perm"], pwrites=[A.cb])
    cx.pool(lambda e: e.memset(A.ones_bf, 1.0), pwrites=[A.cb])
    cx.pool(lambda e: e.memset(A.ones_f, 1.0), pwrites=[A.cb])
    sm = R.small
    A.smb = Buf("small")
    l4, l4b = R.qring.next()
    cx.dma("sp", l4[:], dr["lam4"][0:1, :].partition_broadcast(128), writes=[l4b])
    cx.dve(lambda e: e.tensor_tensor(out=l4[:, 0:128], in0=l4[:, 0:128], in1=l4[:, 128:256], op=ALU.mult),
           reads=[l4b], writes=[l4b])
    cx.dve(lambda e: e.tensor_tensor(out=l4[:, 256:384], in0=l4[:, 256:384], in1=l4[:, 384:512], op=ALU.mult),
           reads=[l4b], writes=[l4b])
    cx.dve(lambda e: e.tensor_reduce(out=sm[:, 8:9], in_=l4[:, 0:128], axis=AX.X, op=ALU.add), reads=[l4b], writes=[A.smb])
    cx.dve(lambda e: e.tensor_reduce(out=sm[:, 9:10], in_=l4[:, 256:384], axis=AX.X, op=ALU.add), reads=[l4b], writes=[A.smb])
    cx.act(lambda e: e.activation(out=sm[:, 10:12], in_=sm[:, 8:10], func=AF.Exp), reads=[A.smb], writes=[A.smb])
    cx.dve(lambda e: e.tensor_tensor(out=sm[:, 0:1], in0=sm[:, 11:12], in1=sm[:, 10:11], op=ALU.subtract),
           reads=[A.smb], writes=[A.smb])
    cx.dve(lambda e: e.tensor_scalar(out=sm[:, 0:1], in0=sm[:, 0:1], scalar1=float(-LAMBDA_INIT), scalar2=None, op0=ALU.add),
           reads=[A.smb], writes=[A.smb])
    cx.dma("sp", sm[:, 1:3], dr["subln_g"].rearrange("o (m p) -> p (o m)", p=128), writes=[A.smb],
           allow_slow_non_contiguous=True)
    cx.dve(lambda e: e.tensor_scalar(out=sm[:, 1:3], in0=sm[:, 1:3], scalar1=float(1.0 - LAMBDA_INIT), scalar2=None,
                                     op0=ALU.mult), reads=[A.smb], writes=[A.smb])
    A.neg_lam = sm[:, 0:1]
    A.gsub = sm[:, 1:3]
    return A


def attn_head(cx, R, A, dr, h):
    T = A.T
    NTI = T // TT
    zT = dr["zT"]
    for grp, dst, dstb in (("q", A.qr, A.qrb), ("k", A.kr, A.krb)):
        first = True
        for c in range(2):
            r0 = ZROW[grp] + h * 256 + c * 128
            for i in range(NTI):
                ts_ = slice(i * TT, (i + 1) * TT)
                raw, rb = R.qring.next()
                cx.dma("sp", raw[:], zT[r0:r0 + 128, ts_], reads=[dr["_zT_b"]], writes=[rb])
                cs, csb = R.qring.next()
                cx.dma("sp", cs[0:32, :], dr["cosT"][:, ts_], writes=[csb])
                sn, snb = R.qring.next()
                cx.dma("sp", sn[0:32, :], dr["sinT"][:, ts_], writes=[snb])
                bi = i % 2
                bank, bb = R.banks[bi], R.bankb[bi]
                cx.pe(lambda e, bank=bank, raw=raw: e.matmul(bank[:], A.perm, raw[:], start=True, stop=True),
                      reads=[rb, A.cb], writes=[bb])
                wr = dict(writes=[dstb]) if first else dict(pwrites=[dstb])
                first = False
                ordb = Buf("ord")
                wr = dict(writes=[dstb, ordb]) if "writes" in wr else dict(pwrites=[dstb], writes=[ordb])
                cx.act(lambda e, raw=raw, dst=dst, c=c, ts_=ts_: e.activation(out=dst[:, c, ts_], in_=raw[:, :],
                                                                        func=AF.Copy), reads=[rb], **wr)
                cx.dve(lambda e, raw=raw, cs=cs: e.tensor_tensor(out=cs[0:32, :], in0=raw[0:32, :], in1=cs[0:32, :],
                                                             op=ALU.mult), reads=[rb, csb], writes=[csb])
                cx.dve(lambda e, bank=bank, sn=sn: e.tensor_tensor(out=sn[0:32, :], in0=sn[0:32, :], in1=bank[0:32, :],
                                                               op=ALU.mult), reads=[bb, snb], writes=[snb])
                cx.dve(lambda e, cs=cs, sn=sn, dst=dst, c=c, ts_=ts_: e.tensor_tensor(
                    out=dst[0:32, c, ts_], in0=cs[0:32, :], in1=sn[0:32, :], op=ALU.add),
                    reads=[csb, snb, ordb], pwrites=[dstb])
    cx.dma("sp", A.V, dr["vat"][:, h * 256:(h + 1) * 256].rearrange("(t p) v -> p t v", p=128),
           reads=[dr["_vat_b"]], writes=[A.Vb])
    for j in range(NTI):
        qs = slice(j * TT, (j + 1) * TT)
        for c in range(2):
            nkt = 4 * j + 4
            ob = [2, 3, 4] if c == 0 else [5, 6, 7]
            for kt in range(nkt):
                sbi = kt % 2
                sbank, sbb = R.banks[sbi], R.bankb[sbi]
                cx.pe(lambda e, sbank=sbank, c=c, kt=kt, qs=qs: e.matmul(
                    sbank[:], A.kr[:, c, kt * 128:(kt + 1) * 128], A.qr[:, c, qs], start=True, stop=True),
                    reads=[A.krb, A.qrb], writes=[sbb])
                pt, ptb = A.P.next()
                cx.act(lambda e, pt=pt, sbank=sbank: e.activation(out=pt, in_=sbank[:], func=AF.Exp, scale=float(ATT_SCALE)),
                       reads=[sbb], writes=[ptb])
                if kt >= 4 * j:
                    m = kt - 4 * j
                    cx.pool(lambda e, pt=pt, m=m: e.tensor_tensor(out=pt, in0=pt, in1=A.masks[:, m, :], op=ALU.mult),
                            reads=[ptb, A.cb], writes=[ptb])
                for mi, lhs in enumerate((A.V[:, kt, 0:128], A.V[:, kt, 128:256], A.ones_bf)):
                    cx.pe(lambda e, mi=mi, lhs=lhs, pt=pt, kt=kt, nkt=nkt, ob=ob: e.matmul(
                        R.banks[ob[mi]][:], lhs, pt, start=(kt == 0), stop=(kt == nkt - 1)),
                        reads=[A.Vb, A.cb, ptb], writes=[R.bankb[ob[mi]]], sig=(mi == 2))
            rec, recb = R.qring.next()
            cx.dve(lambda e, rec=rec, ob=ob: e.reciprocal(out=rec[:], in_=R.banks[ob[2]][:]),
                   reads=[R.bankb[ob[2]]], writes=[recb])
            for m in range(2):
                if c == 0:
                    cx.dve(lambda e, m=m, rec=rec, ob=ob: e.tensor_tensor(out=A.od[m], in0=R.banks[ob[m]][:], in1=rec[:],
                                                                      op=ALU.mult),
                           reads=[R.bankb[ob[m]], recb], writes=[A.odb[m]])
                else:
                    tmp, tmpb = R.qring.next()
                    cx.dve(lambda e, m=m, rec=rec, ob=ob, tmp=tmp: e.scalar_tensor_tensor(
                        out=tmp[:], in0=R.banks[ob[m]][:], scalar=A.neg_lam, in1=rec[:], op0=ALU.mult, op1=ALU.mult),
                        reads=[R.bankb[ob[m]], recb, A.smb], writes=[tmpb])
                    cx.pool(lambda e, m=m, tmp=tmp: e.tensor_tensor(out=A.od[m], in0=A.od[m], in1=tmp[:], op=ALU.add),
                            reads=[tmpb, A.odb[m]], writes=[A.odb[m]])
        for m in range(2):
            cx.act(lambda e, m=m: e.activation(out=A.sq[m], in_=A.od[m], func=AF.Square), reads=[A.odb[m]], writes=[A.sqb[m]])
        ssb_i = 0
        for m in range(2):
            cx.pe(lambda e, m=m: e.matmul(R.banks[ssb_i][:], A.ones_f, A.sq[m], start=(m == 0), stop=(m == 1)),
                  reads=[A.sqb[m], A.cb], writes=[R.bankb[ssb_i]], sig=(m == 1))
        rs, rsb = R.qring.next()
        cx.act(lambda e, rs=rs: e.activation(out=rs[:], in_=R.banks[ssb_i][:], func=AF.Sqrt, bias=float(RMS_EPS),
                                             scale=1.0 / 256), reads=[R.bankb[ssb_i]], writes=[rsb])
        cx.dve(lambda e, rs=rs: e.reciprocal(out=rs[:], in_=rs[:]), reads=[rsb], writes=[rsb])
        for m in range(2):
            fin, finb = A.fin.next()
            cx.dve(lambda e, m=m, rs=rs, fin=fin: e.scalar_tensor_tensor(
                out=fin, in0=A.od[m], scalar=A.gsub[:, m:m + 1], in1=rs[:], op0=ALU.mult, op1=ALU.mult),
                reads=[A.odb[m], rsb, A.smb], writes=[finb])
            cx.dma("sp", dr["oT_loc"][h * 256 + m * 128:h * 256 + (m + 1) * 128, qs], fin, reads=[finb],
                   pwrites=[dr["_oT_b"]])


def host_consts(T):
    import ml_dtypes
    c = {}
    c["ident"] = np.eye(128, dtype=np.float32)
    inv = (500000.0 ** (-np.arange(0, 32, 2, dtype=np.float32) / 32)).astype(np.float32)
    ang = np.arange(T, dtype=np.float32)[None, :] * np.concatenate([inv, inv])[:, None]
    c["cosT"] = np.cos(ang).astype(np.float32)
    c["sinT"] = np.sin(ang).astype(np.float32)
    perm = np.zeros((128, 128), np.float32)
    for d in range(16):
        perm[d + 16, d] = -1.0
        perm[d, d + 16] = 1.0
    c["perm"] = perm
    k = np.arange(128)[None, :, None]
    q = np.arange(512)[None, None, :]
    m = np.arange(4)[:, None, None]
    c["amask"] = (((m * 128 + k) // 64) <= (q // 64)).astype(np.float32).astype(ml_dtypes.bfloat16)
    p = np.arange(128)[:, None] % 64
    f = np.arange(512)[None, :] % 64
    rm = np.zeros((5, 128, 512), np.float32)
    rm[0] = np.broadcast_to((f != 0), (128, 512))
    rm[1] = (p < f)
    rm[2] = (p <= f)
    rm[3] = (f < p)
    rm[4] = (p == f)
    c["rmask"] = rm
    pp = np.arange(128)
    c["bones"] = (pp[:, None] // 64 == pp[None, :] // 64).astype(np.float32)
    return c


def declare_b_inputs(nc, dr, T, with_w=False):
    ext = lambda name, shape, dt=F32: nc.dram_tensor(name, shape, dt, kind="ExternalInput").ap()
    if with_w:
        dr["w_in_mine"] = ext("w_in_mine", [D, NWC])
    dr["lam4"] = ext("lam4", [1, 512])
    dr["subln_g"] = ext("subln_g", [1, 256])
    dr["cosT"] = ext("cosT", [32, T])
    dr["sinT"] = ext("sinT", [32, T])
    dr["perm"] = ext("perm", [128, 128])
    dr["amask"] = ext("amask", [4, 128, 512], BF16)
    dr["rmask"] = ext("rmask", [5, 128, 512])
    dr["bones"] = ext("bones", [128, 128])
    dr["pcols"] = ext("pcols", [8, 1024])
    dr["pcl"] = ext("pcl", [4, 128])
    dr["w2m"] = ext("w2m", [96, 1024])
    dr["a2m"] = ext("a2m", [96, 1024])
    dr["g2m"] = ext("g2m", [256, 1024])
    dr["lnw_t"] = ext("lnw_t", [2, 512])
    dr["lnb_t"] = ext("lnb_t", [2, 512])


def build_b_test(T, upto=9):
    nc = bass.Bass("TRN2", target_bir_lowering=False)
    dr = {}
    dr["ident"] = nc.dram_tensor("ident", [128, 128], F32, kind="ExternalInput").ap()
    dr["h2T_seq"] = nc.dram_tensor("h2T_seq", [1, D, T], BF16, kind="ExternalInput").ap()
    declare_b_inputs(nc, dr, T, with_w=True)
    dr["zT"] = nc.dram_tensor("zT", [NZ, T], F32, kind="ExternalOutput").ap()
    dr["vat"] = nc.dram_tensor("vat", [T, 1024], BF16, kind="Internal").ap()
    dr["oT_loc"] = nc.dram_tensor("oT_loc", [2048, T], BF16, kind="ExternalOutput").ap()
    for k in ("h2T_seq", "zT", "vat", "oT"):
        dr["_%s_b" % k] = Buf(k)
    cx = Cx(nc)
    with ExitStack() as es:
        R = alloc_common(nc, es, cx)
        load_consts(cx, R, dr)
        proj_stage(cx, R, dr, T)
        if upto >= 2 and upto != 3:
            A = attn_setup(cx, R, dr)
            for h in range(4):
                attn_head(cx, R, A, dr, h)
        if upto >= 3:
            cx.barrier()
            Wk = rwkv_setup(cx, R, dr)
            for ti in range(T // TT):
                rwkv_tile(cx, R, Wk, dr, ti)
        cx.final_wait("sp", [dr["_oT_b"], dr["_zT_b"]])
        cx.emit(es)
    return nc, cx


RW_LN_EPS = 64e-5
PC_MU_R, PC_MU_K, PC_MU_V, PC_W0, PC_A0, PC_KK, PC_KA, PC_RK = range(8)


def rwkv_setup(cx, R, dr):
    W = Res()
    cv = Carver(R.big)
    T = dr["zT"].shape[1]
    W.T = T
    f32t = lambda: cv.take(512)
    bft = lambda: cv.take(256, BF16)
    W.names = {}
    for n in ("xr", "xk", "xv", "t1", "EW", "A", "kk", "kmod", "lg", "eg", "egi", "egm", "y1", "y2", "y3"):
        setattr(W, n, f32t())
        setattr(W, n + "b", Buf(n))
    W.raw = Ring([cv.take(516) for _ in range(2)], "raw")
    for n in ("Rt", "Kt", "Bt", "At", "RKt", "Q", "N", "Q2", "N2", "P", "Pp", "MakT", "MrbT", "MrkT",
              "Ktm", "Btm", "Vtm", "otm", "oT", "twl", "tal"):
        setattr(W, n, bft())
        setattr(W, n + "b", Buf(n))
    W.sgl = cv.take(512, BF16).rearrange("p (k t) -> p k t", k=2)
    W.sglb = Buf("sgl")
    W.x1 = Ring([cv.take(32, BF16) for _ in range(2)], "x1")
    W.u = Ring([cv.take(32, BF16) for _ in range(2)], "u")
    W.ht = Ring([cv.take(64) for _ in range(2)], "ht")
    W.Hf = cv.take(512)
    W.Hb = cv.take(256, BF16)
    W.Hbuf = [Buf("H%d" % i) for i in range(8)]
    W.PC = cv.take(64).rearrange("p (i c) -> p i c", c=8)
    W.PL = cv.take(8)
    W.negw0 = cv.take(8)
    W.w2 = cv.take(512, BF16)
    W.a2 = cv.take(512, BF16)
    W.g2 = cv.take(1024, BF16).rearrange("p (k c) -> p k c", k=2)
    W.lnw2 = cv.take(512)
    W.lnb2 = cv.take(512)
    W.lnw = W.lnw2.rearrange("p (a v) -> p a v", v=64)
    W.lnb = W.lnb2.rearrange("p (a v) -> p a v", v=64)
    W.masks = cv.take(2560).rearrange("p (m f) -> p m f", m=5)
    W.bones = cv.take(64, BF16)
    W.identb = cv.take(64, BF16)
    W.onesc = cv.take(4, BF16)
    W.s8 = Ring([cv.take(8) for _ in range(6)], "s8")
    W.cb = Buf("rconst")
    cx.dma("sp", W.masks, dr["rmask"].rearrange("m p f -> p m f"), writes=[W.cb])
    cx.dma("pool", W.bones, dr["bones"], pwrites=[W.cb])
    cx.dma("pool", W.identb, dr["ident"], pwrites=[W.cb])
    cx.dma("pool", W.w2[0:96, :], dr["w2m"], pwrites=[W.cb])
    cx.dma("pool", W.a2[0:96, :], dr["a2m"], pwrites=[W.cb])
    cx.dma("pool", W.g2, dr["g2m"].rearrange("(k p) c -> p k c", p=128), pwrites=[W.cb])
    cx.pool(lambda e: e.memset(W.onesc, 1.0), pwrites=[W.cb])
    for hp in range(2):
        cx.dma("sp", W.lnw2[hp * 64:(hp + 1) * 64, :], dr["lnw_t"][hp:hp + 1, :].partition_broadcast(64), pwrites=[W.cb])
        cx.dma("sp", W.lnb2[hp * 64:(hp + 1) * 64, :], dr["lnb_t"][hp:hp + 1, :].partition_broadcast(64), pwrites=[W.cb])
    t, tb = R.qring.next()
    cx.dma("sp", t[0:64, 0:128], dr["pcols"].rearrange("i (c p) -> (i c) p", p=128), writes=[tb])
    cx.pe(lambda e: e.transpose(out=R.banks[0][:, 0:64], in_=t[0:64, 0:128], identity=R.ident[0:64, 0:64]),
          reads=[tb, R.identb], writes=[R.bankb[0]])
    cx.dve(lambda e: e.tensor_copy(out=W.PC, in_=R.banks[0][:, 0:64].rearrange("p (i c) -> p i c", c=8)),
           reads=[R.bankb[0]], pwrites=[W.cb])
    cx.dve(lambda e: e.tensor_scalar(out=W.negw0, in0=R.banks[0][:, PC_W0 * 8:PC_W0 * 8 + 8], scalar1=-1.0, scalar2=None,
                                     op0=ALU.mult), reads=[R.bankb[0]], pwrites=[W.cb])
    t2, t2b = R.qring.next()
    cx.dma("sp", t2[0:4, 0:128], dr["pcl"], writes=[t2b])
    cx.pe(lambda e: e.transpose(out=R.banks[1][:, 0:4], in_=t2[0:4, 0:128], identity=R.ident[0:4, 0:4]),
          reads=[t2b, R.identb], writes=[R.bankb[1]])
    cx.dve(lambda e: e.tensor_copy(out=W.PL[:, 0:4], in_=R.banks[1][:, 0:4]), reads=[R.bankb[1]], pwrites=[W.cb])
    cx.dve(lambda e: e.memset(W.Hf, 0.0), writes=W.Hbuf)
    cx.dve(lambda e: e.memset(W.Hb, 0.0), pwrites=W.Hbuf)
    return W


def rwkv_tile(cx, R, W, dr, ti):
    zT = dr["zT"]
    t0 = ti * TT
    MSK_RESET, MSK_U, MSK_UD, MSK_L = 0, 1, 2, 3
    cb = W.cb

    def shifted(row0, nrow, mu_col, out, outb, extra_reads=()):
        raw, rb = W.raw.next()
        if ti == 0:
            cx.dve(lambda e: e.memset(raw[0:nrow, 0:1], 0.0), writes=[rb])
            cx.dma("sp", raw[0:nrow, 1:513], zT[row0:row0 + nrow, 0:TT], reads=[dr["_zT_b"], rb], pwrites=[rb])
        else:
            cx.dma("sp", raw[0:nrow, 0:513], zT[row0:row0 + nrow, t0 - 1:t0 + TT], reads=[dr["_zT_b"]], writes=[rb])
        d, db = R.qring.next()
        cx.dve(lambda e: e.tensor_tensor(out=d[0:nrow, :], in0=raw[0:nrow, 0:512], in1=raw[0:nrow, 1:513], op=ALU.subtract),
               reads=[rb], writes=[db])
        cx.dve(lambda e: e.scalar_tensor_tensor(out=out, in0=d[0:nrow, :], scalar=mu_col, in1=raw[0:nrow, 1:513],
                                                op0=ALU.mult, op1=ALU.add),
               reads=[db, rb, cb] + list(extra_reads), writes=[outb])

    tl, tlb = R.qring.next()
    shifted(ZROW["wl"], 96, W.PL[0:96, 0:1], tl[0:96, :], tlb)
    cx.act(lambda e: e.activation(out=W.twl[0:96, :], in_=tl[0:96, :], func=AF.Tanh), reads=[tlb], writes=[W.twlb])
    tl2, tl2b = R.qring.next()
    shifted(ZROW["al"], 96, W.PL[0:96, 1:2], tl2[0:96, :], tl2b)
    cx.act(lambda e: e.activation(out=W.tal[0:96, :], in_=tl2[0:96, :], func=AF.Copy), reads=[tl2b], writes=[W.talb])
    for k in range(2):
        tg, tgb = R.qring.next()
        shifted(ZROW["gl"] + k * 128, 128, W.PL[:, 2 + k:3 + k], tg[:], tgb)
        kw = dict(writes=[W.sglb]) if k == 0 else dict(pwrites=[W.sglb])
        cx.act(lambda e, k=k, tg=tg: e.activation(out=W.sgl[:, k, :], in_=tg[:], func=AF.Sigmoid), reads=[tgb], **kw)

    for cc in range(8):
        col = lambda i: W.PC[:, i, cc:cc + 1]
        shifted(ZROW["r"] + cc * 128, 128, col(PC_MU_R), W.xr, W.xrb)
        shifted(ZROW["rk"] + cc * 128, 128, col(PC_MU_K), W.xk, W.xkb)
        shifted(ZROW["rv"] + cc * 128, 128, col(PC_MU_V), W.xv, W.xvb)
        b0, b1 = R.banks[0], R.banks[1]
        cx.pe(lambda e, cc=cc: e.matmul(b0[:], W.w2[0:96, cc * 128:(cc + 1) * 128], W.twl[0:96, :], start=True, stop=True),
              reads=[cb, W.twlb], writes=[R.bankb[0]])
        cx.dve(lambda e, cc=cc: e.tensor_scalar(out=W.t1, in0=b0[:], scalar1=W.PC[:, PC_W0, cc:cc + 1], scalar2=None,
                                                op0=ALU.add), reads=[R.bankb[0], cb], writes=[W.t1b])
        cx.act(lambda e: e.activation(out=W.t1, in_=W.t1, func=AF.Exp, scale=-1.0), reads=[W.t1b], writes=[W.t1b])
        cx.act(lambda e: e.activation(out=W.t1, in_=W.t1, func=AF.Ln, bias=1.0), reads=[W.t1b], writes=[W.t1b])
        cx.act(lambda e: e.activation(out=W.EW, in_=W.t1, func=AF.Exp, scale=-1.0, bias=-0.5), reads=[W.t1b], writes=[W.EWb])
        cx.pe(lambda e, cc=cc: e.matmul(b1[:], W.a2[0:96, cc * 128:(cc + 1) * 128], W.tal[0:96, :], start=True, stop=True),
              reads=[cb, W.talb], writes=[R.bankb[1]])
        cx.dve(lambda e, cc=cc: e.tensor_scalar(out=W.A, in0=b1[:], scalar1=W.PC[:, PC_A0, cc:cc + 1], scalar2=None,
                                                op0=ALU.add), reads=[R.bankb[1], cb], writes=[W.Ab])
        cx.act(lambda e: e.activation(out=W.A, in_=W.A, func=AF.Sigmoid), reads=[W.Ab], writes=[W.Ab])
        cx.dve(lambda e, cc=cc: e.tensor_scalar(out=W.kk, in0=W.xk, scalar1=W.PC[:, PC_KK, cc:cc + 1], scalar2=None,
                                                op0=ALU.mult), reads=[W.xkb, cb], writes=[W.kkb])
        cx.act(lambda e: e.activation(out=W.Rt, in_=W.kk, func=AF.Square), reads=[W.kkb], writes=[W.Rtb])
        cx.pe(lambda e: e.matmul(b0[:], W.bones, W.Rt, start=True, stop=True), reads=[cb, W.Rtb], writes=[R.bankb[0]])
        cx.act(lambda e: e.activation(out=W.t1, in_=b0[:], func=AF.Sqrt), reads=[R.bankb[0]], writes=[W.t1b])
        cx.dve(lambda e: e.tensor_scalar(out=W.t1, in0=W.t1, scalar1=1e-12, scalar2=None, op0=ALU.max),
               reads=[W.t1b], writes=[W.t1b])
        cx.dve(lambda e: e.reciprocal(out=W.t1, in_=W.t1), reads=[W.t1b], writes=[W.t1b])
        cx.dve(lambda e: e.tensor_tensor(out=W.kk, in0=W.kk, in1=W.t1, op=ALU.mult), reads=[W.kkb, W.t1b], writes=[W.kkb])
        cx.dve(lambda e, cc=cc: e.tensor_scalar(out=W.kmod, in0=W.A, scalar1=-1.0, scalar2=W.PC[:, PC_KA, cc:cc + 1],
                                                op0=ALU.add, op1=ALU.mult), reads=[W.Ab, cb], writes=[W.kmodb])
        cx.dve(lambda e: e.scalar_tensor_tensor(out=W.kmod, in0=W.kmod, scalar=1.0, in1=W.xk, op0=ALU.add, op1=ALU.mult),
               reads=[W.kmodb, W.xkb], writes=[W.kmodb])
        cx.dve(lambda e: e.tensor_tensor_scan(out=W.lg, data0=W.masks[:, MSK_RESET, :], data1=W.EW, initial=0.0,
                                              op0=ALU.mult, op1=ALU.subtract), reads=[W.EWb, cb], writes=[W.lgb])
        cx.act(lambda e: e.activation(out=W.eg, in_=W.lg, func=AF.Exp), reads=[W.lgb], writes=[W.egb])
        cx.act(lambda e: e.activation(out=W.egi, in_=W.lg, func=AF.Exp, scale=-1.0), reads=[W.lgb], writes=[W.egib])
        cx.dve(lambda e: e.tensor_tensor(out=W.egm, in0=W.lg, in1=W.EW, op=ALU.add), reads=[W.lgb, W.EWb], writes=[W.egmb])
        cx.act(lambda e: e.activation(out=W.egm, in_=W.egm, func=AF.Exp), reads=[W.egmb], writes=[W.egmb])
        cx.dve(lambda e: e.tensor_tensor(out=W.Rt, in0=W.xr, in1=W.eg, op=ALU.mult), reads=[W.xrb, W.egb], writes=[W.Rtb])
        cx.dve(lambda e: e.tensor_tensor(out=W.Kt, in0=W.kmod, in1=W.egi, op=ALU.mult), reads=[W.kmodb, W.egib], writes=[W.Ktb])
        cx.dve(lambda e: e.tensor_tensor(out=W.t1, in0=W.kk, in1=W.A, op=ALU.mult), reads=[W.kkb, W.Ab], writes=[W.t1b])
        cx.dve(lambda e: e.tensor_tensor(out=W.Bt, in0=W.t1, in1=W.egi, op=ALU.mult), reads=[W.t1b, W.egib], writes=[W.Btb])
        cx.dve(lambda e: e.scalar_tensor_tensor(out=W.At, in0=W.kk, scalar=-1.0, in1=W.egm, op0=ALU.mult, op1=ALU.mult),
               reads=[W.kkb, W.egmb], writes=[W.Atb])
        cx.dve(lambda e, cc=cc: e.scalar_tensor_tensor(out=W.RKt, in0=W.xr, scalar=W.PC[:, PC_RK, cc:cc + 1], in1=W.kmod,
                                                       op0=ALU.mult, op1=ALU.mult),
               reads=[W.xrb, W.kmodb, cb], writes=[W.RKtb])
        cx.act(lambda e: e.activation(out=W.oT, in_=W.xv, func=AF.Copy), reads=[W.xvb], writes=[W.oTb])

        blk = lambda ap, hp, c: ap[hp * 64:(hp + 1) * 64, c * 64:(c + 1) * 64]

        def blockmm(bank_i, lhs, lhsb, rhs, rhsb):
            bank, bb = R.banks[bank_i], R.bankb[bank_i]
            n = 0
            for hp in range(2):
                for c in range(8):
                    n += 1
                    cx.pe(lambda e, hp=hp, c=c: e.matmul(blk(bank, hp, c), blk(lhs, hp, c), blk(rhs, hp, c),
                                                         start=True, stop=True),
                          reads=[lhsb, rhsb], writes=[bb], sig=(n == 16))
            return bank, bb

        def scoremm(bank_i, lhs, lhsb, rhs, rhsb, mask_i, out, outb, eng):
            bank, bb = blockmm(bank_i, lhs, lhsb, rhs, rhsb)
            cx.dve(lambda e: e.tensor_tensor(out=out, in0=bank[:], in1=W.masks[:, mask_i, :], op=ALU.mult),
                   reads=[bb, cb], writes=[outb])

        scoremm(2, W.Bt, W.Btb, W.At, W.Atb, MSK_U, W.Q, W.Qb, "dve")
        scoremm(3, W.At, W.Atb, W.Bt, W.Btb, MSK_L, W.N, W.Nb, "dve")
        scoremm(4, W.Kt, W.Ktb, W.At, W.Atb, MSK_U, W.MakT, W.MakTb, "dve")
        scoremm(5, W.Bt, W.Btb, W.Rt, W.Rtb, MSK_UD, W.MrbT, W.MrbTb, "dve")
        scoremm(6, W.Kt, W.Ktb, W.Rt, W.Rtb, MSK_UD, W.MrkT, W.MrkTb, "dve")
        for (dst, dstb, src, srcb) in ((W.P, W.Pb, W.Q, W.Qb), (W.Pp, W.Ppb, W.N, W.Nb)):
            cx.dve(lambda e, dst=dst, src=src: e.tensor_tensor(out=dst, in0=src, in1=W.masks[:, 4, :], op=ALU.add),
                   reads=[srcb, cb], writes=[dstb])
        Qc, Qcb, Nc, Ncb = W.Q, W.Qb, W.N, W.Nb
        Qn, Qnb, Nn, Nnb = W.Q2, W.Q2b, W.N2, W.N2b
        for lvl in range(5):
            last = lvl == 4
            bk, bkb = blockmm(2, Nc, Ncb, Qc, Qcb)
            cx.act(lambda e, bk=bk, Qn=Qn: e.activation(out=Qn, in_=bk[:], func=AF.Copy), reads=[bkb], writes=[Qnb])
            if not last:
                bk2, bk2b = blockmm(3, Qc, Qcb, Nc, Ncb)
                cx.act(lambda e, bk2=bk2, Nn=Nn: e.activation(out=Nn, in_=bk2[:], func=AF.Copy), reads=[bk2b], writes=[Nnb])
            bp, bpb = blockmm(4, W.Pp, W.Ppb, Qn, Qnb)
            if not last:
                bq, bqb = blockmm(5, W.P, W.Pb, Nn, Nnb)
            cx.dve(lambda e, bp=bp: e.tensor_tensor(out=W.P, in0=bp[:], in1=W.P, op=ALU.add), reads=[bpb, W.Pb], writes=[W.Pb])
            if not last:
                cx.dve(lambda e, bq=bq: e.tensor_tensor(out=W.Pp, in0=bq[:], in1=W.Pp, op=ALU.add),
                       reads=[bqb, W.Ppb], writes=[W.Ppb])
            Qc, Qcb, Nc, Ncb, Qn, Qnb, Nn, Nnb = Qn, Qnb, Nn, Nnb, Qc, Qcb, Nc, Ncb
        for src, srcb, dst, dstb, bi in ((W.Kt, W.Ktb, W.Ktm, W.Ktmb, 6), (W.Bt, W.Btb, W.Btm, W.Btmb, 7),
                                         (W.oT, W.oTb, W.Vtm, W.Vtmb, 6)):
            bank, bb = R.banks[bi], R.bankb[bi]
            bv = bank[:].bitcast(BF16)
            n = 0
            for hp in range(2):
                for c in range(8):
                    n += 1
                    cx.pe(lambda e, hp=hp, c=c, bv=bv, src=src: e.transpose(
                        out=blk(bv, hp, c), in_=blk(src, hp, c), identity=W.identb[hp * 64:(hp + 1) * 64, hp * 64:(hp + 1) * 64]),
                        reads=[srcb, cb], writes=[bb], sig=(n == 16))
            cx.act(lambda e, bv=bv, dst=dst: e.activation(out=dst, in_=bv[:, 0:512], func=AF.Copy), reads=[bb], writes=[dstb])
        bS, bSb = R.banks[7], R.bankb[7]
        n = 0
        for hp in range(2):
            for c in range(8):
                n += 1
                cx.pe(lambda e, hp=hp, c=c: e.matmul(bS[hp * 64:(hp + 1) * 64, 256 + c:257 + c], blk(W.RKt, hp, c),
                                                     W.onesc[hp * 64:(hp + 1) * 64, 0:1], start=True, stop=True),
                      reads=[W.RKtb, cb], writes=[bSb], sig=(n == 16))
        rk8, rk8b = W.s8.next()
        cx.dve(lambda e, rk8=rk8: e.tensor_copy(out=rk8, in_=bS[:, 256:264]), reads=[bSb], writes=[rk8b])
        bG, bGb = R.banks[1], R.bankb[1]
        n = 0
        for hp in range(2):
            for c in range(8):
                for k in range(2):
                    n += 1
                    cx.pe(lambda e, hp=hp, c=c, k=k, cc=cc: e.matmul(
                        blk(bG, hp, c), W.sgl[:, k, c * 64:(c + 1) * 64],
                        W.g2[:, k, cc * 128 + hp * 64:cc * 128 + (hp + 1) * 64], start=(k == 0), stop=(k == 1)),
                        reads=[W.sglb, cb], writes=[bGb], sig=(n == 32))
        bY, bYb = R.banks[0], R.bankb[0]
        Hcol = slice(cc * 64, (cc + 1) * 64)
        Hbf = W.Hbuf[cc]
        for c in range(8):
            bX, bXb = R.banks[2 + (c % 2)], R.bankb[2 + (c % 2)]
            for hp in range(2):
                rows = slice(hp * 64, (hp + 1) * 64)
                cx.pe(lambda e, bX=bX, Hcol=Hcol, rows=rows, hp=hp, c=c: e.matmul(bX[rows, 0:64], blk(W.At, hp, c), W.Hb[rows, Hcol],
                                                             start=True, stop=False), reads=[W.Atb, Hbf], writes=[bXb], sig=False)
                cx.pe(lambda e, bX=bX, Hcol=Hcol, rows=rows, hp=hp, c=c: e.matmul(bX[rows, 0:64], blk(W.MakT, hp, c), blk(W.Vtm, hp, c),
                                                             start=False, stop=True), reads=[W.MakTb, W.Vtmb], writes=[bXb],
                      sig=(hp == 1))
            x1, x1b = W.x1.next()
            cx.act(lambda e, bX=bX, Hcol=Hcol, x1=x1: e.activation(out=x1, in_=bX[:, 0:64], func=AF.Copy), reads=[bXb], writes=[x1b])
            for hp in range(2):
                rows = slice(hp * 64, (hp + 1) * 64)
                cx.pe(lambda e, bX=bX, Hcol=Hcol, rows=rows, hp=hp, c=c, x1=x1: e.matmul(bX[rows, 64:128], blk(W.P, hp, c), x1[rows, :],
                                                                    start=True, stop=True), reads=[W.Pb, x1b], writes=[bXb],
                      sig=(hp == 1))
            u, ub = W.u.next()
            cx.dve(lambda e, bX=bX, Hcol=Hcol, u=u: e.tensor_copy(out=u, in_=bX[:, 64:128]), reads=[bXb], writes=[ub])
            for hp in range(2):
                rows = slice(hp * 64, (hp + 1) * 64)
                cx.pe(lambda e, bX=bX, Hcol=Hcol, rows=rows, hp=hp, c=c: e.matmul(blk(bY, hp, c), blk(W.Rt, hp, c), W.Hb[rows, Hcol],
                                                             start=True, stop=False), reads=[W.Rtb, Hbf], writes=[bYb], sig=False)
                cx.pe(lambda e, bX=bX, Hcol=Hcol, rows=rows, hp=hp, c=c, u=u: e.matmul(blk(bY, hp, c), blk(W.MrbT, hp, c), u[rows, :],
                                                                  start=False, stop=False), reads=[W.MrbTb, ub], writes=[bYb], sig=False)
                cx.pe(lambda e, bX=bX, Hcol=Hcol, rows=rows, hp=hp, c=c: e.matmul(blk(bY, hp, c), blk(W.MrkT, hp, c), blk(W.Vtm, hp, c),
                                                             start=False, stop=True), reads=[W.MrkTb, W.Vtmb], writes=[bYb], sig=False)
                cx.pe(lambda e, bX=bX, Hcol=Hcol, rows=rows, hp=hp, c=c, u=u: e.matmul(bX[rows, 128:192], blk(W.Btm, hp, c), u[rows, :],
                                                                  start=True, stop=False), reads=[W.Btmb, ub], writes=[bXb], sig=False)
                cx.pe(lambda e, bX=bX, Hcol=Hcol, rows=rows, hp=hp, c=c: e.matmul(bX[rows, 128:192], blk(W.Ktm, hp, c), blk(W.Vtm, hp, c),
                                                             start=False, stop=True), reads=[W.Ktmb, W.Vtmb], writes=[bXb],
                      sig=(hp == 1))
            ht, htb = W.ht.next()
            cx.dve(lambda e, bX=bX, Hcol=Hcol, ht=ht: e.tensor_tensor(out=ht, in0=bX[:, 128:192], in1=W.Hf[:, Hcol], op=ALU.add),
                   reads=[bXb, Hbf], writes=[htb])
            gC = W.eg[:, c * 64 + 63:c * 64 + 64]
            cx.dve(lambda e, bX=bX, Hcol=Hcol, ht=ht, gC=gC: e.tensor_scalar(out=W.Hf[:, Hcol], in0=ht, scalar1=gC, scalar2=None, op0=ALU.mult),
                   reads=[htb, W.egb], writes=[Hbf])
            cx.dve(lambda e, Hcol=Hcol: e.tensor_copy(out=W.Hb[:, Hcol], in_=W.Hf[:, Hcol]), reads=[Hbf], writes=[Hbf])
        v3 = lambda ap: ap.rearrange("p (c v) -> p c v", v=64)
        m8, m8b = W.s8.next()
        cx.dve(lambda e, m8=m8: e.tensor_reduce(out=m8, in_=v3(bY[:]), axis=AX.X, op=ALU.add), reads=[bYb], writes=[m8b])
        cx.dve(lambda e, m8=m8: e.tensor_scalar(out=m8, in0=m8, scalar1=-1.0 / 64, scalar2=None, op0=ALU.mult), reads=[m8b], writes=[m8b])
        cx.dve(lambda e, m8=m8: e.tensor_tensor(out=v3(W.y1), in0=v3(bY[:]), in1=m8.unsqueeze(2).to_broadcast([128, 8, 64]),
                                         op=ALU.add), reads=[bYb, m8b], writes=[W.y1b])
        cx.act(lambda e: e.activation(out=W.y2, in_=W.y1, func=AF.Square), reads=[W.y1b], writes=[W.y2b])
        v8, v8b = W.s8.next()
        cx.dve(lambda e, v8=v8: e.tensor_reduce(out=v8, in_=v3(W.y2), axis=AX.X, op=ALU.add), reads=[W.y2b], writes=[v8b])
        cx.act(lambda e, v8=v8: e.activation(out=v8, in_=v8, func=AF.Sqrt, scale=1.0 / 64, bias=float(RW_LN_EPS)), reads=[v8b], writes=[v8b])
        cx.dve(lambda e, v8=v8: e.reciprocal(out=v8, in_=v8), reads=[v8b], writes=[v8b])
        cx.dve(lambda e, v8=v8: e.tensor_tensor(out=v3(W.y1), in0=v3(W.y1), in1=v8.unsqueeze(2).to_broadcast([128, 8, 64]), op=ALU.mult),
               reads=[W.y1b, v8b], writes=[W.y1b])
        cx.dve(lambda e, cc=cc: e.tensor_tensor(out=v3(W.y1), in0=v3(W.y1),
                                                in1=W.lnw[:, cc:cc + 1, :].to_broadcast([128, 8, 64]), op=ALU.mult),
               reads=[W.y1b, cb], writes=[W.y1b])
        cx.dve(lambda e, cc=cc: e.tensor_tensor(out=v3(W.y1), in0=v3(W.y1),
                                                in1=W.lnb[:, cc:cc + 1, :].to_broadcast([128, 8, 64]), op=ALU.add),
               reads=[W.y1b, cb], writes=[W.y1b])
        cx.dve(lambda e, rk8=rk8: e.tensor_tensor(out=v3(W.y2), in0=v3(W.Vtm), in1=rk8.unsqueeze(2).to_broadcast([128, 8, 64]),
                                         op=ALU.mult), reads=[W.Vtmb, rk8b], writes=[W.y2b])
        cx.dve(lambda e: e.tensor_tensor(out=W.y1, in0=W.y1, in1=W.y2, op=ALU.add), reads=[W.y1b, W.y2b], writes=[W.y1b])
        cx.dve(lambda e: e.tensor_tensor(out=W.otm, in0=W.y1, in1=bG[:], op=ALU.mult), reads=[W.y1b, bGb], writes=[W.otmb])
        bank, bb = R.banks[5], R.bankb[5]
        bv = bank[:].bitcast(BF16)
        n = 0
        for hp in range(2):
            for c in range(8):
                n += 1
                cx.pe(lambda e, hp=hp, c=c, bv=bv: e.transpose(
                    out=blk(bv, hp, c), in_=blk(W.otm, hp, c), identity=W.identb[hp * 64:(hp + 1) * 64, hp * 64:(hp + 1) * 64]),
                    reads=[W.otmb, cb], writes=[bb], sig=(n == 16))
        cx.act(lambda e, bv=bv: e.activation(out=W.N2, in_=bv[:, 0:512], func=AF.Copy), reads=[bb], writes=[W.N2b])
        cx.dma("sp", dr["oT_loc"][1024 + cc * 128:1024 + (cc + 1) * 128, t0:t0 + TT], W.N2, reads=[W.N2b],
               pwrites=[dr["_oT_b"]])


from concourse.bass import ds

SEQ = 4096
IN_PROJ = 12736
WALL = ["ffn1_w_gate", "ffn1_w_up", "ffn1_w_down", "w_out", "ffn2_w_gate", "ffn2_w_up", "ffn2_w_down",
        "ple_w_gate", "ple_w_proj"]
WSHAPE["w_out"] = (D, D)
GALL = ["ffn1_pre_g", "ffn1_post_g", "mix_pre_g", "mix_post_g", "ffn2_pre_g", "ffn2_post_g", "ple_pre_g", "ple_post_g"]
WIN_GROUPS = [(0, WCOL["q"]), (2048, WCOL["k"]), (4096, WCOL["v"]), (6144, WCOL["r"]), (8192, WCOL["rk"]),
              (10240, WCOL["rv"])]
WIN_LORA0 = 12288


def wout_rowmap(k0):
    return {0: 0, 8: 16, 16: 8, 24: 24}[k0]


def build_mega_cc(ncores=NCORES):
    ntok = NTOK
    T = SEQ
    nc = bass.Bass("TRN2", target_bir_lowering=False)
    ext = lambda name, shape, dt=F32: nc.dram_tensor(name, shape, dt, kind="ExternalInput").ap()
    loc = lambda name, shape, dt=F32: nc.dram_tensor(name, shape, dt, kind="Internal").ap()
    dr = {}
    dr["x"] = ext("x", [ntok, D])
    dr["p"] = ext("p", [ntok, 256])
    dr["ident"] = ext("ident", [128, 128])
    for g in GALL:
        dr[g] = ext(g, [1, D])
    shards = {}
    for w in WALL:
        r, c = WSHAPE[w]
        shards[w] = ext(w + "_sh", [r // ncores, c])
    win_sh = [ext("w_in_sh%d" % i, [2048 // ncores, IN_PROJ]) for i in range(2)]
    declare_b_inputs(nc, dr, T)
    out = nc.dram_tensor("out", [ntok, D], F32, kind="ExternalOutput").ap()
    x1, x2, x3 = loc("x1", [ntok, D]), loc("x2", [ntok, D]), loc("x3", [ntok, D])
    fscr = loc("fscr", [TT, D])
    h2T_loc = loc("h2T_loc", [D, ntok], BF16)
    dr["h2T_seq"] = loc("h2T_seq", [2, D, ntok], BF16)
    dr["zT"] = loc("zT", [NZ, T])
    dr["vat"] = loc("vat", [T, 1024], BF16)
    dr["oT_loc"] = loc("oT_loc", [2048, T], BF16)
    oT_mine = loc("oT_mine", [2, 2048, ntok], BF16)
    for k in ("h2T_seq", "zT", "vat", "oT"):
        dr["_%s_b" % k] = Buf(k)
    shared = nc.dram_tensor("wshared", [D * DFF], F32, kind="Internal", addr_space="Shared").ap()
    shared_bf = shared.bitcast(BF16)
    shb = Buf("wshared")
    cx = Cx(nc)
    cx.want_pid = True
    wbufs = {}
    groups = [list(range(ncores))]
    with ExitStack() as es:
        R = alloc_common(nc, es, cx)
        R.pT = es.enter_context(nc.sbuf_tensor("sb_pT", [128, 2, TT], BF16))
        R.pTb = [Buf("pT0"), Buf("pT1")]
        R.gstage = [es.enter_context(nc.sbuf_tensor("sb_gst%d" % i, [128, 512], F32)) for i in range(4)]
        R.gstageb = [Buf("gst%d" % i) for i in range(4)]
        load_consts(cx, R, dr)

        def gather_weight(w):
            r, c = WSHAPE[w]
            bounce = loc(w + "_bn", [r // ncores, c])
            full = loc(w, [r, c])
            shv = shared[0:r * c].rearrange("(r c) -> r c", c=c)
            bb, wb = Buf(w + "_bn"), Buf(w)
            cx.dma("pool", bounce, shards[w], writes=[bb])
            cx.cc(lambda e: e.collective_compute("AllGather", ALU.bypass, replica_groups=groups, ins=[bounce], outs=[shv]),
                  reads=[bb], writes=[shb])
            cx.dma("sp", full, shv, reads=[shb], writes=[wb])
            dr[w] = full
            wbufs[w] = wb

        for w in WALL[:3]:
            gather_weight(w)
        w_in_mine = loc("w_in_mine_l", [D, NWC])
        dr["w_in_mine"] = w_in_mine
        winb = Buf("w_in_mine")
        for i in range(2):
            bounce = loc("w_in_bn%d" % i, [2048 // ncores, IN_PROJ])
            shv = shared[0:2048 * IN_PROJ].rearrange("(r c) -> r c", c=IN_PROJ)
            bb = Buf("w_in_bn%d" % i)
            cx.dma("pool", bounce, win_sh[i], writes=[bb])
            cx.cc(lambda e, bounce=bounce, shv=shv: e.collective_compute("AllGather", ALU.bypass, replica_groups=groups,
                                                                         ins=[bounce], outs=[shv]), reads=[bb], writes=[shb])
            rows = slice(i * 2048, (i + 1) * 2048)
            for gbase, mybase in WIN_GROUPS:
                def fn(e, gbase=gbase, mybase=mybase, rows=rows, shv=shv):
                    half = cx.pid % 2
                    return e.dma_start(out=w_in_mine[rows, mybase:mybase + 1024], in_=shv[:, ds(half * 1024 + gbase, 1024)])
                cx.op("pool", fn, reads=[shb], pwrites=[winb], dma=True)
            cx.dma("pool", w_in_mine[rows, WCOL["wl"]:WCOL["wl"] + 448], shv[:, WIN_LORA0:WIN_LORA0 + 448],
                   reads=[shb], pwrites=[winb])
        for w in WALL[3:]:
            gather_weight(w)

        xb, x1b, x2b, x3b, ob, fb, hlb, omb = (Buf(n) for n in ("x", "x1", "x2", "x3", "out", "f", "h2T_loc", "oT_mine"))
        ntile = ntok // TT
        for tt in range(ntile):
            ffn_block(cx, R, dr, "ffn1", dr["x"], xb, x1, x1b, tt * TT, fscr, fb, wbuf=wbufs["ffn1_w_down"])
            norm_stage(cx, R, x1, x1b, tt * TT, GIDX["mix_pre_g"])
            cx.dma("sp", h2T_loc[:, tt * TT:(tt + 1) * TT].rearrange("(kc p) t -> p kc t", p=128), R.hT[:],
                   reads=R.hTb, pwrites=[hlb])
        shv = shared_bf[0:ncores * D * ntok].rearrange("(r t) -> r t", t=ntok)
        cx.cc(lambda e: e.collective_compute("AllGather", ALU.bypass, replica_groups=groups, ins=[h2T_loc], outs=[shv]),
              reads=[hlb], writes=[shb])

        def fn_h(e):
            pair = cx.pid // 2
            src = shv.rearrange("(q r d) t -> q r d t", q=ncores // 2, r=2)[ds(pair, 1), :, :, :]
            return e.dma_start(out=dr["h2T_seq"], in_=src.rearrange("q r d t -> (q r) d t"))
        cx.op("pool", fn_h, reads=[shb], writes=[dr["_h2T_seq_b"]], dma=True)
        cx.barrier()
        proj_stage(cx, R, dr, T)
        A = attn_setup(cx, R, dr)
        for h in range(4):
            attn_head(cx, R, A, dr, h)
        cx.barrier()
        Wk = rwkv_setup(cx, R, dr)
        for ti in range(T // TT):
            rwkv_tile(cx, R, Wk, dr, ti)
        shv2 = shared_bf[0:ncores * 2048 * T].rearrange("(r t) -> r t", t=T)
        cx.cc(lambda e: e.collective_compute("AllGather", ALU.bypass, replica_groups=groups, ins=[dr["oT_loc"]], outs=[shv2]),
              reads=[dr["_oT_b"]], writes=[shb])

        def fn_o(e):
            pair, half = cx.pid // 2, cx.pid % 2
            src = shv2.rearrange("(q r c) t -> q r c t", q=ncores // 2, r=2)[ds(pair, 1), :, :, ds(half * ntok, ntok)]
            return e.dma_start(out=oT_mine, in_=src.rearrange("q r c t -> (q r) c t"))
        cx.op("pool", fn_o, reads=[shb], writes=[omb], dma=True)
        cx.barrier()
        for tt in range(ntile):
            for r in range(2):
                cx.dma("sp", R.hT[:, r * 16:(r + 1) * 16, :],
                       oT_mine[r, :, tt * TT:(tt + 1) * TT].rearrange("(j p) t -> p j t", p=128),
                       reads=[omb], writes=(R.hTb if r == 0 else []), pwrites=([] if r == 0 else R.hTb))
            lin_tok_stage(cx, R, R.hT, (lambda k, ts: R.hTb[ts]), KC, dr["w_out"], D, make_f_evac(cx, R, fscr, fb),
                          wbuf=wbufs["w_out"], rowmap=wout_rowmap)
            finalize_stage(cx, R, fscr, fb, dr["mix_post_g"], x1, x1b, tt * TT, x2, x2b, tt * TT, 1.0)
            ffn_block(cx, R, dr, "ffn2", x2, x2b, x3, x3b, tt * TT, fscr, fb, wbuf=wbufs["ffn2_w_down"])
            ple_block(cx, R, dr, x3, x3b, out, ob, tt * TT, fscr, fb, wbuf=wbufs["ple_w_proj"])
        cx.final_wait("sp", [ob])
        cx.emit(es)
    return nc, cx


def _core_params(inputs, hf):
    f = lambda k: np.asarray(inputs[k], dtype=np.float32)
    sl = slice(hf * 1024, (hf + 1) * 1024)
    mu = f("rwkv_mu").reshape(-1)
    m = {}
    m["lam4"] = np.concatenate([f("diff_lambda_q1").reshape(-1), f("diff_lambda_k1").reshape(-1),
                                f("diff_lambda_q2").reshape(-1), f("diff_lambda_k2").reshape(-1)]).reshape(1, 512)
    m["subln_g"] = f("diff_subln_g").reshape(1, 256)
    m["pcols"] = np.ascontiguousarray(np.stack([
        mu[0:2048][sl], mu[2048:4096][sl], mu[4096:6144][sl], f("rwkv_w0").reshape(-1)[sl], f("rwkv_a0").reshape(-1)[sl],
        f("rwkv_k_k").reshape(-1)[sl], f("rwkv_k_a").reshape(-1)[sl], f("rwkv_r_k").reshape(-1)[sl]]))
    pcl = np.zeros((4, 128), np.float32)
    pcl[0, :96] = mu[6144:6240]
    pcl[1, :96] = mu[6240:6336]
    pcl[2] = mu[6336:6464]
    pcl[3] = mu[6464:6592]
    m["pcl"] = pcl
    m["w2m"] = np.ascontiguousarray(f("rwkv_w2").reshape(96, 2048)[:, sl])
    m["a2m"] = np.ascontiguousarray(f("rwkv_a2").reshape(96, 2048)[:, sl])
    m["g2m"] = np.ascontiguousarray(f("rwkv_g2").reshape(256, 2048)[:, sl])
    tm = lambda v: np.ascontiguousarray(v.reshape(8, 2, 64).transpose(1, 0, 2).reshape(2, 512))
    m["lnw_t"] = tm(f("rwkv_ln_w").reshape(-1)[sl])
    m["lnb_t"] = tm(f("rwkv_ln_b").reshape(-1)[sl])
    return m


def kernel_cc(**inputs):
    ncores = NCORES
    x = np.ascontiguousarray(inputs["x"], dtype=np.float32).reshape(-1, D)
    p = np.ascontiguousarray(inputs["p"], dtype=np.float32).reshape(-1, 256)
    ntok = x.shape[0] // ncores
    assert ntok == NTOK
    nc, cx = build_mega_cc(ncores)
    consts = host_consts(SEQ)
    w_in = np.asarray(inputs["w_in"]).reshape(D, IN_PROJ)
    in_maps = []
    for c in range(ncores):
        hf = c % 2
        m = {"x": x[c * ntok:(c + 1) * ntok], "p": p[c * ntok:(c + 1) * ntok]}
        m.update(consts)
        for g in GALL:
            m[g] = np.ascontiguousarray(inputs[g], dtype=np.float32).reshape(1, D)
        for w in WALL:
            r, cdim = WSHAPE[w]
            wf = np.asarray(inputs[w]).reshape(r, cdim)
            m[w + "_sh"] = np.ascontiguousarray(wf[c * (r // ncores):(c + 1) * (r // ncores)], dtype=np.float32)
        for i in range(2):
            r0 = i * 2048 + c * 256
            m["w_in_sh%d" % i] = np.ascontiguousarray(w_in[r0:r0 + 256], dtype=np.float32)
        m.update(_core_params(inputs, hf))
        in_maps.append(m)
    res = run_bass_kernel_spmd(nc, in_maps, core_ids=list(range(ncores)))
    out = np.concatenate([r["out"] for r in res.results], axis=0)
    return out.reshape(inputs["x"].shape).astype(np.float32)


BPAR = ["lam4", "subln_g", "pcols", "pcl", "w2m", "a2m", "g2m", "lnw_t", "lnb_t"]
BPSHAPE = {"lam4": [1, 512], "subln_g": [1, 256], "pcols": [8, 1024], "pcl": [4, 128], "w2m": [96, 1024],
           "a2m": [96, 1024], "g2m": [256, 1024], "lnw_t": [2, 512], "lnb_t": [2, 512]}
WFULL = WALL + ["w_in"]
WSHAPE["w_in"] = (D, IN_PROJ)


def build_mega(ncores=NCORES):
    ntok = NTOK
    T = SEQ
    nc = bass.Bass("TRN2", target_bir_lowering=False)
    ext = lambda name, shape, dt=F32: nc.dram_tensor(name, shape, dt, kind="ExternalInput").ap()
    loc = lambda name, shape, dt=F32: nc.dram_tensor(name, shape, dt, kind="Internal").ap()
    dr = {}
    dr["xseq"] = ext("xseq", [T, D])
    dr["p"] = ext("p", [ntok, 256])
    dr["ident"] = ext("ident", [128, 128])
    for g in GALL:
        dr[g] = ext(g, [1, D])
    for w in WFULL:
        dr[w] = ext(w, list(WSHAPE[w]))
    for k, shp, dt in (("cosT", [32, T], F32), ("sinT", [32, T], F32), ("perm", [128, 128], F32),
                       ("amask", [4, 128, 512], BF16), ("rmask", [5, 128, 512], F32), ("bones", [128, 128], F32)):
        dr[k] = ext(k, shp, dt)
    drh = []
    for hfp in range(2):
        d2 = {}
        for k in BPAR:
            d2[k] = ext("%s_%d" % (k, hfp), BPSHAPE[k])
        drh.append(d2)
    out = nc.dram_tensor("out", [ntok, D], F32, kind="ExternalOutput").ap()
    x1f = loc("x1f", [T, D])
    x1, x2, x3 = loc("x1", [ntok, D]), loc("x2", [ntok, D]), loc("x3", [ntok, D])
    fscr = loc("fscr", [TT, D])
    dr["h2T_seq"] = loc("h2T_seq", [1, D, T], BF16)
    dr["zT"] = loc("zT", [NZ, T])
    dr["vat"] = loc("vat", [T, 1024], BF16)
    oT_loc = [loc("oT_loc%d" % i, [2048, T], BF16) for i in range(2)]
    oT_mine = loc("oT_mine", [2, 2048, ntok], BF16)
    for k in ("h2T_seq", "zT", "vat"):
        dr["_%s_b" % k] = Buf(k)
    oTb = [Buf("oT0"), Buf("oT1")]
    cx = Cx(nc)
    cx.want_pid = True
    with ExitStack() as es:
        R = alloc_common(nc, es, cx)
        R.pT = es.enter_context(nc.sbuf_tensor("sb_pT", [128, 2, TT], BF16))
        R.pTb = [Buf("pT0"), Buf("pT1")]
        R.gstage = [es.enter_context(nc.sbuf_tensor("sb_gst%d" % i, [128, 512], F32)) for i in range(4)]
        R.gstageb = [Buf("gst%d" % i) for i in range(4)]
        load_consts(cx, R, dr)
        xb, x1fb, x1b, x2b, x3b, ob, fb, omb = (Buf(n) for n in ("x", "x1f", "x1", "x2", "x3", "out", "f", "oT_mine"))
        for tt in range(T // TT):
            ffn_block(cx, R, dr, "ffn1", dr["xseq"], xb, x1f, x1fb, tt * TT, fscr, fb)
            norm_stage(cx, R, x1f, x1fb, tt * TT, GIDX["mix_pre_g"])
            cx.dma("sp", dr["h2T_seq"][0, :, tt * TT:(tt + 1) * TT].rearrange("(kc p) t -> p kc t", p=128), R.hT[:],
                   reads=R.hTb, pwrites=[dr["_h2T_seq_b"]])

        def fn_x(e):
            half = cx.pid % 2
            return e.dma_start(out=x1, in_=x1f[ds(half * ntok, ntok), :])
        cx.op("pool", fn_x, reads=[x1fb], writes=[x1b], dma=True)
        cx.barrier()
        for hfp in range(2):
            d2 = dict(dr)
            d2.update(drh[hfp])
            d2["w_in_mine"] = dr["w_in"]
            d2["oT_loc"] = oT_loc[hfp]
            d2["_oT_b"] = oTb[hfp]
            wc = {"q": hfp * 1024, "k": 2048 + hfp * 1024, "v": 4096 + hfp * 1024, "r": 6144 + hfp * 1024,
                  "rk": 8192 + hfp * 1024, "rv": 10240 + hfp * 1024, "wl": 12288, "al": 12384, "gl": 12480}
            proj_stage(cx, R, d2, T, WCOL=wc)
            A = attn_setup(cx, R, d2)
            for h in range(4):
                attn_head(cx, R, A, d2, h)
            cx.barrier()
            Wk = rwkv_setup(cx, R, d2)
            for ti in range(T // TT):
                rwkv_tile(cx, R, Wk, d2, ti)
            cx.barrier()

        def fn_o(e, r):
            half = cx.pid % 2
            return e.dma_start(out=oT_mine[r], in_=oT_loc[r][:, ds(half * ntok, ntok)])
        for r in range(2):
            cx.op("pool", (lambda e, r=r: fn_o(e, r)), reads=[oTb[r]], pwrites=[omb], dma=True)
        cx.barrier()
        for tt in range(ntok // TT):
            for r in range(2):
                cx.dma("sp", R.hT[:, r * 16:(r + 1) * 16, :],
                       oT_mine[r, :, tt * TT:(tt + 1) * TT].rearrange("(j p) t -> p j t", p=128),
                       reads=[omb], writes=(R.hTb if r == 0 else []), pwrites=([] if r == 0 else R.hTb))
            lin_tok_stage(cx, R, R.hT, (lambda k, ts: R.hTb[ts]), KC, dr["w_out"], D, make_f_evac(cx, R, fscr, fb),
                          rowmap=wout_rowmap)
            finalize_stage(cx, R, fscr, fb, dr["mix_post_g"], x1, x1b, tt * TT, x2, x2b, tt * TT, 1.0)
            ffn_block(cx, R, dr, "ffn2", x2, x2b, x3, x3b, tt * TT, fscr, fb)
            ple_block(cx, R, dr, x3, x3b, out, ob, tt * TT, fscr, fb)
        cx.final_wait("sp", [ob])
        cx.emit(es)
    return nc, cx


def kernel(**inputs):
    ncores = NCORES
    x = np.ascontiguousarray(inputs["x"], dtype=np.float32).reshape(-1, SEQ, D)
    p = np.ascontiguousarray(inputs["p"], dtype=np.float32).reshape(-1, 256)
    nc, cx = build_mega(ncores)
    consts = host_consts(SEQ)
    shared = dict(consts)
    for g in GALL:
        shared[g] = np.ascontiguousarray(inputs[g], dtype=np.float32).reshape(1, D)
    for w in WFULL:
        shared[w] = np.ascontiguousarray(np.asarray(inputs[w]).reshape(WSHAPE[w]), dtype=np.float32)
    for hfp in range(2):
        for k, v in _core_params(inputs, hfp).items():
            shared["%s_%d" % (k, hfp)] = v
    in_maps = []
    for c in range(ncores):
        m = dict(shared)
        m["xseq"] = x[c // 2]
        m["p"] = p[c * NTOK:(c + 1) * NTOK]
        in_maps.append(m)
    res = run_bass_kernel_spmd(nc, in_maps, core_ids=list(range(ncores)))
    out = np.concatenate([r["out"] for r in res.results], axis=0)
    return out.reshape(inputs["x"].shape).astype(np.float32)
```

```python
import math
import os
DBG = os.environ.get('KDBG', '')
from contextlib import ExitStack

import numpy as np
import concourse.bass as bass
import concourse.mybir as mybir
from concourse.bass_utils import run_bass_kernel_spmd

F32 = mybir.dt.float32
BF16 = mybir.dt.bfloat16
AF = mybir.ActivationFunctionType
ALU = mybir.AluOpType
AX = mybir.AxisListType

D = 4096
DFF = 11008
NFC = DFF // 128
KC = D // 128
TT = 512
RMS_EPS = 1e-6
NCORES = 8


class Buf:
    __slots__ = ("name", "w", "r")

    def __init__(self, name=""):
        self.name = name
        self.w = {}
        self.r = {}


class _Eng:
    def __init__(self, name):
        self.name = name
        self.count = 0
        self.prog = []
        self.waited = {}


class Cx:
    DMA_K = 8

    def __init__(self, nc):
        self.nc = nc
        self.eng = {n: _Eng(n) for n in ("pe", "dve", "act", "pool", "sp")}
        self.dman = {"sp": 0, "pool": 0, "act": 0}
        self.semkeys = set()
        self.nops = 0

    def op(self, en, fn, reads=(), writes=(), sig=True, dma=False, pwrites=()):
        e = self.eng[en]
        waits = {}

        def need(tok):
            k, v = tok
            if k == "pe" and en == "pe":
                return
            if waits.get(k, 0) < v:
                waits[k] = v

        for b in reads:
            for tok in b.w.items():
                need(tok)
        for b in writes:
            for tok in b.w.items():
                need(tok)
            for tok in b.r.items():
                need(tok)
        for b in pwrites:
            for tok in b.r.items():
                need(tok)
        if dma:
            i = self.dman[en]
            self.dman[en] = i + 1
            k = "dma_%s_%d" % (en, i % self.DMA_K)
            v = 16 * (i // self.DMA_K + 1)
            if i >= self.DMA_K:
                need((k, v - 16))
            tok = (k, v)
            inc = (k, 16)
        elif sig:
            e.count += 1
            tok = (en, e.count)
            inc = (en, 1)
        else:
            tok = (en, e.count + 1)
            inc = None
        wl = []
        for k, v in waits.items():
            if e.waited.get(k, 0) >= v:
                continue
            e.waited[k] = v
            wl.append((k, v))
            self.semkeys.add(k)
        if inc is not None:
            self.semkeys.add(inc[0])
        e.prog.append((wl, fn, inc))
        self.nops += 1
        for b in writes:
            b.w = {tok[0]: tok[1]}
            b.r = {}
        for b in pwrites:
            if b.w.get(tok[0], 0) < tok[1]:
                b.w[tok[0]] = tok[1]
        for b in reads:
            if b in writes:
                continue
            if b.r.get(tok[0], 0) < tok[1]:
                b.r[tok[0]] = tok[1]
        return tok

    def pe(self, fn, reads=(), writes=(), sig=True, pwrites=()):
        return self.op("pe", fn, reads, writes, sig, pwrites=pwrites)

    def dve(self, fn, reads=(), writes=(), pwrites=()):
        return self.op("dve", fn, reads, writes, pwrites=pwrites)

    def act(self, fn, reads=(), writes=(), pwrites=()):
        return self.op("act", fn, reads, writes, pwrites=pwrites)

    def pool(self, fn, reads=(), writes=(), pwrites=()):
        return self.op("pool", fn, reads, writes, pwrites=pwrites)

    def dma(self, q, out, in_, reads=(), writes=(), pwrites=(), **kw):
        return self.op(q, lambda e: e.dma_start(out=out, in_=in_, **kw), reads, writes, dma=True, pwrites=pwrites)

    def cc(self, fn, reads=(), writes=()):
        e = self.eng["pool"]
        self.ncc = getattr(self, "ncc", 0) + 1
        k = "cc_%d" % self.ncc
        waits = {}
        for b in reads:
            for kk, v in b.w.items():
                waits[kk] = max(waits.get(kk, 0), v)
        for b in writes:
            for kk, v in list(b.w.items()) + list(b.r.items()):
                waits[kk] = max(waits.get(kk, 0), v)
        wl = []
        for kk, v in waits.items():
            if e.waited.get(kk, 0) >= v:
                continue
            e.waited[kk] = v
            wl.append((kk, v))
            self.semkeys.add(kk)
        self.semkeys.add(k)
        e.prog.append((wl, fn, (k, 1)))
        for b in writes:
            b.w = {k: 1}
            b.r = {}
        for b in reads:
            b.r[k] = 1

    def barrier(self):
        toks = {}
        for n, e in self.eng.items():
            if n != "sp" and e.count > 0:
                toks[n] = e.count
        for q, n in self.dman.items():
            for i in range(min(n, self.DMA_K)):
                last = ((n - 1 - i) // self.DMA_K) * self.DMA_K + i
                toks["dma_%s_%d" % (q, last % self.DMA_K)] = 16 * (last // self.DMA_K + 1)
        for i in range(getattr(self, "ncc", 0)):
            toks["cc_%d" % (i + 1)] = 1
        for n, e in self.eng.items():
            wl = []
            for k, v in toks.items():
                if e.waited.get(k, 0) < v:
                    e.waited[k] = v
                    wl.append((k, v))
                    self.semkeys.add(k)
            e.prog.append((wl, None, None))

    def final_wait(self, en, bufs):
        e = self.eng[en]
        wl = []
        for b in bufs:
            for k, v in b.w.items():
                if e.waited.get(k, 0) < v:
                    e.waited[k] = v
                    wl.append((k, v))
        e.prog.append((wl, None, None))

    def emit(self, es):
        nc = self.nc
        sems = {k: es.enter_context(nc.semaphore("s_" + k)) for k in sorted(self.semkeys)}
        block = es.enter_context(nc.Block())

        def replay(en):
            def f(engine):
                if en == "pool" and getattr(self, "want_pid", False):
                    self.pid = engine.partition_id()
                for wl, fn, inc in self.eng[en].prog:
                    for k, v in wl:
                        engine.wait_ge(sems[k], v)
                    if fn is None:
                        continue
                    ins = fn(engine)
                    if inc is not None:
                        ins.then_inc(sems[inc[0]], inc[1])
            return f

        block.tensor(replay("pe"))
        block.vector(replay("dve"))
        block.scalar(replay("act"))
        block.gpsimd(replay("pool"))
        block.sync(replay("sp"))


class Ring:
    def __init__(self, tiles, name):
        self.tiles = tiles
        self.bufs = [Buf("%s%d" % (name, i)) for i in range(len(tiles))]
        self.i = 0

    def next(self):
        j = self.i % len(self.tiles)
        self.i += 1
        return self.tiles[j], self.bufs[j]


class Res:
    pass


def alloc_common(nc, es, cx):
    R = Res()
    R.nc = nc
    sb = lambda name, shape, dt: es.enter_context(nc.sbuf_tensor("sb_" + name, shape, dt))
    R.ident = sb("ident", [128, 128], F32)
    R.identb = Buf("ident")
    R.hT = sb("hT", [128, KC, TT], BF16)
    R.hTb = [Buf("hT%d" % i) for i in range(4)]
    R.big = sb("big", [128, NFC * TT // 2], F32)
    R.aT = R.big[:].bitcast(BF16).rearrange("p (k c) -> p k c", c=TT)
    R.aTb = [Buf("aT%d" % i) for i in range(NFC)]
    R.wring = Ring([sb("wr%d" % i, [128, 4096], BF16) for i in range(5)], "wr")
    R.qring = Ring([sb("qr%d" % i, [128, 512], F32) for i in range(10)], "qr")
    R.stage = Ring([sb("stg%d" % i, [128, 512], F32) for i in range(2)], "stg")
    R.tmp = Ring([sb("tmp%d" % i, [128, 512], F32) for i in range(2)], "tmp")
    R.junk = sb("junk", [128, 512], BF16)
    R.junkb = Buf("junk")
    R.gcol = sb("gcol", [128, 8, KC], F32)
    R.gcolb = Buf("gcol")
    R.small = sb("small", [128, 64], F32)
    R.ssq = sb("ssq", [128, 4, 8], F32)
    R.ssqb = [Buf("ssq%d" % i) for i in range(4)]
    R.st = Ring([sb("st%d" % i, [128, 8], F32) for i in range(4)], "st")
    R.banks = [es.enter_context(nc.psum_tensor("pb%d" % i, [128, 512], F32)) for i in range(8)]
    R.bankb = [Buf("bank%d" % i) for i in range(8)]
    return R


GIDX = {"ffn1_pre_g": 0, "mix_pre_g": 1, "ffn2_pre_g": 2, "ple_pre_g": 3}


def load_consts(cx, R, dr):
    cx.dma("sp", R.ident[:], dr["ident"], writes=[R.identb])
    for name, gi in GIDX.items():
        if name in dr:
            t, tb = R.qring.next()
            cx.dma("sp", t[0:KC, 0:128], dr[name].rearrange("o (kc p) -> (o kc) p", p=128), writes=[tb])
            cx.pe(lambda e, t=t: e.transpose(out=R.banks[0][:, 0:KC], in_=t[0:KC, 0:128], identity=R.ident[0:KC, 0:KC]),
                  reads=[tb, R.identb], writes=[R.bankb[0]])
            cx.dve(lambda e, gi=gi: e.tensor_copy(out=R.gcol[:, gi, :], in_=R.banks[0][:, 0:KC]),
                   reads=[R.bankb[0]], pwrites=[R.gcolb])


def rstd_from_ss(cx, R, ss_ap, ssb, n, coef=1.0):
    t, tb = R.st.next()
    cx.act(lambda e: e.activation(out=t[:, 0:1], in_=ss_ap, func=AF.Sqrt, bias=float(RMS_EPS), scale=1.0 / n),
           reads=[ssb], writes=[tb])
    cx.dve(lambda e: e.reciprocal(out=t[:, 1:2], in_=t[:, 0:1]), reads=[tb], writes=[tb])
    if coef != 1.0:
        cx.dve(lambda e: e.tensor_scalar(out=t[:, 1:2], in0=t[:, 1:2], scalar1=float(coef), scalar2=None,
                                         op0=ALU.mult), reads=[tb], writes=[tb])
    return t[:, 1:2], tb


def norm_stage(cx, R, src, srcb, row0, gi):
    for ts in range(4):
        r0 = row0 + ts * 128
        ss, ssb = R.st.next()
        for q in range(8):
            xt, xb = R.qring.next()
            cx.dma("sp", xt[:], src[r0:r0 + 128, q * 512:(q + 1) * 512], reads=[srcb], writes=[xb])
            cx.act(lambda e, xt=xt, q=q, ss=ss: e.activation(out=R.junk[:], in_=xt[:], func=AF.Square,
                                                          accum_out=ss[:, q:q + 1]),
                   reads=[xb], writes=[R.junkb], pwrites=[ssb])
        rs, rsb = R.st.next()
        cx.dve(lambda e, ss=ss, rs=rs: e.tensor_reduce(out=rs[:, 4:5], in_=ss[:, 0:8], axis=AX.X, op=ALU.add),
               reads=[ssb], writes=[rsb])
        rstd, rb = rstd_from_ss(cx, R, rs[:, 4:5], rsb, D)
        for q in range(8):
            xt, xb = R.qring.next()
            cx.dma("sp", xt[:], src[r0:r0 + 128, q * 512:(q + 1) * 512], reads=[srcb], writes=[xb])
            if True:
                cx.dve(lambda e, xt=xt, rstd=rstd: e.tensor_scalar(out=xt[:], in0=xt[:], scalar1=rstd, scalar2=None,
                                                                op0=ALU.mult), reads=[xb, rb], writes=[xb])
            else:
                cx.act(lambda e, xt=xt, rstd=rstd: e.activation(out=xt[:], in_=xt[:], func=AF.Copy, scale=rstd),
                       reads=[xb, rb], writes=[xb])
            bi = (ts * 8 + q) % 2
            bank, bb = R.banks[bi], R.bankb[bi]
            for j in range(4):
                cx.pe(lambda e, bank=bank, j=j, xt=xt: e.transpose(
                    out=bank[:, j * 128:(j + 1) * 128], in_=xt[:, j * 128:(j + 1) * 128], identity=R.ident[:]),
                    reads=[xb, R.identb], writes=[bb], sig=(j == 3))
            for j in range(4):
                kc = q * 4 + j
                if True:
                    cx.dve(lambda e, bank=bank, j=j, kc=kc, ts=ts: e.tensor_scalar(
                        out=R.hT[:, kc, ts * 128:(ts + 1) * 128], in0=bank[:, j * 128:(j + 1) * 128],
                        scalar1=R.gcol[:, gi, kc:kc + 1], scalar2=None, op0=ALU.mult),
                        reads=[bb, R.gcolb], pwrites=[R.hTb[ts]])
                else:
                    cx.act(lambda e, bank=bank, j=j, kc=kc, ts=ts: e.activation(
                        out=R.hT[:, kc, ts * 128:(ts + 1) * 128], in_=bank[:, j * 128:(j + 1) * 128],
                        func=AF.Copy, scale=R.gcol[:, gi, kc:kc + 1]),
                        reads=[bb, R.gcolb], pwrites=[R.hTb[ts]])


def gateup_stage(cx, R, Wg, Wu, wbuf=None, wbuf2=None):
    Wgv = Wg.rearrange("(kc p) c -> p kc c", p=128)
    Wuv = Wu.rearrange("(kc p) c -> p kc c", p=128)
    nblk = NFC // 2
    for blk in range(nblk):
        c0 = blk * 256
        gb = [(blk % 2) * 4 + 0, (blk % 2) * 4 + 1]
        ub = [(blk % 2) * 4 + 2, (blk % 2) * 4 + 3]
        for half in range(2):
            pieces = []
            for Wv in (Wgv, Wuv):
                wt, wb = R.wring.next()
                wv = wt[:].rearrange("p (k c) -> p k c", c=256)
                cx.dma("pool", wv, Wv[:, half * 16:(half + 1) * 16, c0:c0 + 256],
                       reads=[b_ for b_ in (wbuf, wbuf2) if b_ is not None], writes=[wb])
                pieces.append((wv, wb))
            for (wv, wb), banks in zip(pieces, (gb, ub)):
                for j in range(2):
                    bank, bb = R.banks[banks[j]], R.bankb[banks[j]]
                    for kcl in range(16):
                        kc = half * 16 + kcl
                        cx.pe(lambda e, bank=bank, wv=wv, kcl=kcl, j=j, kc=kc: e.matmul(
                            bank[:], wv[:, kcl, j * 128:(j + 1) * 128], R.hT[:, kc, :],
                            start=(kc == 0), stop=(kc == KC - 1)),
                            reads=[wb] + R.hTb, writes=[bb], sig=(kcl == 15))
        for j in range(2):
            fc = blk * 2 + j
            t, tb = R.tmp.next()
            gbank, ubank = R.banks[gb[j]], R.banks[ub[j]]
            cx.act(lambda e, t=t, gbank=gbank: e.activation(out=t[:], in_=gbank[:], func=AF.Silu),
                   reads=[R.bankb[gb[j]]], writes=[tb])
            cx.dve(lambda e, t=t, fc=fc, ubank=ubank: e.tensor_tensor(out=R.aT[:, fc, :], in0=t[:], in1=ubank[:],
                                                                   op=ALU.mult),
                   reads=[tb, R.bankb[ub[j]]], writes=[R.aTb[fc]])


def lin_tok_stage(cx, R, actT, actb, nK, W, ncols, evac, cbs=None, par=0, wbuf=None, rowmap=None):
    Wv = W.rearrange("(k p) c -> p k c", p=128)
    G = 8
    ncb = ncols // 512
    for cb in (range(ncb) if cbs is None else cbs):
        banks = [((cb + par) % 2) * 4 + ts for ts in range(4)]
        for k0 in range(0, nK, G):
            g = min(G, nK - k0)
            wt, wb = R.wring.next()
            wv = wt[:].rearrange("p (k c) -> p k c", c=512)
            rk0 = k0 if rowmap is None else rowmap(k0)
            cx.dma("pool", wv[:, 0:g, :], Wv[:, rk0:rk0 + g, cb * 512:(cb + 1) * 512],
                   reads=([wbuf] if wbuf is not None else []), writes=[wb])
            for kl in range(g):
                k = k0 + kl
                for ts in range(4):
                    bank = R.banks[banks[ts]]
                    cx.pe(lambda e, bank=bank, k=k, kl=kl, ts=ts, wv=wv: e.matmul(
                        bank[:], actT[:, k, ts * 128:(ts + 1) * 128], wv[:, kl, :],
                        start=(k == 0), stop=(k == nK - 1)),
                        reads=[wb, (actb(k, ts) if callable(actb) else actb[k])], writes=[R.bankb[banks[ts]]],
                        sig=(k == nK - 1 or kl == g - 1))
        for ts in range(4):
            evac(cb, ts, R.banks[banks[ts]], R.bankb[banks[ts]])


def make_f_evac(cx, R, fscr, fb):
    def evac(cb, ts, bank, bb):
        s, sb_ = R.stage.next()
        cx.act(lambda e: e.activation(out=s[:], in_=bank[:], func=AF.Copy), reads=[bb], writes=[sb_])
        cx.act(lambda e: e.activation(out=R.junk[:], in_=s[:], func=AF.Square,
                                      accum_out=R.ssq[:, ts, cb:cb + 1]),
               reads=[sb_], writes=[R.junkb], pwrites=[R.ssqb[ts]])
        cx.dma("sp", fscr[ts * 128:(ts + 1) * 128, cb * 512:(cb + 1) * 512], s[:], reads=[sb_], pwrites=[fb])
    return evac


def finalize_stage(cx, R, fscr, fb, gpost, src, srcb, srow0, dst, dstb, drow0, coef, ncb=8):
    steps = [(ts, q) for ts in range(4) for q in range(8)]
    rcs = {}
    loaded = {}

    def load(i):
        ts, q = steps[i]
        cs = slice(q * 512, (q + 1) * 512)
        ft, fbq = R.qring.next()
        cx.dma("sp", ft[:], fscr[ts * 128:(ts + 1) * 128, cs], reads=[fb], writes=[fbq])
        xt, xb = R.qring.next()
        cx.dma("sp", xt[:], src[srow0 + ts * 128:srow0 + (ts + 1) * 128, cs], reads=[srcb], writes=[xb])
        gt, gb_ = R.qring.next()
        cx.dma("sp", gt[:], gpost[0:1, cs].partition_broadcast(128), writes=[gb_])
        loaded[i] = (ft, fbq, xt, xb, gt, gb_)

    def compute(i):
        ts, q = steps[i]
        cs = slice(q * 512, (q + 1) * 512)
        if ts not in rcs:
            ss, ssb = R.st.next()
            cx.dve(lambda e, ss=ss, ts=ts: e.tensor_reduce(out=ss[:, 0:1], in_=R.ssq[:, ts, 0:ncb], axis=AX.X,
                                                        op=ALU.add), reads=[R.ssqb[ts]], writes=[ssb])
            rcs[ts] = rstd_from_ss(cx, R, ss[:, 0:1], ssb, D, coef)
        rc, rb = rcs[ts]
        ft, fbq, xt, xb, gt, gb_ = loaded.pop(i)
        cx.dve(lambda e: e.scalar_tensor_tensor(out=ft[:], in0=ft[:], scalar=rc, in1=gt[:], op0=ALU.mult,
                                                op1=ALU.mult), reads=[fbq, gb_, rb], writes=[fbq])
        cx.pool(lambda e: e.tensor_tensor(out=ft[:], in0=ft[:], in1=xt[:], op=ALU.add),
                reads=[fbq, xb], writes=[fbq])
        cx.dma("sp", dst[drow0 + ts * 128:drow0 + (ts + 1) * 128, cs], ft[:], reads=[fbq], pwrites=[dstb])

    load(0)
    load(1)
    for i in range(len(steps)):
        if i + 2 < len(steps):
            load(i + 2)
        compute(i)


def ffn_block(cx, R, dr, pfx, src, srcb, dst, dstb, row0, fscr, fb, upto=9, wbuf=None, wbufs=None):
    if upto >= 1:
        norm_stage(cx, R, src, srcb, row0, GIDX[pfx + "_pre_g"])
    if upto >= 2:
        gateup_stage(cx, R, dr[pfx + "_w_gate"], dr[pfx + "_w_up"], wbuf,
                     wbuf2=(wbufs or {}).get(pfx + "_w_gate"))
    if upto >= 3:
        lin_tok_stage(cx, R, R.aT, R.aTb, NFC, dr[pfx + "_w_down"], D, make_f_evac(cx, R, fscr, fb), wbuf=wbuf)
    if upto >= 4:
        finalize_stage(cx, R, fscr, fb, dr[pfx + "_post_g"], src, srcb, row0, dst, dstb, row0, 0.5)


def build_test_ffn(ntok, upto=9):
    nc = bass.Bass("TRN2", target_bir_lowering=False)
    dr = {}
    dr["x"] = nc.dram_tensor("x", [ntok, D], F32, kind="ExternalInput").ap()
    dr["ident"] = nc.dram_tensor("ident", [128, 128], F32, kind="ExternalInput").ap()
    dr["ffn1_pre_g"] = nc.dram_tensor("ffn1_pre_g", [1, D], F32, kind="ExternalInput").ap()
    dr["ffn1_post_g"] = nc.dram_tensor("ffn1_post_g", [1, D], F32, kind="ExternalInput").ap()
    if upto >= 2:
        dr["ffn1_w_gate"] = nc.dram_tensor("ffn1_w_gate", [D, DFF], F32, kind="ExternalInput").ap()
        dr["ffn1_w_up"] = nc.dram_tensor("ffn1_w_up", [D, DFF], F32, kind="ExternalInput").ap()
        dr["ffn1_w_down"] = nc.dram_tensor("ffn1_w_down", [DFF, D], F32, kind="ExternalInput").ap()
    y = nc.dram_tensor("y", [ntok, D], F32, kind="ExternalOutput").ap()
    fscr = nc.dram_tensor("fscr", [TT, D], F32, kind="Internal").ap()
    cx = Cx(nc)
    with ExitStack() as es:
        R = alloc_common(nc, es, cx)
        load_consts(cx, R, dr)
        xb, yb, fb = Buf("x"), Buf("y"), Buf("f")
        for tt in range(ntok // TT):
            ffn_block(cx, R, dr, "ffn1", dr["x"], xb, y, yb, tt * TT, fscr, fb, upto)
        if upto < 4:
            t, tb = R.stage.next()
            cx.dve(lambda e: e.tensor_copy(out=t[:], in_=(R.hT[:, 0, :] if upto < 2 else R.aT[:, 0, :])),
                   reads=R.hTb + R.aTb + [R.gcolb], writes=[tb])
            cx.dma("sp", y[0:128, 0:512], t[:], reads=[tb], pwrites=[yb])
        cx.final_wait("sp", [yb, fb])
        for en in ("pe", "dve", "act", "pool"):
            cx.final_wait("sp", [])
        cx.emit(es)
    return nc, cx


def ple_block(cx, R, dr, src, srcb, dst, dstb, row0, fscr, fb, wbuf=None):
    norm_stage(cx, R, src, srcb, row0, GIDX["ple_pre_g"])
    for ts in range(4):
        pt, pb = R.qring.next()
        cx.dma("sp", pt[:, 0:256], dr["p"][row0 + ts * 128:row0 + (ts + 1) * 128, :], writes=[pb])
        bank, bb = R.banks[ts % 2], R.bankb[ts % 2]
        for j in range(2):
            cx.pe(lambda e, bank=bank, j=j, pt=pt: e.transpose(out=bank[:, j * 128:(j + 1) * 128],
                                                             in_=pt[:, j * 128:(j + 1) * 128], identity=R.ident[:]),
                  reads=[pb, R.identb], writes=[bb], sig=(j == 1))
        cx.dve(lambda e, bank=bank, ts=ts: e.tensor_copy(
            out=R.pT[:, :, ts * 128:(ts + 1) * 128], in_=bank[:, 0:256].rearrange("p (j t) -> p j t", j=2)),
            reads=[bb], pwrites=[R.pTb[0], R.pTb[1]])
    gst = {}

    def gate_evac(cb, ts, bank, bb):
        cx.act(lambda e: e.activation(out=R.gstage[ts][:], in_=bank[:], func=AF.Sigmoid),
               reads=[bb], writes=[R.gstageb[ts]])

    def pp_evac(cb, ts, bank, bb):
        s_, sb_ = R.stage.next()
        cx.dve(lambda e: e.tensor_tensor(out=s_[:], in0=R.gstage[ts][:], in1=bank[:], op=ALU.mult),
               reads=[bb, R.gstageb[ts]], writes=[sb_])
        cx.act(lambda e: e.activation(out=R.junk[:], in_=s_[:], func=AF.Square, accum_out=R.ssq[:, ts, cb:cb + 1]),
               reads=[sb_], writes=[R.junkb], pwrites=[R.ssqb[ts]])
        cx.dma("sp", fscr[ts * 128:(ts + 1) * 128, cb * 512:(cb + 1) * 512], s_[:], reads=[sb_], pwrites=[fb])

    for cb in range(8):
        lin_tok_stage(cx, R, R.hT, (lambda k, ts: R.hTb[ts]), KC, dr["ple_w_gate"], D, gate_evac, cbs=[cb], par=0, wbuf=wbuf)
        lin_tok_stage(cx, R, R.pT, R.pTb, 2, dr["ple_w_proj"], D, pp_evac, cbs=[cb], par=1, wbuf=wbuf)
    finalize_stage(cx, R, fscr, fb, dr["ple_post_g"], src, srcb, row0, dst, dstb, row0, 1.0)


WNAMES = ["ffn1_w_gate", "ffn1_w_up", "ffn1_w_down", "ffn2_w_gate", "ffn2_w_up", "ffn2_w_down",
          "ple_w_gate", "ple_w_proj"]
WSHAPE = {"ffn1_w_gate": (D, DFF), "ffn1_w_up": (D, DFF), "ffn1_w_down": (DFF, D),
          "ffn2_w_gate": (D, DFF), "ffn2_w_up": (D, DFF), "ffn2_w_down": (DFF, D),
          "ple_w_gate": (D, D), "ple_w_proj": (256, D)}
GNAMES = ["ffn1_pre_g", "ffn1_post_g", "mix_pre_g", "ffn2_pre_g", "ffn2_post_g", "ple_pre_g", "ple_post_g"]
NTOK = 2048


def build_full(ntok=NTOK, ncores=NCORES, ag=True):
    nc = bass.Bass("TRN2", target_bir_lowering=False)
    dr = {}
    dr["x"] = nc.dram_tensor("x", [ntok, D], F32, kind="ExternalInput").ap()
    dr["p"] = nc.dram_tensor("p", [ntok, 256], F32, kind="ExternalInput").ap()
    dr["ident"] = nc.dram_tensor("ident", [128, 128], F32, kind="ExternalInput").ap()
    for g in GNAMES:
        dr[g] = nc.dram_tensor(g, [1, D], F32, kind="ExternalInput").ap()
    cx = Cx(nc)
    wbufs = {}
    shards = {}
    for w in WNAMES:
        r, c = WSHAPE[w]
        if ag:
            shards[w] = nc.dram_tensor(w + "_sh", [r // ncores, c], F32, kind="ExternalInput").ap()
        else:
            dr[w] = nc.dram_tensor(w, [r, c], F32, kind="ExternalInput").ap()
            wbufs[w] = None
    out = nc.dram_tensor("out", [ntok, D], F32, kind="ExternalOutput").ap()
    x1 = nc.dram_tensor("x1", [ntok, D], F32, kind="Internal").ap()
    x3 = nc.dram_tensor("x3", [ntok, D], F32, kind="Internal").ap()
    fscr = nc.dram_tensor("fscr", [TT, D], F32, kind="Internal").ap()
    with ExitStack() as es:
        R = alloc_common(nc, es, cx)
        R.pT = es.enter_context(nc.sbuf_tensor("sb_pT", [128, 2, TT], BF16))
        R.pTb = [Buf("pT0"), Buf("pT1")]
        R.gstage = [es.enter_context(nc.sbuf_tensor("sb_gst%d" % i, [128, 512], F32)) for i in range(4)]
        R.gstageb = [Buf("gst%d" % i) for i in range(4)]
        load_consts(cx, R, dr)
        if ag:
            shared = nc.dram_tensor("wshared", [D * DFF], F32, kind="Internal", addr_space="Shared").ap()
            shb = Buf("wshared")
        for w in (WNAMES if ag else []):
            r, c = WSHAPE[w]
            bounce = nc.dram_tensor(w + "_bn", [r // ncores, c], F32, kind="Internal").ap()
            full = nc.dram_tensor(w, [r, c], F32, kind="Internal").ap()
            shv = shared[0:r * c].rearrange("(r c) -> r c", c=c)
            bb, wb = Buf(w + "_bn"), Buf(w)
            cx.dma("pool", bounce, shards[w], writes=[bb])
            cx.cc(lambda e, bounce=bounce, shv=shv: e.collective_compute(
                "AllGather", ALU.bypass, replica_groups=[list(range(ncores))], ins=[bounce], outs=[shv]),
                reads=[bb], writes=[shb])
            cx.dma("sp", full, shv, reads=[shb], writes=[wb])
            dr[w] = full
            wbufs[w] = wb
        xb, x1b, x3b, ob, fb = Buf("x"), Buf("x1"), Buf("x3"), Buf("out"), Buf("f")
        ntile = ntok // TT
        for tt in range(ntile):
            ffn_block(cx, R, dr, "ffn1", dr["x"], xb, x1, x1b, tt * TT, fscr, fb, wbuf=wbufs["ffn1_w_down"], wbufs=wbufs)
        for tt in range(ntile):
            ffn_block(cx, R, dr, "ffn2", x1, x1b, x3, x3b, tt * TT, fscr, fb, wbuf=wbufs["ffn2_w_down"], wbufs=wbufs)
        for tt in range(ntile):
            ple_block(cx, R, dr, x3, x3b, out, ob, tt * TT, fscr, fb, wbuf=wbufs["ple_w_proj"])
        cx.final_wait("sp", [ob])
        cx.emit(es)
    return nc, cx


NQK = 1024
ZROW = {"q": 0, "k": 1024, "r": 2048, "rk": 3072, "rv": 4096, "wl": 5120, "al": 5216, "gl": 5312}
NZ = 5568
WCOL = {"q": 0, "k": 1024, "v": 2048, "r": 3072, "rk": 4096, "rv": 5120, "wl": 6144, "al": 6240, "gl": 6336}
NWC = 6592
ATT_SCALE = 128 ** -0.5
LAMBDA_INIT = 0.8 - 0.6 * math.exp(-0.3 * 0)


class Carver:
    def __init__(self, big, limit=22016):
        self.big, self.o, self.limit = big, 0, limit

    def take(self, nwords, dt=F32):
        a = self.o
        self.o += nwords
        assert self.o <= self.limit, self.o
        v = self.big[:, a:a + nwords]
        return v.bitcast(BF16) if dt == BF16 else v


def proj_stage(cx, R, dr, T, WCOL=WCOL):
    Wm = dr["w_in_mine"]
    Wv = Wm.rearrange("(kc p) c -> p kc c", p=128)
    TL = dr["h2T_seq"].shape[2]
    hseq = dr["h2T_seq"]
    blocks = []
    for g in ("q", "k", "r", "rk", "rv"):
        for b in range(4):
            blocks.append((WCOL[g] + b * 256, ZROW[g] + b * 256, 256, 128))
    blocks.append((WCOL["wl"], ZROW["wl"], 192, 96))
    blocks.append((WCOL["gl"], ZROW["gl"], 256, 128))
    for ti in range(T // TT):
        rk, c0t = (ti * TT) // TL, (ti * TT) % TL
        cx.dma("sp", R.hT[:], hseq[rk, :, c0t:c0t + TT].rearrange("(kc p) t -> p kc t", p=128),
               reads=[dr["_h2T_seq_b"]], writes=R.hTb)
        for bi, (wc0, zr0, ncol, cw) in enumerate(blocks):
            banks = [(bi % 2) * 2, (bi % 2) * 2 + 1]
            for half in range(2):
                wt, wb = R.wring.next()
                wv = wt[:].rearrange("p (k c) -> p k c", c=256)
                cx.dma("pool", wv[:, :, 0:ncol], Wv[:, half * 16:(half + 1) * 16, wc0:wc0 + ncol], writes=[wb])
                for j in range(2):
                    bank, bb = R.banks[banks[j]], R.bankb[banks[j]]
                    for kcl in range(16):
                        kc = half * 16 + kcl
                        cx.pe(lambda e, bank=bank, wv=wv, kcl=kcl, j=j, kc=kc, cw=cw: e.matmul(
                            bank[0:cw, :], wv[:, kcl, j * cw:(j + 1) * cw], R.hT[:, kc, :],
                            start=(kc == 0), stop=(kc == KC - 1)),
                            reads=[wb] + R.hTb, writes=[bb], sig=(kcl == 15))
            for j in range(2):
                bank, bb = R.banks[banks[j]], R.bankb[banks[j]]
                st, stb = R.stage.next()
                if j == 0:
                    cx.act(lambda e, st=st, bank=bank, cw=cw: e.activation(out=st[0:cw, :], in_=bank[0:cw, :], func=AF.Copy),
                           reads=[bb], writes=[stb])
                else:
                    cx.dve(lambda e, st=st, bank=bank, cw=cw: e.tensor_copy(out=st[0:cw, :], in_=bank[0:cw, :]),
                           reads=[bb], writes=[stb])
                cx.dma("sp", dr["zT"][zr0 + j * cw:zr0 + (j + 1) * cw, ti * TT:(ti + 1) * TT], st[0:cw, :],
                       reads=[stb], pwrites=[dr["_zT_b"]])

        def v_evac(cb, ts, bank, bb, ti=ti):
            st, stb = R.tmp.next()
            stv = st[:].bitcast(BF16)[:, 0:512]
            cx.act(lambda e: e.activation(out=stv, in_=bank[:], func=AF.Copy), reads=[bb], writes=[stb])
            cx.dma("sp", dr["vat"][ti * TT + ts * 128:ti * TT + (ts + 1) * 128, cb * 512:(cb + 1) * 512], stv,
                   reads=[stb], pwrites=[dr["_vat_b"]])
        lin_tok_stage(cx, R, R.hT, (lambda k, ts: R.hTb[ts]), KC, Wm[:, WCOL["v"]:WCOL["v"] + 1024], 1024, v_evac)


def attn_setup(cx, R, dr):
    A = Res()
    cv = Carver(R.big)
    T = dr["zT"].shape[1]
    A.T = T
    A.qr = cv.take(T, BF16).rearrange("p (c t) -> p c t", c=2)
    A.kr = cv.take(T, BF16).rearrange("p (c t) -> p c t", c=2)
    A.V = cv.take(T, BF16).rearrange("p (t v) -> p t v", v=256)
    A.masks = cv.take(1024, BF16).rearrange("p (m q) -> p m q", m=4)
    A.ones_bf = cv.take(64, BF16)
    A.ones_f = cv.take(128)
    A.perm = cv.take(128)
    A.P = Ring([cv.take(256, BF16) for _ in range(3)], "P")
    A.od = [cv.take(512) for _ in range(2)]
    A.odb = [Buf("od0"), Buf("od1")]
    A.sq = [cv.take(512) for _ in range(2)]
    A.sqb = [Buf("sq0"), Buf("sq1")]
    A.fin = Ring([cv.take(256, BF16) for _ in range(2)], "fin")
    A.qrb, A.krb, A.Vb, A.cb = Buf("qr"), Buf("kr"), Buf("V"), Buf("aconst")
    cx.dma("sp", A.masks, dr["amask"].rearrange("m p q -> p m q"), writes=[A.cb])
    cx.dma("sp", A.perm, dr["perm"], pwrites=[A.cb])
    cx.pool(lambda e: e.memset(A.ones_bf, 1.0), pwrites=[A.cb])
    cx.pool(lambda e: e.memset(A.ones_f, 1.0), pwrites=[A.cb])
    sm = R.small
    A.smb = Buf("small")
    l4, l4b = R.qring.next()
    cx.dma("sp", l4[:], dr["lam4"][0:1, :].partition_broadcast(128), writes=[l4b])
    cx.dve(lambda e: e.tensor_tensor(out=l4[:, 0:128], in0=l4[:, 0:128], in1=l4[:, 128:256], op=ALU.mult),
           reads=[l4b], writes=[l4b])
    cx.dve(lambda e: e.tensor_tensor(out=l4[:, 256:384], in0=l4[:, 256:384], in1=l4[:, 384:512], op=ALU.mult),
           reads=[l4b], writes=[l4b])
    cx.dve(lambda e: e.tensor_reduce(out=sm[:, 8:9], in_=l4[:, 0:128], axis=AX.X, op=ALU.add), reads=[l4b], writes=[A.smb])
    cx.dve(lambda e: e.tensor_reduce(out=sm[:, 9:10], in_=l4[:, 256:384], axis=AX.X, op=ALU.add), reads=[l4b], writes=[A.smb])
    cx.act(lambda e: e.activation(out=sm[:, 10:12], in_=sm[:, 8:10], func=AF.Exp), reads=[A.smb], writes=[A.smb])
    cx.dve(lambda e: e.tensor_tensor(out=sm[:, 0:1], in0=sm[:, 11:12], in1=sm[:, 10:11], op=ALU.subtract),
           reads=[A.smb], writes=[A.smb])
    cx.dve(lambda e: e.tensor_scalar(out=sm[:, 0:1], in0=sm[:, 0:1], scalar1=float(-LAMBDA_INIT), scalar2=None, op0=ALU.add),
           reads=[A.smb], writes=[A.smb])
    cx.dma("sp", sm[:, 1:3], dr["subln_g"].rearrange("o (m p) -> p (o m)", p=128), writes=[A.smb],
           allow_slow_non_contiguous=True)
    cx.dve(lambda e: e.tensor_scalar(out=sm[:, 1:3], in0=sm[:, 1:3], scalar1=float(1.0 - LAMBDA_INIT), scalar2=None,
                                     op0=ALU.mult), reads=[A.smb], writes=[A.smb])
    A.neg_lam = sm[:, 0:1]
    A.gsub = sm[:, 1:3]
    return A


def attn_head(cx, R, A, dr, h):
    T = A.T
    NTI = T // TT
    zT = dr["zT"]
    for grp, dst, dstb in (("q", A.qr, A.qrb), ("k", A.kr, A.krb)):
        first = True
        for c in range(2):
            r0 = ZROW[grp] + h * 256 + c * 128
            for i in range(NTI):
                ts_ = slice(i * TT, (i + 1) * TT)
                raw, rb = R.qring.next()
                cx.dma("sp", raw[:], zT[r0:r0 + 128, ts_], reads=[dr["_zT_b"]], writes=[rb])
                cs, csb = R.qring.next()
                cx.dma("sp", cs[0:32, :], dr["cosT"][:, ts_], writes=[csb])
                sn, snb = R.qring.next()
                cx.dma("sp", sn[0:32, :], dr["sinT"][:, ts_], writes=[snb])
                bi = i % 2
                bank, bb = R.banks[bi], R.bankb[bi]
                cx.pe(lambda e, bank=bank, raw=raw: e.matmul(bank[:], A.perm, raw[:], start=True, stop=True),
                      reads=[rb, A.cb], writes=[bb])
                wr = dict(writes=[dstb]) if first else dict(pwrites=[dstb])
                first = False
                ordb = Buf("ord")
                wr = dict(writes=[dstb, ordb]) if "writes" in wr else dict(pwrites=[dstb], writes=[ordb])
                cx.act(lambda e, raw=raw, dst=dst, c=c, ts_=ts_: e.activation(out=dst[:, c, ts_], in_=raw[:, :],
                                                                        func=AF.Copy), reads=[rb], **wr)
                cx.dve(lambda e, raw=raw, cs=cs: e.tensor_tensor(out=cs[0:32, :], in0=raw[0:32, :], in1=cs[0:32, :],
                                                             op=ALU.mult), reads=[rb, csb], writes=[csb])
                cx.dve(lambda e, bank=bank, sn=sn: e.tensor_tensor(out=sn[0:32, :], in0=sn[0:32, :], in1=bank[0:32, :],
                                                               op=ALU.mult), reads=[bb, snb], writes=[snb])
                cx.dve(lambda e, cs=cs, sn=sn, dst=dst, c=c, ts_=ts_: e.tensor_tensor(
                    out=dst[0:32, c, ts_], in0=cs[0:32, :], in1=sn[0:32, :], op=ALU.add),
                    reads=[csb, snb, ordb], pwrites=[dstb])
    cx.dma("sp", A.V, dr["vat"][:, h * 256:(h + 1) * 256].rearrange("(t p) v -> p t v", p=128),
           reads=[dr["_vat_b"]], writes=[A.Vb])
    for j in range(NTI):
        qs = slice(j * TT, (j + 1) * TT)
        for c in range(2):
            nkt = 4 * j + 4
            ob = [2, 3, 4] if c == 0 else [5, 6, 7]
            for kt in range(nkt):
                sbi = kt % 2
                sbank, sbb = R.banks[sbi], R.bankb[sbi]
                cx.pe(lambda e, sbank=sbank, c=c, kt=kt, qs=qs: e.matmul(
                    sbank[:], A.kr[:, c, kt * 128:(kt + 1) * 128], A.qr[:, c, qs], start=True, stop=True),
                    reads=[A.krb, A.qrb], writes=[sbb])
                pt, ptb = A.P.next()
                cx.act(lambda e, pt=pt, sbank=sbank: e.activation(out=pt, in_=sbank[:], func=AF.Exp, scale=float(ATT_SCALE)),
                       reads=[sbb], writes=[ptb])
                if kt >= 4 * j:
                    m = kt - 4 * j
                    cx.pool(lambda e, pt=pt, m=m: e.tensor_tensor(out=pt, in0=pt, in1=A.masks[:, m, :], op=ALU.mult),
                            reads=[ptb, A.cb], writes=[ptb])
                for mi, lhs in enumerate((A.V[:, kt, 0:128], A.V[:, kt, 128:256], A.ones_bf)):
                    cx.pe(lambda e, mi=mi, lhs=lhs, pt=pt, kt=kt, nkt=nkt, ob=ob: e.matmul(
                        R.banks[ob[mi]][:], lhs, pt, start=(kt == 0), stop=(kt == nkt - 1)),
                        reads=[A.Vb, A.cb, ptb], writes=[R.bankb[ob[mi]]], sig=(mi == 2))
            rec, recb = R.qring.next()
            cx.dve(lambda e, rec=rec, ob=ob: e.reciprocal(out=rec[:], in_=R.banks[ob[2]][:]),
                   reads=[R.bankb[ob[2]]], writes=[recb])
            for m in range(2):
                if c == 0:
                    cx.dve(lambda e, m=m, rec=rec, ob=ob: e.tensor_tensor(out=A.od[m], in0=R.banks[ob[m]][:], in1=rec[:],
                                                                      op=ALU.mult),
                           reads=[R.bankb[ob[m]], recb], writes=[A.odb[m]])
                else:
                    tmp, tmpb = R.qring.next()
                    cx.dve(lambda e, m=m, rec=rec, ob=ob, tmp=tmp: e.scalar_tensor_tensor(
                        out=tmp[:], in0=R.banks[ob[m]][:], scalar=A.neg_lam, in1=rec[:], op0=ALU.mult, op1=ALU.mult),
                        reads=[R.bankb[ob[m]], recb, A.smb], writes=[tmpb])
                    cx.pool(lambda e, m=m, tmp=tmp: e.tensor_tensor(out=A.od[m], in0=A.od[m], in1=tmp[:], op=ALU.add),
                            reads=[tmpb, A.odb[m]], writes=[A.odb[m]])
        for m in range(2):
            cx.act(lambda e, m=m: e.activation(out=A.sq[m], in_=A.od[m], func=AF.Square), reads=[A.odb[m]], writes=[A.sqb[m]])
        ssb_i = 0
        for m in range(2):
            cx.pe(lambda e, m=m: e.matmul(R.banks[ssb_i][:], A.ones_f, A.sq[m], start=(m == 0), stop=(m == 1)),
                  reads=[A.sqb[m], A.cb], writes=[R.bankb[ssb_i]], sig=(m == 1))
        rs, rsb = R.qring.next()
        cx.act(lambda e, rs=rs: e.activation(out=rs[:], in_=R.banks[ssb_i][:], func=AF.Sqrt, bias=float(RMS_EPS),
                                             scale=1.0 / 256), reads=[R.bankb[ssb_i]], writes=[rsb])
        cx.dve(lambda e, rs=rs: e.reciprocal(out=rs[:], in_=rs[:]), reads=[rsb], writes=[rsb])
        for m in range(2):
            fin, finb = A.fin.next()
            cx.dve(lambda e, m=m, rs=rs, fin=fin: e.scalar_tensor_tensor(
                out=fin, in0=A.od[m], scalar=A.gsub[:, m:m + 1], in1=rs[:], op0=ALU.mult, op1=ALU.mult),
                reads=[A.odb[m], rsb, A.smb], writes=[finb])
            cx.dma("sp", dr["oT_loc"][h * 256 + m * 128:h * 256 + (m + 1) * 128, qs], fin, reads=[finb],
                   pwrites=[dr["_oT_b"]])


def host_consts(T):
    import ml_dtypes
    c = {}
    c["ident"] = np.eye(128, dtype=np.float32)
    inv = (500000.0 ** (-np.arange(0, 32, 2, dtype=np.float32) / 32)).astype(np.float32)
    ang = np.arange(T, dtype=np.float32)[None, :] * np.concatenate([inv, inv])[:, None]
    c["cosT"] = np.cos(ang).astype(np.float32)
    c["sinT"] = np.sin(ang).astype(np.float32)
    perm = np.zeros((128, 128), np.float32)
    for d in range(16):
        perm[d + 16, d] = -1.0
        perm[d, d + 16] = 1.0
    c["perm"] = perm
    k = np.arange(128)[None, :, None]
    q = np.arange(512)[None, None, :]
    m = np.arange(4)[:, None, None]
    c["amask"] = (((m * 128 + k) // 64) <= (q // 64)).astype(np.float32).astype(ml_dtypes.bfloat16)
    p = np.arange(128)[:, None] % 64
    f = np.arange(512)[None, :] % 64
    rm = np.zeros((5, 128, 512), np.float32)
    rm[0] = np.broadcast_to((f != 0), (128, 512))
    rm[1] = (p < f)
    rm[2] = (p <= f)
    rm[3] = (f < p)
    rm[4] = (p == f)
    c["rmask"] = rm
    pp = np.arange(128)
    c["bones"] = (pp[:, None] // 64 == pp[None, :] // 64).astype(np.float32)
    return c


def declare_b_inputs(nc, dr, T, with_w=False):
    ext = lambda name, shape, dt=F32: nc.dram_tensor(name, shape, dt, kind="ExternalInput").ap()
    if with_w:
        dr["w_in_mine"] = ext("w_in_mine", [D, NWC])
    dr["lam4"] = ext("lam4", [1, 512])
    dr["subln_g"] = ext("subln_g", [1, 256])
    dr["cosT"] = ext("cosT", [32, T])
    dr["sinT"] = ext("sinT", [32, T])
    dr["perm"] = ext("perm", [128, 128])
    dr["amask"] = ext("amask", [4, 128, 512], BF16)
    dr["rmask"] = ext("rmask", [5, 128, 512])
    dr["bones"] = ext("bones", [128, 128])
    dr["pcols"] = ext("pcols", [8, 1024])
    dr["pcl"] = ext("pcl", [4, 128])
    dr["w2m"] = ext("w2m", [96, 1024])
    dr["a2m"] = ext("a2m", [96, 1024])
    dr["g2m"] = ext("g2m", [256, 1024])
    dr["lnw_t"] = ext("lnw_t", [2, 512])
    dr["lnb_t"] = ext("lnb_t", [2, 512])


def build_b_test(T, upto=9):
    nc = bass.Bass("TRN2", target_bir_lowering=False)
    dr = {}
    dr["ident"] = nc.dram_tensor("ident", [128, 128], F32, kind="ExternalInput").ap()
    dr["h2T_seq"] = nc.dram_tensor("h2T_seq", [1, D, T], BF16, kind="ExternalInput").ap()
    declare_b_inputs(nc, dr, T, with_w=True)
    dr["zT"] = nc.dram_tensor("zT", [NZ, T], F32, kind="ExternalOutput").ap()
    dr["vat"] = nc.dram_tensor("vat", [T, 1024], BF16, kind="Internal").ap()
    dr["oT_loc"] = nc.dram_tensor("oT_loc", [2048, T], BF16, kind="ExternalOutput").ap()
    for k in ("h2T_seq", "zT", "vat", "oT"):
        dr["_%s_b" % k] = Buf(k)
    cx = Cx(nc)
    with ExitStack() as es:
        R = alloc_common(nc, es, cx)
        load_consts(cx, R, dr)
        proj_stage(cx, R, dr, T)
        if upto >= 2 and upto != 3:
            A = attn_setup(cx, R, dr)
            for h in range(4):
                attn_head(cx, R, A, dr, h)
        if upto >= 3:
            cx.barrier()
            Wk = rwkv_setup(cx, R, dr)
            for ti in range(T // TT):
                rwkv_tile(cx, R, Wk, dr, ti)
        cx.final_wait("sp", [dr["_oT_b"], dr["_zT_b"]])
        cx.emit(es)
    return nc, cx


RW_LN_EPS = 64e-5
PC_MU_R, PC_MU_K, PC_MU_V, PC_W0, PC_A0, PC_KK, PC_KA, PC_RK = range(8)


def rwkv_setup(cx, R, dr):
    W = Res()
    cv = Carver(R.big)
    T = dr["zT"].shape[1]
    W.T = T
    f32t = lambda: cv.take(512)
    bft = lambda: cv.take(256, BF16)
    W.names = {}
    for n in ("xr", "xk", "xv", "t1", "EW", "A", "kk", "kmod", "lg", "eg", "egi", "egm", "y1", "y2", "y3"):
        setattr(W, n, f32t())
        setattr(W, n + "b", Buf(n))
    W.raw = Ring([cv.take(516) for _ in range(2)], "raw")
    for n in ("Rt", "Kt", "Bt", "At", "RKt", "Q", "N", "Q2", "N2", "P", "Pp", "MakT", "MrbT", "MrkT",
              "Ktm", "Btm", "Vtm", "otm", "oT", "twl", "tal"):
        setattr(W, n, bft())
        setattr(W, n + "b", Buf(n))
    W.sgl = cv.take(512, BF16).rearrange("p (k t) -> p k t", k=2)
    W.sglb = Buf("sgl")
    W.x1 = Ring([cv.take(32, BF16) for _ in range(2)], "x1")
    W.u = Ring([cv.take(32, BF16) for _ in range(2)], "u")
    W.ht = Ring([cv.take(64) for _ in range(2)], "ht")
    W.Hf = cv.take(512)
    W.Hb = cv.take(256, BF16)
    W.Hst = cv.take(288, BF16).rearrange("p (c v) -> p c v", v=64)
    W.Hstb = [Buf("Hst%d" % i) for i in range(9)]
    W.Hbuf = [Buf("H%d" % i) for i in range(8)]
    W.PC = cv.take(64).rearrange("p (i c) -> p i c", c=8)
    W.PL = cv.take(8)
    W.negw0 = cv.take(8)
    W.w2 = cv.take(512, BF16)
    W.a2 = cv.take(512, BF16)
    W.g2 = cv.take(1024, BF16).rearrange("p (k c) -> p k c", k=2)
    W.lnw2 = cv.take(512)
    W.lnb2 = cv.take(512)
    W.lnw = W.lnw2.rearrange("p (a v) -> p a v", v=64)
    W.lnb = W.lnb2.rearrange("p (a v) -> p a v", v=64)
    W.masks = cv.take(2560).rearrange("p (m f) -> p m f", m=5)
    W.bones = cv.take(64, BF16)
    W.identb = cv.take(64, BF16)
    W.onesc = cv.take(4, BF16)
    W.s8 = Ring([cv.take(8) for _ in range(6)], "s8")
    W.cb = Buf("rconst")
    cx.dma("sp", W.masks, dr["rmask"].rearrange("m p f -> p m f"), writes=[W.cb])
    cx.dma("pool", W.bones, dr["bones"], pwrites=[W.cb])
    cx.dma("pool", W.identb, dr["ident"], pwrites=[W.cb])
    cx.dma("pool", W.w2[0:96, :], dr["w2m"], pwrites=[W.cb])
    cx.dma("pool", W.a2[0:96, :], dr["a2m"], pwrites=[W.cb])
    cx.dma("pool", W.g2, dr["g2m"].rearrange("(k p) c -> p k c", p=128), pwrites=[W.cb])
    cx.pool(lambda e: e.memset(W.onesc, 1.0), pwrites=[W.cb])
    for hp in range(2):
        cx.dma("sp", W.lnw2[hp * 64:(hp + 1) * 64, :], dr["lnw_t"][hp:hp + 1, :].partition_broadcast(64), pwrites=[W.cb])
        cx.dma("sp", W.lnb2[hp * 64:(hp + 1) * 64, :], dr["lnb_t"][hp:hp + 1, :].partition_broadcast(64), pwrites=[W.cb])
    t, tb = R.qring.next()
    cx.dma("sp", t[0:64, 0:128], dr["pcols"].rearrange("i (c p) -> (i c) p", p=128), writes=[tb])
    cx.pe(lambda e: e.transpose(out=R.banks[0][:, 0:64], in_=t[0:64, 0:128], identity=R.ident[0:64, 0:64]),
          reads=[tb, R.identb], writes=[R.bankb[0]])
    cx.dve(lambda e: e.tensor_copy(out=W.PC, in_=R.banks[0][:, 0:64].rearrange("p (i c) -> p i c", c=8)),
           reads=[R.bankb[0]], pwrites=[W.cb])
    cx.dve(lambda e: e.tensor_scalar(out=W.negw0, in0=R.banks[0][:, PC_W0 * 8:PC_W0 * 8 + 8], scalar1=-1.0, scalar2=None,
                                     op0=ALU.mult), reads=[R.bankb[0]], pwrites=[W.cb])
    t2, t2b = R.qring.next()
    cx.dma("sp", t2[0:4, 0:128], dr["pcl"], writes=[t2b])
    cx.pe(lambda e: e.transpose(out=R.banks[1][:, 0:4], in_=t2[0:4, 0:128], identity=R.ident[0:4, 0:4]),
          reads=[t2b, R.identb], writes=[R.bankb[1]])
    cx.dve(lambda e: e.tensor_copy(out=W.PL[:, 0:4], in_=R.banks[1][:, 0:4]), reads=[R.bankb[1]], pwrites=[W.cb])
    cx.dve(lambda e: e.memset(W.Hf, 0.0), writes=W.Hbuf)
    cx.dve(lambda e: e.memset(W.Hb, 0.0), pwrites=W.Hbuf)
    return W


def rwkv_tile(cx, R, W, dr, ti):
    zT = dr["zT"]
    t0 = ti * TT
    MSK_RESET, MSK_U, MSK_UD, MSK_L = 0, 1, 2, 3
    cb = W.cb

    def shifted(row0, nrow, mu_col, out, outb, extra_reads=()):
        raw, rb = W.raw.next()
        if ti == 0:
            cx.dve(lambda e: e.memset(raw[0:nrow, 0:1], 0.0), writes=[rb])
            cx.dma("sp", raw[0:nrow, 1:513], zT[row0:row0 + nrow, 0:TT], reads=[dr["_zT_b"], rb], pwrites=[rb])
        else:
            cx.dma("sp", raw[0:nrow, 0:513], zT[row0:row0 + nrow, t0 - 1:t0 + TT], reads=[dr["_zT_b"]], writes=[rb])
        d, db = R.qring.next()
        cx.dve(lambda e: e.tensor_tensor(out=d[0:nrow, :], in0=raw[0:nrow, 0:512], in1=raw[0:nrow, 1:513], op=ALU.subtract),
               reads=[rb], writes=[db])
        cx.dve(lambda e: e.scalar_tensor_tensor(out=out, in0=d[0:nrow, :], scalar=mu_col, in1=raw[0:nrow, 1:513],
                                                op0=ALU.mult, op1=ALU.add),
               reads=[db, rb, cb] + list(extra_reads), writes=[outb])

    tl, tlb = R.qring.next()
    shifted(ZROW["wl"], 96, W.PL[0:96, 0:1], tl[0:96, :], tlb)
    cx.act(lambda e: e.activation(out=W.twl[0:96, :], in_=tl[0:96, :], func=AF.Tanh), reads=[tlb], writes=[W.twlb])
    tl2, tl2b = R.qring.next()
    shifted(ZROW["al"], 96, W.PL[0:96, 1:2], tl2[0:96, :], tl2b)
    cx.act(lambda e: e.activation(out=W.tal[0:96, :], in_=tl2[0:96, :], func=AF.Copy), reads=[tl2b], writes=[W.talb])
    for k in range(2):
        tg, tgb = R.qring.next()
        shifted(ZROW["gl"] + k * 128, 128, W.PL[:, 2 + k:3 + k], tg[:], tgb)
        kw = dict(writes=[W.sglb]) if k == 0 else dict(pwrites=[W.sglb])
        cx.act(lambda e, k=k, tg=tg: e.activation(out=W.sgl[:, k, :], in_=tg[:], func=AF.Sigmoid), reads=[tgb], **kw)

    for cc in range(8):
        col = lambda i: W.PC[:, i, cc:cc + 1]
        shifted(ZROW["r"] + cc * 128, 128, col(PC_MU_R), W.xr, W.xrb)
        shifted(ZROW["rk"] + cc * 128, 128, col(PC_MU_K), W.xk, W.xkb)
        shifted(ZROW["rv"] + cc * 128, 128, col(PC_MU_V), W.xv, W.xvb)
        b0, b1 = R.banks[0], R.banks[1]
        cx.pe(lambda e, cc=cc: e.matmul(b0[:], W.w2[0:96, cc * 128:(cc + 1) * 128], W.twl[0:96, :], start=True, stop=True),
              reads=[cb, W.twlb], writes=[R.bankb[0]])
        cx.dve(lambda e, cc=cc: e.tensor_scalar(out=W.t1, in0=b0[:], scalar1=W.PC[:, PC_W0, cc:cc + 1], scalar2=None,
                                                op0=ALU.add), reads=[R.bankb[0], cb], writes=[W.t1b])
        cx.act(lambda e: e.activation(out=W.t1, in_=W.t1, func=AF.Exp, scale=-1.0), reads=[W.t1b], writes=[W.t1b])
        cx.act(lambda e: e.activation(out=W.t1, in_=W.t1, func=AF.Ln, bias=1.0), reads=[W.t1b], writes=[W.t1b])
        cx.act(lambda e: e.activation(out=W.EW, in_=W.t1, func=AF.Exp, scale=-1.0, bias=-0.5), reads=[W.t1b], writes=[W.EWb])
        cx.pe(lambda e, cc=cc: e.matmul(b1[:], W.a2[0:96, cc * 128:(cc + 1) * 128], W.tal[0:96, :], start=True, stop=True),
              reads=[cb, W.talb], writes=[R.bankb[1]])
        cx.dve(lambda e, cc=cc: e.tensor_scalar(out=W.A, in0=b1[:], scalar1=W.PC[:, PC_A0, cc:cc + 1], scalar2=None,
                                                op0=ALU.add), reads=[R.bankb[1], cb], writes=[W.Ab])
        cx.act(lambda e: e.activation(out=W.A, in_=W.A, func=AF.Sigmoid), reads=[W.Ab], writes=[W.Ab])
        cx.dve(lambda e, cc=cc: e.tensor_scalar(out=W.kk, in0=W.xk, scalar1=W.PC[:, PC_KK, cc:cc + 1], scalar2=None,
                                                op0=ALU.mult), reads=[W.xkb, cb], writes=[W.kkb])
        cx.act(lambda e: e.activation(out=W.Rt, in_=W.kk, func=AF.Square), reads=[W.kkb], writes=[W.Rtb])
        cx.pe(lambda e: e.matmul(b0[:], W.bones, W.Rt, start=True, stop=True), reads=[cb, W.Rtb], writes=[R.bankb[0]])
        cx.act(lambda e: e.activation(out=W.t1, in_=b0[:], func=AF.Sqrt), reads=[R.bankb[0]], writes=[W.t1b])
        cx.dve(lambda e: e.tensor_scalar(out=W.t1, in0=W.t1, scalar1=1e-12, scalar2=None, op0=ALU.max),
               reads=[W.t1b], writes=[W.t1b])
        cx.dve(lambda e: e.reciprocal(out=W.t1, in_=W.t1), reads=[W.t1b], writes=[W.t1b])
        cx.dve(lambda e: e.tensor_tensor(out=W.kk, in0=W.kk, in1=W.t1, op=ALU.mult), reads=[W.kkb, W.t1b], writes=[W.kkb])
        cx.dve(lambda e, cc=cc: e.tensor_scalar(out=W.kmod, in0=W.A, scalar1=-1.0, scalar2=W.PC[:, PC_KA, cc:cc + 1],
                                                op0=ALU.add, op1=ALU.mult), reads=[W.Ab, cb], writes=[W.kmodb])
        cx.dve(lambda e: e.scalar_tensor_tensor(out=W.kmod, in0=W.kmod, scalar=1.0, in1=W.xk, op0=ALU.add, op1=ALU.mult),
               reads=[W.kmodb, W.xkb], writes=[W.kmodb])
        cx.dve(lambda e: e.tensor_tensor_scan(out=W.lg, data0=W.masks[:, MSK_RESET, :], data1=W.EW, initial=0.0,
                                              op0=ALU.mult, op1=ALU.subtract), reads=[W.EWb, cb], writes=[W.lgb])
        cx.act(lambda e: e.activation(out=W.eg, in_=W.lg, func=AF.Exp), reads=[W.lgb], writes=[W.egb])
        cx.act(lambda e: e.activation(out=W.egi, in_=W.lg, func=AF.Exp, scale=-1.0), reads=[W.lgb], writes=[W.egib])
        cx.dve(lambda e: e.tensor_tensor(out=W.egm, in0=W.lg, in1=W.EW, op=ALU.add), reads=[W.lgb, W.EWb], writes=[W.egmb])
        cx.act(lambda e: e.activation(out=W.egm, in_=W.egm, func=AF.Exp), reads=[W.egmb], writes=[W.egmb])
        cx.dve(lambda e: e.tensor_tensor(out=W.Rt, in0=W.xr, in1=W.eg, op=ALU.mult), reads=[W.xrb, W.egb], writes=[W.Rtb])
        cx.dve(lambda e: e.tensor_tensor(out=W.Kt, in0=W.kmod, in1=W.egi, op=ALU.mult), reads=[W.kmodb, W.egib], writes=[W.Ktb])
        cx.dve(lambda e: e.tensor_tensor(out=W.t1, in0=W.kk, in1=W.A, op=ALU.mult), reads=[W.kkb, W.Ab], writes=[W.t1b])
        cx.dve(lambda e: e.tensor_tensor(out=W.Bt, in0=W.t1, in1=W.egi, op=ALU.mult), reads=[W.t1b, W.egib], writes=[W.Btb])
        cx.dve(lambda e: e.scalar_tensor_tensor(out=W.At, in0=W.kk, scalar=-1.0, in1=W.egm, op0=ALU.mult, op1=ALU.mult),
               reads=[W.kkb, W.egmb], writes=[W.Atb])
        cx.dve(lambda e, cc=cc: e.scalar_tensor_tensor(out=W.RKt, in0=W.xr, scalar=W.PC[:, PC_RK, cc:cc + 1], in1=W.kmod,
                                                       op0=ALU.mult, op1=ALU.mult),
               reads=[W.xrb, W.kmodb, cb], writes=[W.RKtb])
        cx.act(lambda e: e.activation(out=W.oT, in_=W.xv, func=AF.Copy), reads=[W.xvb], writes=[W.oTb])

        blk = lambda ap, hp, c: ap[hp * 64:(hp + 1) * 64, c * 64:(c + 1) * 64]

        def blockmm(bank_i, lhs, lhsb, rhs, rhsb):
            bank, bb = R.banks[bank_i], R.bankb[bank_i]
            n = 0
            for hp in range(2):
                for c in range(8):
                    n += 1
                    cx.pe(lambda e, hp=hp, c=c: e.matmul(blk(bank, hp, c), blk(lhs, hp, c), blk(rhs, hp, c),
                                                         start=True, stop=True),
                          reads=[lhsb, rhsb], writes=[bb], sig=(n == 16))
            return bank, bb

        def scoremm(bank_i, lhs, lhsb, rhs, rhsb, mask_i, out, outb, eng):
            bank, bb = blockmm(bank_i, lhs, lhsb, rhs, rhsb)
            cx.dve(lambda e: e.tensor_tensor(out=out, in0=bank[:], in1=W.masks[:, mask_i, :], op=ALU.mult),
                   reads=[bb, cb], writes=[outb])

        scoremm(2, W.Bt, W.Btb, W.At, W.Atb, MSK_U, W.Q, W.Qb, "dve")
        scoremm(3, W.At, W.Atb, W.Bt, W.Btb, MSK_L, W.N, W.Nb, "dve")
        scoremm(4, W.Kt, W.Ktb, W.At, W.Atb, MSK_U, W.MakT, W.MakTb, "dve")
        scoremm(5, W.Bt, W.Btb, W.Rt, W.Rtb, MSK_UD, W.MrbT, W.MrbTb, "dve")
        scoremm(6, W.Kt, W.Ktb, W.Rt, W.Rtb, MSK_UD, W.MrkT, W.MrkTb, "dve")
        for (dst, dstb, src, srcb) in ((W.P, W.Pb, W.Q, W.Qb), (W.Pp, W.Ppb, W.N, W.Nb)):
            cx.dve(lambda e, dst=dst, src=src: e.tensor_tensor(out=dst, in0=src, in1=W.masks[:, 4, :], op=ALU.add),
                   reads=[srcb, cb], writes=[dstb])
        Qc, Qcb, Nc, Ncb = W.Q, W.Qb, W.N, W.Nb
        Qn, Qnb, Nn, Nnb = W.Q2, W.Q2b, W.N2, W.N2b
        for lvl in range(5):
            last = lvl == 4
            bk, bkb = blockmm(2, Nc, Ncb, Qc, Qcb)
            cx.act(lambda e, bk=bk, Qn=Qn: e.activation(out=Qn, in_=bk[:], func=AF.Copy), reads=[bkb], writes=[Qnb])
            if not last:
                bk2, bk2b = blockmm(3, Qc, Qcb, Nc, Ncb)
                cx.act(lambda e, bk2=bk2, Nn=Nn: e.activation(out=Nn, in_=bk2[:], func=AF.Copy), reads=[bk2b], writes=[Nnb])
            bp, bpb = blockmm(4, W.Pp, W.Ppb, Qn, Qnb)
            if not last:
                bq, bqb = blockmm(5, W.P, W.Pb, Nn, Nnb)
            cx.dve(lambda e, bp=bp: e.tensor_tensor(out=W.P, in0=bp[:], in1=W.P, op=ALU.add), reads=[bpb, W.Pb], writes=[W.Pb])
            if not last:
                cx.dve(lambda e, bq=bq: e.tensor_tensor(out=W.Pp, in0=bq[:], in1=W.Pp, op=ALU.add),
                       reads=[bqb, W.Ppb], writes=[W.Ppb])
            Qc, Qcb, Nc, Ncb, Qn, Qnb, Nn, Nnb = Qn, Qnb, Nn, Nnb, Qc, Qcb, Nc, Ncb
        c3 = lambda ap: ap.rearrange("p (c j) -> p c j", j=64)
        egc = c3(W.eg)[:, :, 63:64]
        for tl_, tlb_ in ((W.Bt, W.Btb), (W.Kt, W.Ktb)):
            cx.dve(lambda e, tl_=tl_: e.tensor_tensor(out=c3(tl_), in0=c3(tl_), in1=egc.to_broadcast([128, 8, 64]), op=ALU.mult),
                   reads=[tlb_, W.egb], writes=[tlb_])
        Atm, Atmb = W.Q, W.Qb
        for src, srcb, dst, dstb, bi in ((W.Kt, W.Ktb, W.Ktm, W.Ktmb, 6), (W.Bt, W.Btb, W.Btm, W.Btmb, 7),
                                         (W.oT, W.oTb, W.Vtm, W.Vtmb, 6), (W.At, W.Atb, Atm, Atmb, 7)):
            bank, bb = R.banks[bi], R.bankb[bi]
            bv = bank[:].bitcast(BF16)
            n = 0
            for hp in range(2):
                for c in range(8):
                    n += 1
                    cx.pe(lambda e, hp=hp, c=c, bv=bv, src=src: e.transpose(
                        out=blk(bv, hp, c), in_=blk(src, hp, c), identity=W.identb[hp * 64:(hp + 1) * 64, hp * 64:(hp + 1) * 64]),
                        reads=[srcb, cb], writes=[bb], sig=(n == 16))
            cx.act(lambda e, bv=bv, dst=dst: e.activation(out=dst, in_=bv[:, 0:512], func=AF.Copy), reads=[bb], writes=[dstb])
        Wa, Wab = W.N, W.Nb
        McT, McTb = W.Pp, W.Ppb
        X0, X0b = W.Q2, W.Q2b
        Uv, Uvb = W.N2, W.N2b
        bk, bkb = blockmm(2, W.P, W.Pb, Atm, Atmb)
        cx.act(lambda e, bk=bk: e.activation(out=Wa, in_=bk[:], func=AF.Copy), reads=[bkb], writes=[Wab])
        bk, bkb = blockmm(3, W.MakT, W.MakTb, W.Vtm, W.Vtmb)
        cx.dve(lambda e, bk=bk: e.tensor_copy(out=X0, in_=bk[:]), reads=[bkb], writes=[X0b])
        bk, bkb = blockmm(4, Wa, Wab, W.Btm, W.Btmb)
        cx.dve(lambda e: e.tensor_tensor(out=c3(W.y3), in0=c3(W.masks[:, 4, :]), in1=egc.to_broadcast([128, 8, 64]), op=ALU.mult),
               reads=[cb, W.egb], writes=[W.y3b])
        cx.dve(lambda e, bk=bk: e.tensor_tensor(out=McT, in0=bk[:], in1=W.y3, op=ALU.add), reads=[bkb, W.y3b], writes=[McTb])
        bk, bkb = blockmm(5, W.P, W.Pb, X0, X0b)
        cx.act(lambda e, bk=bk: e.activation(out=Uv, in_=bk[:], func=AF.Copy), reads=[bkb], writes=[Uvb])
        bk, bkb = blockmm(2, Wa, Wab, W.MrbT, W.MrbTb)
        cx.dve(lambda e, bk=bk: e.tensor_tensor(out=W.Rt, in0=bk[:], in1=W.Rt, op=ALU.add), reads=[bkb, W.Rtb], writes=[W.Rtb])
        bD, bDb = R.banks[3], R.bankb[3]
        n = 0
        for hp in range(2):
            for c in range(8):
                n += 1
                cx.pe(lambda e, hp=hp, c=c: e.matmul(blk(bD, hp, c), blk(W.Btm, hp, c), blk(Uv, hp, c), start=True, stop=False),
                      reads=[W.Btmb, Uvb], writes=[bDb], sig=False)
                cx.pe(lambda e, hp=hp, c=c: e.matmul(blk(bD, hp, c), blk(W.Ktm, hp, c), blk(W.Vtm, hp, c), start=False, stop=True),
                      reads=[W.Ktmb, W.Vtmb], writes=[bDb], sig=(n == 16))
        cx.act(lambda e: e.activation(out=W.y3, in_=bD[:], func=AF.Copy), reads=[bDb], writes=[W.y3b])
        bS, bSb = R.banks[7], R.bankb[7]
        n = 0
        for hp in range(2):
            for c in range(8):
                n += 1
                cx.pe(lambda e, hp=hp, c=c: e.matmul(bS[hp * 64:(hp + 1) * 64, 256 + c:257 + c], blk(W.RKt, hp, c),
                                                     W.onesc[hp * 64:(hp + 1) * 64, 0:1], start=True, stop=True),
                      reads=[W.RKtb, cb], writes=[bSb], sig=(n == 16))
        rk8, rk8b = W.s8.next()
        cx.dve(lambda e, rk8=rk8: e.tensor_copy(out=rk8, in_=bS[:, 256:264]), reads=[bSb], writes=[rk8b])
        bG, bGb = R.banks[1], R.bankb[1]
        n = 0
        for hp in range(2):
            for c in range(8):
                for k in range(2):
                    n += 1
                    cx.pe(lambda e, hp=hp, c=c, k=k, cc=cc: e.matmul(
                        blk(bG, hp, c), W.sgl[:, k, c * 64:(c + 1) * 64],
                        W.g2[:, k, cc * 128 + hp * 64:cc * 128 + (hp + 1) * 64], start=(k == 0), stop=(k == 1)),
                        reads=[W.sglb, cb], writes=[bGb], sig=(n == 32))
        bY, bYb = R.banks[0], R.bankb[0]
        Hbf = W.Hbuf[cc]
        Hcol = slice(cc * 64, (cc + 1) * 64)
        cx.dve(lambda e, Hcol=Hcol: e.tensor_copy(out=W.Hst[:, 0, :], in_=W.Hb[:, Hcol]), reads=[Hbf], writes=[W.Hstb[0]])
        bH, bHb = R.banks[4], R.bankb[4]
        for c in range(8):
            for hp in range(2):
                rows = slice(hp * 64, (hp + 1) * 64)
                cx.pe(lambda e, rows=rows, hp=hp, c=c: e.matmul(blk(bH, hp, c), blk(McT, hp, c), W.Hst[rows, c, :],
                                                             start=True, stop=True),
                      reads=[McTb, W.Hstb[c]], writes=[bHb], sig=(hp == 1))
            cx.dve(lambda e, c=c: e.tensor_tensor(out=W.Hst[:, c + 1, :], in0=bH[:, c * 64:(c + 1) * 64],
                                                  in1=W.y3[:, c * 64:(c + 1) * 64], op=ALU.add),
                   reads=[bHb, W.y3b], writes=[W.Hstb[c + 1]])
        cx.act(lambda e, Hcol=Hcol: e.activation(out=W.Hb[:, Hcol], in_=W.Hst[:, 8, :], func=AF.Copy),
               reads=[W.Hstb[8]], writes=[Hbf])
        n = 0
        for c in range(8):
            for hp in range(2):
                rows = slice(hp * 64, (hp + 1) * 64)
                n += 1
                cx.pe(lambda e, rows=rows, hp=hp, c=c: e.matmul(blk(bY, hp, c), blk(W.Rt, hp, c), W.Hst[rows, c, :],
                                                             start=True, stop=False), reads=[W.Rtb, W.Hstb[c]], writes=[bYb], sig=False)
                cx.pe(lambda e, hp=hp, c=c: e.matmul(blk(bY, hp, c), blk(W.MrbT, hp, c), blk(Uv, hp, c),
                                                     start=False, stop=False), reads=[W.MrbTb, Uvb], writes=[bYb], sig=False)
                cx.pe(lambda e, hp=hp, c=c: e.matmul(blk(bY, hp, c), blk(W.MrkT, hp, c), blk(W.Vtm, hp, c),
                                                     start=False, stop=True), reads=[W.MrkTb, W.Vtmb], writes=[bYb], sig=(n == 16))
        v3 = lambda ap: ap.rearrange("p (c v) -> p c v", v=64)
        m8, m8b = W.s8.next()
        cx.dve(lambda e, m8=m8: e.tensor_reduce(out=m8, in_=v3(bY[:]), axis=AX.X, op=ALU.add), reads=[bYb], writes=[m8b])
        cx.dve(lambda e, m8=m8: e.tensor_scalar(out=m8, in0=m8, scalar1=-1.0 / 64, scalar2=None, op0=ALU.mult), reads=[m8b], writes=[m8b])
        cx.dve(lambda e, m8=m8: e.tensor_tensor(out=v3(W.y1), in0=v3(bY[:]), in1=m8.unsqueeze(2).to_broadcast([128, 8, 64]),
                                         op=ALU.add), reads=[bYb, m8b], writes=[W.y1b])
        cx.act(lambda e: e.activation(out=W.y2, in_=W.y1, func=AF.Square), reads=[W.y1b], writes=[W.y2b])
        v8, v8b = W.s8.next()
        cx.dve(lambda e, v8=v8: e.tensor_reduce(out=v8, in_=v3(W.y2), axis=AX.X, op=ALU.add), reads=[W.y2b], writes=[v8b])
        cx.act(lambda e, v8=v8: e.activation(out=v8, in_=v8, func=AF.Sqrt, scale=1.0 / 64, bias=float(RW_LN_EPS)), reads=[v8b], writes=[v8b])
        cx.dve(lambda e, v8=v8: e.reciprocal(out=v8, in_=v8), reads=[v8b], writes=[v8b])
        cx.dve(lambda e, v8=v8: e.tensor_tensor(out=v3(W.y1), in0=v3(W.y1), in1=v8.unsqueeze(2).to_broadcast([128, 8, 64]), op=ALU.mult),
               reads=[W.y1b, v8b], writes=[W.y1b])
        cx.dve(lambda e, cc=cc: e.tensor_tensor(out=v3(W.y1), in0=v3(W.y1),
                                                in1=W.lnw[:, cc:cc + 1, :].to_broadcast([128, 8, 64]), op=ALU.mult),
               reads=[W.y1b, cb], writes=[W.y1b])
        cx.dve(lambda e, cc=cc: e.tensor_tensor(out=v3(W.y1), in0=v3(W.y1),
                                                in1=W.lnb[:, cc:cc + 1, :].to_broadcast([128, 8, 64]), op=ALU.add),
               reads=[W.y1b, cb], writes=[W.y1b])
        cx.dve(lambda e, rk8=rk8: e.tensor_tensor(out=v3(W.y2), in0=v3(W.Vtm), in1=rk8.unsqueeze(2).to_broadcast([128, 8, 64]),
                                         op=ALU.mult), reads=[W.Vtmb, rk8b], writes=[W.y2b])
        cx.dve(lambda e: e.tensor_tensor(out=W.y1, in0=W.y1, in1=W.y2, op=ALU.add), reads=[W.y1b, W.y2b], writes=[W.y1b])
        cx.dve(lambda e: e.tensor_tensor(out=W.otm, in0=W.y1, in1=bG[:], op=ALU.mult), reads=[W.y1b, bGb], writes=[W.otmb])
        bank, bb = R.banks[5], R.bankb[5]
        bv = bank[:].bitcast(BF16)
        n = 0
        for hp in range(2):
            for c in range(8):
                n += 1
                cx.pe(lambda e, hp=hp, c=c, bv=bv: e.transpose(
                    out=blk(bv, hp, c), in_=blk(W.otm, hp, c), identity=W.identb[hp * 64:(hp + 1) * 64, hp * 64:(hp + 1) * 64]),
                    reads=[W.otmb, cb], writes=[bb], sig=(n == 16))
        cx.act(lambda e, bv=bv: e.activation(out=W.N2, in_=bv[:, 0:512], func=AF.Copy), reads=[bb], writes=[W.N2b])
        cx.dma("sp", dr["oT_loc"][1024 + cc * 128:1024 + (cc + 1) * 128, t0:t0 + TT], W.N2, reads=[W.N2b],
               pwrites=[dr["_oT_b"]])


from concourse.bass import ds

SEQ = 4096
IN_PROJ = 12736
WALL = ["ffn1_w_gate", "ffn1_w_up", "ffn1_w_down", "w_out", "ffn2_w_gate", "ffn2_w_up", "ffn2_w_down",
        "ple_w_gate", "ple_w_proj"]
WSHAPE["w_out"] = (D, D)
GALL = ["ffn1_pre_g", "ffn1_post_g", "mix_pre_g", "mix_post_g", "ffn2_pre_g", "ffn2_post_g", "ple_pre_g", "ple_post_g"]
WIN_GROUPS = [(0, WCOL["q"]), (2048, WCOL["k"]), (4096, WCOL["v"]), (6144, WCOL["r"]), (8192, WCOL["rk"]),
              (10240, WCOL["rv"])]
WIN_LORA0 = 12288


def wout_rowmap(k0):
    return {0: 0, 8: 16, 16: 8, 24: 24}[k0]


def build_mega_cc(ncores=NCORES):
    ntok = NTOK
    T = SEQ
    nc = bass.Bass("TRN2", target_bir_lowering=False)
    ext = lambda name, shape, dt=F32: nc.dram_tensor(name, shape, dt, kind="ExternalInput").ap()
    loc = lambda name, shape, dt=F32: nc.dram_tensor(name, shape, dt, kind="Internal").ap()
    dr = {}
    dr["x"] = ext("x", [ntok, D])
    dr["p"] = ext("p", [ntok, 256])
    dr["ident"] = ext("ident", [128, 128])
    for g in GALL:
        dr[g] = ext(g, [1, D])
    shards = {}
    for w in WALL:
        r, c = WSHAPE[w]
        shards[w] = ext(w + "_sh", [r // ncores, c])
    win_sh = [ext("w_in_sh%d" % i, [2048 // ncores, IN_PROJ]) for i in range(2)]
    declare_b_inputs(nc, dr, T)
    out = nc.dram_tensor("out", [ntok, D], F32, kind="ExternalOutput").ap()
    x1, x2, x3 = loc("x1", [ntok, D]), loc("x2", [ntok, D]), loc("x3", [ntok, D])
    fscr = loc("fscr", [TT, D])
    h2T_loc = loc("h2T_loc", [D, ntok], BF16)
    dr["h2T_seq"] = loc("h2T_seq", [2, D, ntok], BF16)
    dr["zT"] = loc("zT", [NZ, T])
    dr["vat"] = loc("vat", [T, 1024], BF16)
    dr["oT_loc"] = loc("oT_loc", [2048, T], BF16)
    oT_mine = loc("oT_mine", [2, 2048, ntok], BF16)
    for k in ("h2T_seq", "zT", "vat", "oT"):
        dr["_%s_b" % k] = Buf(k)
    shared = nc.dram_tensor("wshared", [D * DFF], F32, kind="Internal", addr_space="Shared").ap()
    shared_bf = shared.bitcast(BF16)
    shb = Buf("wshared")
    cx = Cx(nc)
    cx.want_pid = True
    wbufs = {}
    groups = [list(range(ncores))]
    with ExitStack() as es:
        R = alloc_common(nc, es, cx)
        R.pT = es.enter_context(nc.sbuf_tensor("sb_pT", [128, 2, TT], BF16))
        R.pTb = [Buf("pT0"), Buf("pT1")]
        R.gstage = [es.enter_context(nc.sbuf_tensor("sb_gst%d" % i, [128, 512], F32)) for i in range(4)]
        R.gstageb = [Buf("gst%d" % i) for i in range(4)]
        load_consts(cx, R, dr)

        def gather_weight(w):
            r, c = WSHAPE[w]
            bounce = loc(w + "_bn", [r // ncores, c])
            full = loc(w, [r, c])
            shv = shared[0:r * c].rearrange("(r c) -> r c", c=c)
            bb, wb = Buf(w + "_bn"), Buf(w)
            cx.dma("pool", bounce, shards[w], writes=[bb])
            cx.cc(lambda e: e.collective_compute("AllGather", ALU.bypass, replica_groups=groups, ins=[bounce], outs=[shv]),
                  reads=[bb], writes=[shb])
            cx.dma("sp", full, shv, reads=[shb], writes=[wb])
            dr[w] = full
            wbufs[w] = wb

        for w in WALL[:3]:
            gather_weight(w)
        w_in_mine = loc("w_in_mine_l", [D, NWC])
        dr["w_in_mine"] = w_in_mine
        winb = Buf("w_in_mine")
        for i in range(2):
            bounce = loc("w_in_bn%d" % i, [2048 // ncores, IN_PROJ])
            shv = shared[0:2048 * IN_PROJ].rearrange("(r c) -> r c", c=IN_PROJ)
            bb = Buf("w_in_bn%d" % i)
            cx.dma("pool", bounce, win_sh[i], writes=[bb])
            cx.cc(lambda e, bounce=bounce, shv=shv: e.collective_compute("AllGather", ALU.bypass, replica_groups=groups,
                                                                         ins=[bounce], outs=[shv]), reads=[bb], writes=[shb])
            rows = slice(i * 2048, (i + 1) * 2048)
            for gbase, mybase in WIN_GROUPS:
                def fn(e, gbase=gbase, mybase=mybase, rows=rows, shv=shv):
                    half = cx.pid % 2
                    return e.dma_start(out=w_in_mine[rows, mybase:mybase + 1024], in_=shv[:, ds(half * 1024 + gbase, 1024)])
                cx.op("pool", fn, reads=[shb], pwrites=[winb], dma=True)
            cx.dma("pool", w_in_mine[rows, WCOL["wl"]:WCOL["wl"] + 448], shv[:, WIN_LORA0:WIN_LORA0 + 448],
                   reads=[shb], pwrites=[winb])
        for w in WALL[3:]:
            gather_weight(w)

        xb, x1b, x2b, x3b, ob, fb, hlb, omb = (Buf(n) for n in ("x", "x1", "x2", "x3", "out", "f", "h2T_loc", "oT_mine"))
        ntile = ntok // TT
        for tt in range(ntile):
            ffn_block(cx, R, dr, "ffn1", dr["x"], xb, x1, x1b, tt * TT, fscr, fb, wbuf=wbufs["ffn1_w_down"])
            norm_stage(cx, R, x1, x1b, tt * TT, GIDX["mix_pre_g"])
            cx.dma("sp", h2T_loc[:, tt * TT:(tt + 1) * TT].rearrange("(kc p) t -> p kc t", p=128), R.hT[:],
                   reads=R.hTb, pwrites=[hlb])
        shv = shared_bf[0:ncores * D * ntok].rearrange("(r t) -> r t", t=ntok)
        cx.cc(lambda e: e.collective_compute("AllGather", ALU.bypass, replica_groups=groups, ins=[h2T_loc], outs=[shv]),
              reads=[hlb], writes=[shb])

        def fn_h(e):
            pair = cx.pid // 2
            src = shv.rearrange("(q r d) t -> q r d t", q=ncores // 2, r=2)[ds(pair, 1), :, :, :]
            return e.dma_start(out=dr["h2T_seq"], in_=src.rearrange("q r d t -> (q r) d t"))
        cx.op("pool", fn_h, reads=[shb], writes=[dr["_h2T_seq_b"]], dma=True)
        cx.barrier()
        proj_stage(cx, R, dr, T)
        A = attn_setup(cx, R, dr)
        for h in range(4):
            attn_head(cx, R, A, dr, h)
        cx.barrier()
        Wk = rwkv_setup(cx, R, dr)
        for ti in range(T // TT):
            rwkv_tile(cx, R, Wk, dr, ti)
        shv2 = shared_bf[0:ncores * 2048 * T].rearrange("(r t) -> r t", t=T)
        cx.cc(lambda e: e.collective_compute("AllGather", ALU.bypass, replica_groups=groups, ins=[dr["oT_loc"]], outs=[shv2]),
              reads=[dr["_oT_b"]], writes=[shb])

        def fn_o(e):
            pair, half = cx.pid // 2, cx.pid % 2
            src = shv2.rearrange("(q r c) t -> q r c t", q=ncores // 2, r=2)[ds(pair, 1), :, :, ds(half * ntok, ntok)]
            return e.dma_start(out=oT_mine, in_=src.rearrange("q r c t -> (q r) c t"))
        cx.op("pool", fn_o, reads=[shb], writes=[omb], dma=True)
        cx.barrier()
        for tt in range(ntile):
            for r in range(2):
                cx.dma("sp", R.hT[:, r * 16:(r + 1) * 16, :],
                       oT_mine[r, :, tt * TT:(tt + 1) * TT].rearrange("(j p) t -> p j t", p=128),
                       reads=[omb], writes=(R.hTb if r == 0 else []), pwrites=([] if r == 0 else R.hTb))
            lin_tok_stage(cx, R, R.hT, (lambda k, ts: R.hTb[ts]), KC, dr["w_out"], D, make_f_evac(cx, R, fscr, fb),
                          wbuf=wbufs["w_out"], rowmap=wout_rowmap)
            finalize_stage(cx, R, fscr, fb, dr["mix_post_g"], x1, x1b, tt * TT, x2, x2b, tt * TT, 1.0)
            ffn_block(cx, R, dr, "ffn2", x2, x2b, x3, x3b, tt * TT, fscr, fb, wbuf=wbufs["ffn2_w_down"])
            ple_block(cx, R, dr, x3, x3b, out, ob, tt * TT, fscr, fb, wbuf=wbufs["ple_w_proj"])
        cx.final_wait("sp", [ob])
        cx.emit(es)
    return nc, cx


def _core_params(inputs, hf):
    f = lambda k: np.asarray(inputs[k], dtype=np.float32)
    sl = slice(hf * 1024, (hf + 1) * 1024)
    mu = f("rwkv_mu").reshape(-1)
    m = {}
    m["lam4"] = np.concatenate([f("diff_lambda_q1").reshape(-1), f("diff_lambda_k1").reshape(-1),
                                f("diff_lambda_q2").reshape(-1), f("diff_lambda_k2").reshape(-1)]).reshape(1, 512)
    m["subln_g"] = f("diff_subln_g").reshape(1, 256)
    m["pcols"] = np.ascontiguousarray(np.stack([
        mu[0:2048][sl], mu[2048:4096][sl], mu[4096:6144][sl], f("rwkv_w0").reshape(-1)[sl], f("rwkv_a0").reshape(-1)[sl],
        f("rwkv_k_k").reshape(-1)[sl], f("rwkv_k_a").reshape(-1)[sl], f("rwkv_r_k").reshape(-1)[sl]]))
    pcl = np.zeros((4, 128), np.float32)
    pcl[0, :96] = mu[6144:6240]
    pcl[1, :96] = mu[6240:6336]
    pcl[2] = mu[6336:6464]
    pcl[3] = mu[6464:6592]
    m["pcl"] = pcl
    m["w2m"] = np.ascontiguousarray(f("rwkv_w2").reshape(96, 2048)[:, sl])
    m["a2m"] = np.ascontiguousarray(f("rwkv_a2").reshape(96, 2048)[:, sl])
    m["g2m"] = np.ascontiguousarray(f("rwkv_g2").reshape(256, 2048)[:, sl])
    tm = lambda v: np.ascontiguousarray(v.reshape(8, 2, 64).transpose(1, 0, 2).reshape(2, 512))
    m["lnw_t"] = tm(f("rwkv_ln_w").reshape(-1)[sl])
    m["lnb_t"] = tm(f("rwkv_ln_b").reshape(-1)[sl])
    return m


def kernel_cc(**inputs):
    ncores = NCORES
    x = np.ascontiguousarray(inputs["x"], dtype=np.float32).reshape(-1, D)
    p = np.ascontiguousarray(inputs["p"], dtype=np.float32).reshape(-1, 256)
    ntok = x.shape[0] // ncores
    assert ntok == NTOK
    nc, cx = build_mega_cc(ncores)
    consts = host_consts(SEQ)
    w_in = np.asarray(inputs["w_in"]).reshape(D, IN_PROJ)
    in_maps = []
    for c in range(ncores):
        hf = c % 2
        m = {"x": x[c * ntok:(c + 1) * ntok], "p": p[c * ntok:(c + 1) * ntok]}
        m.update(consts)
        for g in GALL:
            m[g] = np.ascontiguousarray(inputs[g], dtype=np.float32).reshape(1, D)
        for w in WALL:
            r, cdim = WSHAPE[w]
            wf = np.asarray(inputs[w]).reshape(r, cdim)
            m[w + "_sh"] = np.ascontiguousarray(wf[c * (r // ncores):(c + 1) * (r // ncores)], dtype=np.float32)
        for i in range(2):
            r0 = i * 2048 + c * 256
            m["w_in_sh%d" % i] = np.ascontiguousarray(w_in[r0:r0 + 256], dtype=np.float32)
        m.update(_core_params(inputs, hf))
        in_maps.append(m)
    res = run_bass_kernel_spmd(nc, in_maps, core_ids=list(range(ncores)))
    out = np.concatenate([r["out"] for r in res.results], axis=0)
    return out.reshape(inputs["x"].shape).astype(np.float32)


BPAR = ["lam4", "subln_g", "pcols", "pcl", "w2m", "a2m", "g2m", "lnw_t", "lnb_t"]
BPSHAPE = {"lam4": [1, 512], "subln_g": [1, 256], "pcols": [8, 1024], "pcl": [4, 128], "w2m": [96, 1024],
           "a2m": [96, 1024], "g2m": [256, 1024], "lnw_t": [2, 512], "lnb_t": [2, 512]}
WFULL = WALL + ["w_in"]
WSHAPE["w_in"] = (D, IN_PROJ)


def build_mega(ncores=NCORES):
    ntok = NTOK
    T = SEQ
    nc = bass.Bass("TRN2", target_bir_lowering=False)
    ext = lambda name, shape, dt=F32: nc.dram_tensor(name, shape, dt, kind="ExternalInput").ap()
    loc = lambda name, shape, dt=F32: nc.dram_tensor(name, shape, dt, kind="Internal").ap()
    dr = {}
    dr["xseq"] = ext("xseq", [T, D])
    dr["p"] = ext("p", [ntok, 256])
    dr["ident"] = ext("ident", [128, 128])
    for g in GALL:
        dr[g] = ext(g, [1, D])
    for w in WFULL:
        dr[w] = ext(w, list(WSHAPE[w]))
    for k, shp, dt in (("cosT", [32, T], F32), ("sinT", [32, T], F32), ("perm", [128, 128], F32),
                       ("amask", [4, 128, 512], BF16), ("rmask", [5, 128, 512], F32), ("bones", [128, 128], F32)):
        dr[k] = ext(k, shp, dt)
    drh = []
    for hfp in range(2):
        d2 = {}
        for k in BPAR:
            d2[k] = ext("%s_%d" % (k, hfp), BPSHAPE[k])
        drh.append(d2)
    out = nc.dram_tensor("out", [ntok, D], F32, kind="ExternalOutput").ap()
    x1f = loc("x1f", [T, D])
    x1, x2, x3 = loc("x1", [ntok, D]), loc("x2", [ntok, D]), loc("x3", [ntok, D])
    fscr = loc("fscr", [TT, D])
    dr["h2T_seq"] = loc("h2T_seq", [1, D, T], BF16)
    dr["zT"] = loc("zT", [NZ, T])
    dr["vat"] = loc("vat", [T, 1024], BF16)
    oT_loc = [loc("oT_loc%d" % i, [2048, T], BF16) for i in range(2)]
    oT_mine = loc("oT_mine", [2, 2048, ntok], BF16)
    for k in ("h2T_seq", "zT", "vat"):
        dr["_%s_b" % k] = Buf(k)
    oTb = [Buf("oT0"), Buf("oT1")]
    cx = Cx(nc)
    cx.want_pid = True
    with ExitStack() as es:
        R = alloc_common(nc, es, cx)
        R.pT = es.enter_context(nc.sbuf_tensor("sb_pT", [128, 2, TT], BF16))
        R.pTb = [Buf("pT0"), Buf("pT1")]
        R.gstage = [es.enter_context(nc.sbuf_tensor("sb_gst%d" % i, [128, 512], F32)) for i in range(4)]
        R.gstageb = [Buf("gst%d" % i) for i in range(4)]
        load_consts(cx, R, dr)
        xb, x1fb, x1b, x2b, x3b, ob, fb, omb = (Buf(n) for n in ("x", "x1f", "x1", "x2", "x3", "out", "f", "oT_mine"))
        for tt in range(T // TT):
            ffn_block(cx, R, dr, "ffn1", dr["xseq"], xb, x1f, x1fb, tt * TT, fscr, fb)
            norm_stage(cx, R, x1f, x1fb, tt * TT, GIDX["mix_pre_g"])
            cx.dma("sp", dr["h2T_seq"][0, :, tt * TT:(tt + 1) * TT].rearrange("(kc p) t -> p kc t", p=128), R.hT[:],
                   reads=R.hTb, pwrites=[dr["_h2T_seq_b"]])

        def fn_x(e):
            half = cx.pid % 2
            return e.dma_start(out=x1, in_=x1f[ds(half * ntok, ntok), :])
        cx.op("pool", fn_x, reads=[x1fb], writes=[x1b], dma=True)
        cx.barrier()
        for hfp in range(2):
            d2 = dict(dr)
            d2.update(drh[hfp])
            d2["w_in_mine"] = dr["w_in"]
            d2["oT_loc"] = oT_loc[hfp]
            d2["_oT_b"] = oTb[hfp]
            wc = {"q": hfp * 1024, "k": 2048 + hfp * 1024, "v": 4096 + hfp * 1024, "r": 6144 + hfp * 1024,
                  "rk": 8192 + hfp * 1024, "rv": 10240 + hfp * 1024, "wl": 12288, "al": 12384, "gl": 12480}
            proj_stage(cx, R, d2, T, WCOL=wc)
            A = attn_setup(cx, R, d2)
            for h in range(4):
                attn_head(cx, R, A, d2, h)
            cx.barrier()
            Wk = rwkv_setup(cx, R, d2)
            for ti in range(T // TT):
                rwkv_tile(cx, R, Wk, d2, ti)
            cx.barrier()

        def fn_o(e, r):
            half = cx.pid % 2
            return e.dma_start(out=oT_mine[r], in_=oT_loc[r][:, ds(half * ntok, ntok)])
        for r in range(2):
            cx.op("pool", (lambda e, r=r: fn_o(e, r)), reads=[oTb[r]], pwrites=[omb], dma=True)
        cx.barrier()
        for tt in range(ntok // TT):
            for r in range(2):
                cx.dma("sp", R.hT[:, r * 16:(r + 1) * 16, :],
                       oT_mine[r, :, tt * TT:(tt + 1) * TT].rearrange("(j p) t -> p j t", p=128),
                       reads=[omb], writes=(R.hTb if r == 0 else []), pwrites=([] if r == 0 else R.hTb))
            lin_tok_stage(cx, R, R.hT, (lambda k, ts: R.hTb[ts]), KC, dr["w_out"], D, make_f_evac(cx, R, fscr, fb),
                          rowmap=wout_rowmap)
            finalize_stage(cx, R, fscr, fb, dr["mix_post_g"], x1, x1b, tt * TT, x2, x2b, tt * TT, 1.0)
            ffn_block(cx, R, dr, "ffn2", x2, x2b, x3, x3b, tt * TT, fscr, fb)
            ple_block(cx, R, dr, x3, x3b, out, ob, tt * TT, fscr, fb)
        cx.final_wait("sp", [ob])
        cx.emit(es)
    return nc, cx


def kernel(**inputs):
    ncores = NCORES
    x = np.ascontiguousarray(inputs["x"], dtype=np.float32).reshape(-1, SEQ, D)
    p = np.ascontiguousarray(inputs["p"], dtype=np.float32).reshape(-1, 256)
    nc, cx = build_mega(ncores)
    consts = host_consts(SEQ)
    shared = dict(consts)
    for g in GALL:
        shared[g] = np.ascontiguousarray(inputs[g], dtype=np.float32).reshape(1, D)
    for w in WFULL:
        shared[w] = np.ascontiguousarray(np.asarray(inputs[w]).reshape(WSHAPE[w]), dtype=np.float32)
    for hfp in range(2):
        for k, v in _core_params(inputs, hfp).items():
            shared["%s_%d" % (k, hfp)] = v
    in_maps = []
    for c in range(ncores):
        m = dict(shared)
        m["xseq"] = x[c // 2]
        m["p"] = p[c * NTOK:(c + 1) * NTOK]
        in_maps.append(m)
    res = run_bass_kernel_spmd(nc, in_maps, core_ids=list(range(ncores)))
    out = np.concatenate([r["out"] for r in res.results], axis=0)
    return out.reshape(inputs["x"].shape).astype(np.float32)
```

```python
import math
import os
DBG = os.environ.get('KDBG', '')
from contextlib import ExitStack

import numpy as np
import concourse.bass as bass
import concourse.mybir as mybir
from concourse.bass_utils import run_bass_kernel_spmd

F32 = mybir.dt.float32
BF16 = mybir.dt.bfloat16
AF = mybir.ActivationFunctionType
ALU = mybir.AluOpType
AX = mybir.AxisListType

D = 4096
DFF = 11008
NFC = DFF // 128
KC = D // 128
TT = 512
RMS_EPS = 1e-6
NCORES = 8


class Buf:
    __slots__ = ("name", "w", "r")

    def __init__(self, name=""):
        self.name = name
        self.w = {}
        self.r = {}


class _Eng:
    def __init__(self, name):
        self.name = name
        self.count = 0
        self.prog = []
        self.waited = {}


class Cx:
    DMA_K = 8

    def __init__(self, nc):
        self.nc = nc
        self.eng = {n: _Eng(n) for n in ("pe", "dve", "act", "pool", "sp")}
        self.dman = {"sp": 0, "pool": 0, "act": 0}
        self.semkeys = set()
        self.nops = 0

    def op(self, en, fn, reads=(), writes=(), sig=True, dma=False, pwrites=()):
        e = self.eng[en]
        waits = {}

        def need(tok):
            k, v = tok
            if k == "pe" and en == "pe":
                return
            if waits.get(k, 0) < v:
                waits[k] = v

        for b in reads:
            for tok in b.w.items():
                need(tok)
        for b in writes:
            for tok in b.w.items():
                need(tok)
            for tok in b.r.items():
                need(tok)
        for b in pwrites:
            for tok in b.r.items():
                need(tok)
        if dma:
            i = self.dman[en]
            self.dman[en] = i + 1
            k = "dma_%s_%d" % (en, i % self.DMA_K)
            v = 16 * (i // self.DMA_K + 1)
            if i >= self.DMA_K:
                need((k, v - 16))
            tok = (k, v)
            inc = (k, 16)
        elif sig:
            e.count += 1
            tok = (en, e.count)
            inc = (en, 1)
        else:
            tok = (en, e.count + 1)
            inc = None
        wl = []
        for k, v in waits.items():
            if e.waited.get(k, 0) >= v:
                continue
            e.waited[k] = v
            wl.append((k, v))
            self.semkeys.add(k)
        if inc is not None:
            self.semkeys.add(inc[0])
        e.prog.append((wl, fn, inc))
        self.nops += 1
        for b in writes:
            b.w = {tok[0]: tok[1]}
            b.r = {}
        for b in pwrites:
            if b.w.get(tok[0], 0) < tok[1]:
                b.w[tok[0]] = tok[1]
        for b in reads:
            if b in writes:
                continue
            if b.r.get(tok[0], 0) < tok[1]:
                b.r[tok[0]] = tok[1]
        return tok

    def pe(self, fn, reads=(), writes=(), sig=True, pwrites=()):
        return self.op("pe", fn, reads, writes, sig, pwrites=pwrites)

    def dve(self, fn, reads=(), writes=(), pwrites=()):
        return self.op("dve", fn, reads, writes, pwrites=pwrites)

    def act(self, fn, reads=(), writes=(), pwrites=()):
        return self.op("act", fn, reads, writes, pwrites=pwrites)

    def pool(self, fn, reads=(), writes=(), pwrites=()):
        return self.op("pool", fn, reads, writes, pwrites=pwrites)

    def dma(self, q, out, in_, reads=(), writes=(), pwrites=(), **kw):
        return self.op(q, lambda e: e.dma_start(out=out, in_=in_, **kw), reads, writes, dma=True, pwrites=pwrites)

    def cc(self, fn, reads=(), writes=()):
        e = self.eng["pool"]
        self.ncc = getattr(self, "ncc", 0) + 1
        k = "cc_%d" % self.ncc
        waits = {}
        for b in reads:
            for kk, v in b.w.items():
                waits[kk] = max(waits.get(kk, 0), v)
        for b in writes:
            for kk, v in list(b.w.items()) + list(b.r.items()):
                waits[kk] = max(waits.get(kk, 0), v)
        wl = []
        for kk, v in waits.items():
            if e.waited.get(kk, 0) >= v:
                continue
            e.waited[kk] = v
            wl.append((kk, v))
            self.semkeys.add(kk)
        self.semkeys.add(k)
        e.prog.append((wl, fn, (k, 1)))
        for b in writes:
            b.w = {k: 1}
            b.r = {}
        for b in reads:
            b.r[k] = 1

    def barrier(self):
        toks = {}
        for n, e in self.eng.items():
            if n != "sp" and e.count > 0:
                toks[n] = e.count
        for q, n in self.dman.items():
            for i in range(min(n, self.DMA_K)):
                last = ((n - 1 - i) // self.DMA_K) * self.DMA_K + i
                toks["dma_%s_%d" % (q, last % self.DMA_K)] = 16 * (last // self.DMA_K + 1)
        for i in range(getattr(self, "ncc", 0)):
            toks["cc_%d" % (i + 1)] = 1
        for n, e in self.eng.items():
            wl = []
            for k, v in toks.items():
                if e.waited.get(k, 0) < v:
                    e.waited[k] = v
                    wl.append((k, v))
                    self.semkeys.add(k)
            e.prog.append((wl, None, None))

    def final_wait(self, en, bufs):
        e = self.eng[en]
        wl = []
        for b in bufs:
            for k, v in b.w.items():
                if e.waited.get(k, 0) < v:
                    e.waited[k] = v
                    wl.append((k, v))
        e.prog.append((wl, None, None))

    def emit(self, es):
        nc = self.nc
        sems = {k: es.enter_context(nc.semaphore("s_" + k)) for k in sorted(self.semkeys)}
        block = es.enter_context(nc.Block())

        def replay(en):
            def f(engine):
                if en == "pool" and getattr(self, "want_pid", False):
                    self.pid = engine.partition_id()
                for wl, fn, inc in self.eng[en].prog:
                    for k, v in wl:
                        engine.wait_ge(sems[k], v)
                    if fn is None:
                        continue
                    ins = fn(engine)
                    if inc is not None:
                        ins.then_inc(sems[inc[0]], inc[1])
            return f

        block.tensor(replay("pe"))
        block.vector(replay("dve"))
        block.scalar(replay("act"))
        block.gpsimd(replay("pool"))
        block.sync(replay("sp"))


class Ring:
    def __init__(self, tiles, name):
        self.tiles = tiles
        self.bufs = [Buf("%s%d" % (name, i)) for i in range(len(tiles))]
        self.i = 0

    def next(self):
        j = self.i % len(self.tiles)
        self.i += 1
        return self.tiles[j], self.bufs[j]


class Res:
    pass


def alloc_common(nc, es, cx):
    R = Res()
    R.nc = nc
    sb = lambda name, shape, dt: es.enter_context(nc.sbuf_tensor("sb_" + name, shape, dt))
    R.ident = sb("ident", [128, 128], F32)
    R.identb = Buf("ident")
    R.hT = sb("hT", [128, KC, TT], BF16)
    R.hTb = [Buf("hT%d" % i) for i in range(4)]
    R.big = sb("big", [128, NFC * TT // 2], F32)
    R.aT = R.big[:].bitcast(BF16).rearrange("p (k c) -> p k c", c=TT)
    R.aTb = [Buf("aT%d" % i) for i in range(NFC)]
    R.wring = Ring([sb("wr%d" % i, [128, 4096], BF16) for i in range(5)], "wr")
    R.qring = Ring([sb("qr%d" % i, [128, 512], F32) for i in range(10)], "qr")
    R.stage = Ring([sb("stg%d" % i, [128, 512], F32) for i in range(2)], "stg")
    R.tmp = Ring([sb("tmp%d" % i, [128, 512], F32) for i in range(2)], "tmp")
    R.gring = Ring([sb("gbc%d" % i, [128, 512], F32) for i in range(2)], "gbc")
    R.junk = sb("junk", [128, 512], BF16)
    R.junkb = Buf("junk")
    R.gcol = sb("gcol", [128, 8, KC], F32)
    R.gcolb = Buf("gcol")
    R.small = sb("small", [128, 64], F32)
    R.ssq = sb("ssq", [128, 4, 8], F32)
    R.ssqb = [Buf("ssq%d" % i) for i in range(4)]
    R.st = Ring([sb("st%d" % i, [128, 8], F32) for i in range(12)], "st")
    R.banks = [es.enter_context(nc.psum_tensor("pb%d" % i, [128, 512], F32)) for i in range(8)]
    R.bankb = [Buf("bank%d" % i) for i in range(8)]
    return R


GIDX = {"ffn1_pre_g": 0, "mix_pre_g": 1, "ffn2_pre_g": 2, "ple_pre_g": 3}


def load_consts(cx, R, dr):
    cx.dma("sp", R.ident[:], dr["ident"], writes=[R.identb])
    for name, gi in GIDX.items():
        if name in dr:
            t, tb = R.qring.next()
            cx.dma("sp", t[0:KC, 0:128], dr[name].rearrange("o (kc p) -> (o kc) p", p=128), writes=[tb])
            cx.pe(lambda e, t=t: e.transpose(out=R.banks[0][:, 0:KC], in_=t[0:KC, 0:128], identity=R.ident[0:KC, 0:KC]),
                  reads=[tb, R.identb], writes=[R.bankb[0]])
            cx.dve(lambda e, gi=gi: e.tensor_copy(out=R.gcol[:, gi, :], in_=R.banks[0][:, 0:KC]),
                   reads=[R.bankb[0]], pwrites=[R.gcolb])


def rstd_from_ss(cx, R, ss_ap, ssb, n, coef=1.0):
    t, tb = R.st.next()
    cx.act(lambda e: e.activation(out=t[:, 0:1], in_=ss_ap, func=AF.Sqrt, bias=float(RMS_EPS), scale=1.0 / n),
           reads=[ssb], writes=[tb])
    cx.dve(lambda e: e.reciprocal(out=t[:, 1:2], in_=t[:, 0:1]), reads=[tb], writes=[tb])
    if coef != 1.0:
        cx.dve(lambda e: e.tensor_scalar(out=t[:, 1:2], in0=t[:, 1:2], scalar1=float(coef), scalar2=None,
                                         op0=ALU.mult), reads=[tb], writes=[tb])
    return t[:, 1:2], tb


def norm_stage(cx, R, src, srcb, row0, gi):
    for ts in range(4):
        r0 = row0 + ts * 128
        ss, ssb = R.st.next()
        rows = []
        for q in range(8):
            xt, xb = R.qring.next()
            rows.append((xt, xb))
            cx.dma("sp", xt[:], src[r0:r0 + 128, q * 512:(q + 1) * 512], reads=[srcb], writes=[xb])
            cx.act(lambda e, xt=xt, q=q, ss=ss: e.activation(out=R.junk[:], in_=xt[:], func=AF.Square,
                                                          accum_out=ss[:, q:q + 1]),
                   reads=[xb], writes=[R.junkb], pwrites=[ssb])
        rs, rsb = R.st.next()
        cx.dve(lambda e, ss=ss, rs=rs: e.tensor_reduce(out=rs[:, 4:5], in_=ss[:, 0:8], axis=AX.X, op=ALU.add),
               reads=[ssb], writes=[rsb])
        rstd, rb = rstd_from_ss(cx, R, rs[:, 4:5], rsb, D)
        for q in range(8):
            xt, xb = rows[q]
            if True:
                cx.dve(lambda e, xt=xt, rstd=rstd: e.tensor_scalar(out=xt[:], in0=xt[:], scalar1=rstd, scalar2=None,
                                                                op0=ALU.mult), reads=[xb, rb], writes=[xb])
            else:
                cx.act(lambda e, xt=xt, rstd=rstd: e.activation(out=xt[:], in_=xt[:], func=AF.Copy, scale=rstd),
                       reads=[xb, rb], writes=[xb])
            bi = (ts * 8 + q) % 2
            bank, bb = R.banks[bi], R.bankb[bi]
            for j in range(4):
                cx.pe(lambda e, bank=bank, j=j, xt=xt: e.transpose(
                    out=bank[:, j * 128:(j + 1) * 128], in_=xt[:, j * 128:(j + 1) * 128], identity=R.ident[:]),
                    reads=[xb, R.identb], writes=[bb], sig=(j == 3))
            for j in range(4):
                kc = q * 4 + j
                if True:
                    cx.dve(lambda e, bank=bank, j=j, kc=kc, ts=ts: e.tensor_scalar(
                        out=R.hT[:, kc, ts * 128:(ts + 1) * 128], in0=bank[:, j * 128:(j + 1) * 128],
                        scalar1=R.gcol[:, gi, kc:kc + 1], scalar2=None, op0=ALU.mult),
                        reads=[bb, R.gcolb], pwrites=[R.hTb[ts]])
                else:
                    cx.act(lambda e, bank=bank, j=j, kc=kc, ts=ts: e.activation(
                        out=R.hT[:, kc, ts * 128:(ts + 1) * 128], in_=bank[:, j * 128:(j + 1) * 128],
                        func=AF.Copy, scale=R.gcol[:, gi, kc:kc + 1]),
                        reads=[bb, R.gcolb], pwrites=[R.hTb[ts]])


def gateup_stage(cx, R, Wg, Wu, wbuf=None, wbuf2=None):
    Wgv = Wg.rearrange("(kc p) c -> p kc c", p=128)
    Wuv = Wu.rearrange("(kc p) c -> p kc c", p=128)
    nblk = NFC // 2
    for blk in range(nblk):
        c0 = blk * 256
        gb = [(blk % 2) * 4 + 0, (blk % 2) * 4 + 1]
        ub = [(blk % 2) * 4 + 2, (blk % 2) * 4 + 3]
        for half in range(2):
            pieces = []
            for Wv in (Wgv, Wuv):
                wt, wb = R.wring.next()
                wv = wt[:].rearrange("p (k c) -> p k c", c=256)
                cx.dma("pool", wv, Wv[:, half * 16:(half + 1) * 16, c0:c0 + 256],
                       reads=[b_ for b_ in (wbuf, wbuf2) if b_ is not None], writes=[wb])
                pieces.append((wv, wb))
            for (wv, wb), banks in zip(pieces, (gb, ub)):
                for j in range(2):
                    bank, bb = R.banks[banks[j]], R.bankb[banks[j]]
                    for kcl in range(16):
                        kc = half * 16 + kcl
                        cx.pe(lambda e, bank=bank, wv=wv, kcl=kcl, j=j, kc=kc: e.matmul(
                            bank[:], wv[:, kcl, j * 128:(j + 1) * 128], R.hT[:, kc, :],
                            start=(kc == 0), stop=(kc == KC - 1)),
                            reads=[wb] + R.hTb, writes=[bb], sig=(kcl == 15))
        for j in range(2):
            fc = blk * 2 + j
            t, tb = R.tmp.next()
            gbank, ubank = R.banks[gb[j]], R.banks[ub[j]]
            cx.act(lambda e, t=t, gbank=gbank: e.activation(out=t[:], in_=gbank[:], func=AF.Silu),
                   reads=[R.bankb[gb[j]]], writes=[tb])
            cx.dve(lambda e, t=t, fc=fc, ubank=ubank: e.tensor_tensor(out=R.aT[:, fc, :], in0=t[:], in1=ubank[:],
                                                                   op=ALU.mult),
                   reads=[tb, R.bankb[ub[j]]], writes=[R.aTb[fc]])


def lin_tok_stage(cx, R, actT, actb, nK, W, ncols, evac, cbs=None, par=0, wbuf=None, rowmap=None):
    Wv = W.rearrange("(k p) c -> p k c", p=128)
    G = 8
    ncb = ncols // 512
    for cb in (range(ncb) if cbs is None else cbs):
        banks = [((cb + par) % 2) * 4 + ts for ts in range(4)]
        for k0 in range(0, nK, G):
            g = min(G, nK - k0)
            wt, wb = R.wring.next()
            wv = wt[:].rearrange("p (k c) -> p k c", c=512)
            rk0 = k0 if rowmap is None else rowmap(k0)
            cx.dma("pool", wv[:, 0:g, :], Wv[:, rk0:rk0 + g, cb * 512:(cb + 1) * 512],
                   reads=([wbuf] if wbuf is not None else []), writes=[wb])
            for kl in range(g):
                k = k0 + kl
                for ts in range(4):
                    bank = R.banks[banks[ts]]
                    cx.pe(lambda e, bank=bank, k=k, kl=kl, ts=ts, wv=wv: e.matmul(
                        bank[:], actT[:, k, ts * 128:(ts + 1) * 128], wv[:, kl, :],
                        start=(k == 0), stop=(k == nK - 1)),
                        reads=[wb, (actb(k, ts) if callable(actb) else actb[k])], writes=[R.bankb[banks[ts]]],
                        sig=(k == nK - 1 or kl == g - 1))
        for ts in range(4):
            evac(cb, ts, R.banks[banks[ts]], R.bankb[banks[ts]])


def make_f_evac(cx, R, fscr, fb):
    def evac(cb, ts, bank, bb):
        s, sb_ = R.stage.next()
        cx.act(lambda e: e.activation(out=s[:], in_=bank[:], func=AF.Copy), reads=[bb], writes=[sb_])
        cx.act(lambda e: e.activation(out=R.junk[:], in_=s[:], func=AF.Square,
                                      accum_out=R.ssq[:, ts, cb:cb + 1]),
               reads=[sb_], writes=[R.junkb], pwrites=[R.ssqb[ts]])
        cx.dma("sp", fscr[ts * 128:(ts + 1) * 128, cb * 512:(cb + 1) * 512], s[:], reads=[sb_], pwrites=[fb])
    return evac


def finalize_stage(cx, R, fscr, fb, gpost, src, srcb, srow0, dst, dstb, drow0, coef, ncb=8):
    steps = [(ts, q) for q in range(8) for ts in range(4)]
    rcs = {}
    loaded = {}
    gcur = {}

    def load(i):
        ts, q = steps[i]
        cs = slice(q * 512, (q + 1) * 512)
        if ts == 0:
            gt, gb_ = R.gring.next()
            cx.dma("sp", gt[:], gpost[0:1, cs].partition_broadcast(128), writes=[gb_])
            gcur[q] = (gt, gb_)
        gt, gb_ = gcur[q]
        ft, fbq = R.qring.next()
        cx.dma("sp", ft[:], fscr[ts * 128:(ts + 1) * 128, cs], reads=[fb], writes=[fbq])
        xt, xb = R.qring.next()
        cx.dma("sp", xt[:], src[srow0 + ts * 128:srow0 + (ts + 1) * 128, cs], reads=[srcb], writes=[xb])
        loaded[i] = (ft, fbq, xt, xb, gt, gb_)

    def compute(i):
        ts, q = steps[i]
        cs = slice(q * 512, (q + 1) * 512)
        if ts not in rcs:
            ss, ssb = R.st.next()
            cx.dve(lambda e, ss=ss, ts=ts: e.tensor_reduce(out=ss[:, 0:1], in_=R.ssq[:, ts, 0:ncb], axis=AX.X,
                                                        op=ALU.add), reads=[R.ssqb[ts]], writes=[ssb])
            rcs[ts] = rstd_from_ss(cx, R, ss[:, 0:1], ssb, D, coef)
        rc, rb = rcs[ts]
        ft, fbq, xt, xb, gt, gb_ = loaded.pop(i)
        cx.dve(lambda e: e.scalar_tensor_tensor(out=ft[:], in0=ft[:], scalar=rc, in1=gt[:], op0=ALU.mult,
                                                op1=ALU.mult), reads=[fbq, gb_, rb], writes=[fbq])
        cx.pool(lambda e: e.tensor_tensor(out=ft[:], in0=ft[:], in1=xt[:], op=ALU.add),
                reads=[fbq, xb], writes=[fbq])
        cx.dma("sp", dst[drow0 + ts * 128:drow0 + (ts + 1) * 128, cs], ft[:], reads=[fbq], pwrites=[dstb])

    load(0)
    load(1)
    for i in range(len(steps)):
        if i + 2 < len(steps):
            load(i + 2)
        compute(i)


def ffn_block(cx, R, dr, pfx, src, srcb, dst, dstb, row0, fscr, fb, upto=9, wbuf=None, wbufs=None):
    if upto >= 1:
        norm_stage(cx, R, src, srcb, row0, GIDX[pfx + "_pre_g"])
    if upto >= 2:
        gateup_stage(cx, R, dr[pfx + "_w_gate"], dr[pfx + "_w_up"], wbuf,
                     wbuf2=(wbufs or {}).get(pfx + "_w_gate"))
    if upto >= 3:
        lin_tok_stage(cx, R, R.aT, R.aTb, NFC, dr[pfx + "_w_down"], D, make_f_evac(cx, R, fscr, fb), wbuf=wbuf)
    if upto >= 4:
        finalize_stage(cx, R, fscr, fb, dr[pfx + "_post_g"], src, srcb, row0, dst, dstb, row0, 0.5)


def build_test_ffn(ntok, upto=9):
    nc = bass.Bass("TRN2", target_bir_lowering=False)
    dr = {}
    dr["x"] = nc.dram_tensor("x", [ntok, D], F32, kind="ExternalInput").ap()
    dr["ident"] = nc.dram_tensor("ident", [128, 128], F32, kind="ExternalInput").ap()
    dr["ffn1_pre_g"] = nc.dram_tensor("ffn1_pre_g", [1, D], F32, kind="ExternalInput").ap()
    dr["ffn1_post_g"] = nc.dram_tensor("ffn1_post_g", [1, D], F32, kind="ExternalInput").ap()
    if upto >= 2:
        dr["ffn1_w_gate"] = nc.dram_tensor("ffn1_w_gate", [D, DFF], F32, kind="ExternalInput").ap()
        dr["ffn1_w_up"] = nc.dram_tensor("ffn1_w_up", [D, DFF], F32, kind="ExternalInput").ap()
        dr["ffn1_w_down"] = nc.dram_tensor("ffn1_w_down", [DFF, D], F32, kind="ExternalInput").ap()
    y = nc.dram_tensor("y", [ntok, D], F32, kind="ExternalOutput").ap()
    fscr = nc.dram_tensor("fscr", [TT, D], F32, kind="Internal").ap()
    cx = Cx(nc)
    with ExitStack() as es:
        R = alloc_common(nc, es, cx)
        load_consts(cx, R, dr)
        xb, yb, fb = Buf("x"), Buf("y"), Buf("f")
        for tt in range(ntok // TT):
            ffn_block(cx, R, dr, "ffn1", dr["x"], xb, y, yb, tt * TT, fscr, fb, upto)
        if upto < 4:
            t, tb = R.stage.next()
            cx.dve(lambda e: e.tensor_copy(out=t[:], in_=(R.hT[:, 0, :] if upto < 2 else R.aT[:, 0, :])),
                   reads=R.hTb + R.aTb + [R.gcolb], writes=[tb])
            cx.dma("sp", y[0:128, 0:512], t[:], reads=[tb], pwrites=[yb])
        cx.final_wait("sp", [yb, fb])
        for en in ("pe", "dve", "act", "pool"):
            cx.final_wait("sp", [])
        cx.emit(es)
    return nc, cx


def ple_block(cx, R, dr, src, srcb, dst, dstb, row0, fscr, fb, wbuf=None):
    norm_stage(cx, R, src, srcb, row0, GIDX["ple_pre_g"])
    for ts in range(4):
        pt, pb = R.qring.next()
        cx.dma("sp", pt[:, 0:256], dr["p"][row0 + ts * 128:row0 + (ts + 1) * 128, :], writes=[pb])
        bank, bb = R.banks[ts % 2], R.bankb[ts % 2]
        for j in range(2):
            cx.pe(lambda e, bank=bank, j=j, pt=pt: e.transpose(out=bank[:, j * 128:(j + 1) * 128],
                                                             in_=pt[:, j * 128:(j + 1) * 128], identity=R.ident[:]),
                  reads=[pb, R.identb], writes=[bb], sig=(j == 1))
        cx.dve(lambda e, bank=bank, ts=ts: e.tensor_copy(
            out=R.pT[:, :, ts * 128:(ts + 1) * 128], in_=bank[:, 0:256].rearrange("p (j t) -> p j t", j=2)),
            reads=[bb], pwrites=[R.pTb[0], R.pTb[1]])
    gst = {}

    def gate_evac(cb, ts, bank, bb):
        cx.act(lambda e: e.activation(out=R.gstage[ts][:], in_=bank[:], func=AF.Sigmoid),
               reads=[bb], writes=[R.gstageb[ts]])

    def pp_evac(cb, ts, bank, bb):
        s_, sb_ = R.stage.next()
        cx.dve(lambda e: e.tensor_tensor(out=s_[:], in0=R.gstage[ts][:], in1=bank[:], op=ALU.mult),
               reads=[bb, R.gstageb[ts]], writes=[sb_])
        cx.act(lambda e: e.activation(out=R.junk[:], in_=s_[:], func=AF.Square, accum_out=R.ssq[:, ts, cb:cb + 1]),
               reads=[sb_], writes=[R.junkb], pwrites=[R.ssqb[ts]])
        cx.dma("sp", fscr[ts * 128:(ts + 1) * 128, cb * 512:(cb + 1) * 512], s_[:], reads=[sb_], pwrites=[fb])

    for cb in range(8):
        lin_tok_stage(cx, R, R.hT, (lambda k, ts: R.hTb[ts]), KC, dr["ple_w_gate"], D, gate_evac, cbs=[cb], par=0, wbuf=wbuf)
        lin_tok_stage(cx, R, R.pT, R.pTb, 2, dr["ple_w_proj"], D, pp_evac, cbs=[cb], par=1, wbuf=wbuf)
    finalize_stage(cx, R, fscr, fb, dr["ple_post_g"], src, srcb, row0, dst, dstb, row0, 1.0)


WNAMES = ["ffn1_w_gate", "ffn1_w_up", "ffn1_w_down", "ffn2_w_gate", "ffn2_w_up", "ffn2_w_down",
          "ple_w_gate", "ple_w_proj"]
WSHAPE = {"ffn1_w_gate": (D, DFF), "ffn1_w_up": (D, DFF), "ffn1_w_down": (DFF, D),
          "ffn2_w_gate": (D, DFF), "ffn2_w_up": (D, DFF), "ffn2_w_down": (DFF, D),
          "ple_w_gate": (D, D), "ple_w_proj": (256, D)}
GNAMES = ["ffn1_pre_g", "ffn1_post_g", "mix_pre_g", "ffn2_pre_g", "ffn2_post_g", "ple_pre_g", "ple_post_g"]
NTOK = 2048


def build_full(ntok=NTOK, ncores=NCORES, ag=True):
    nc = bass.Bass("TRN2", target_bir_lowering=False)
    dr = {}
    dr["x"] = nc.dram_tensor("x", [ntok, D], F32, kind="ExternalInput").ap()
    dr["p"] = nc.dram_tensor("p", [ntok, 256], F32, kind="ExternalInput").ap()
    dr["ident"] = nc.dram_tensor("ident", [128, 128], F32, kind="ExternalInput").ap()
    for g in GNAMES:
        dr[g] = nc.dram_tensor(g, [1, D], F32, kind="ExternalInput").ap()
    cx = Cx(nc)
    wbufs = {}
    shards = {}
    for w in WNAMES:
        r, c = WSHAPE[w]
        if ag:
            shards[w] = nc.dram_tensor(w + "_sh", [r // ncores, c], F32, kind="ExternalInput").ap()
        else:
            dr[w] = nc.dram_tensor(w, [r, c], F32, kind="ExternalInput").ap()
            wbufs[w] = None
    out = nc.dram_tensor("out", [ntok, D], F32, kind="ExternalOutput").ap()
    x1 = nc.dram_tensor("x1", [ntok, D], F32, kind="Internal").ap()
    x3 = nc.dram_tensor("x3", [ntok, D], F32, kind="Internal").ap()
    fscr = nc.dram_tensor("fscr", [TT, D], F32, kind="Internal").ap()
    with ExitStack() as es:
        R = alloc_common(nc, es, cx)
        R.pT = es.enter_context(nc.sbuf_tensor("sb_pT", [128, 2, TT], BF16))
        R.pTb = [Buf("pT0"), Buf("pT1")]
        R.gstage = [es.enter_context(nc.sbuf_tensor("sb_gst%d" % i, [128, 512], F32)) for i in range(4)]
        R.gstageb = [Buf("gst%d" % i) for i in range(4)]
        load_consts(cx, R, dr)
        if ag:
            shared = nc.dram_tensor("wshared", [D * DFF], F32, kind="Internal", addr_space="Shared").ap()
            shb = Buf("wshared")
        for w in (WNAMES if ag else []):
            r, c = WSHAPE[w]
            bounce = nc.dram_tensor(w + "_bn", [r // ncores, c], F32, kind="Internal").ap()
            full = nc.dram_tensor(w, [r, c], F32, kind="Internal").ap()
            shv = shared[0:r * c].rearrange("(r c) -> r c", c=c)
            bb, wb = Buf(w + "_bn"), Buf(w)
            cx.dma("pool", bounce, shards[w], writes=[bb])
            cx.cc(lambda e, bounce=bounce, shv=shv: e.collective_compute(
                "AllGather", ALU.bypass, replica_groups=[list(range(ncores))], ins=[bounce], outs=[shv]),
                reads=[bb], writes=[shb])
            cx.dma("sp", full, shv, reads=[shb], writes=[wb])
            dr[w] = full
            wbufs[w] = wb
        xb, x1b, x3b, ob, fb = Buf("x"), Buf("x1"), Buf("x3"), Buf("out"), Buf("f")
        ntile = ntok // TT
        for tt in range(ntile):
            ffn_block(cx, R, dr, "ffn1", dr["x"], xb, x1, x1b, tt * TT, fscr, fb, wbuf=wbufs["ffn1_w_down"], wbufs=wbufs)
        for tt in range(ntile):
            ffn_block(cx, R, dr, "ffn2", x1, x1b, x3, x3b, tt * TT, fscr, fb, wbuf=wbufs["ffn2_w_down"], wbufs=wbufs)
        for tt in range(ntile):
            ple_block(cx, R, dr, x3, x3b, out, ob, tt * TT, fscr, fb, wbuf=wbufs["ple_w_proj"])
        cx.final_wait("sp", [ob])
        cx.emit(es)
    return nc, cx


NQK = 1024
ZROW = {"q": 0, "k": 1024, "r": 2048, "rk": 3072, "rv": 4096, "wl": 5120, "al": 5216, "gl": 5312}
NZ = 5568
WCOL = {"q": 0, "k": 1024, "v": 2048, "r": 3072, "rk": 4096, "rv": 5120, "wl": 6144, "al": 6240, "gl": 6336}
NWC = 6592
ATT_SCALE = 128 ** -0.5
LAMBDA_INIT = 0.8 - 0.6 * math.exp(-0.3 * 0)


class Carver:
    def __init__(self, big, limit=22016):
        self.big, self.o, self.limit = big, 0, limit

    def take(self, nwords, dt=F32):
        a = self.o
        self.o += nwords
        assert self.o <= self.limit, self.o
        v = self.big[:, a:a + nwords]
        return v.bitcast(BF16) if dt == BF16 else v


def proj_stage(cx, R, dr, T, WCOL=WCOL):
    Wm = dr["w_in_mine"]
    Wv = Wm.rearrange("(kc p) c -> p kc c", p=128)
    TL = dr["h2T_seq"].shape[2]
    hseq = dr["h2T_seq"]
    blocks = []
    for g in ("q", "k", "r", "rk", "rv"):
        for b in range(4):
            blocks.append((WCOL[g] + b * 256, ZROW[g] + b * 256, 256, 128))
    blocks.append((WCOL["wl"], ZROW["wl"], 192, 96))
    blocks.append((WCOL["gl"], ZROW["gl"], 256, 128))
    for ti in range(T // TT):
        rk, c0t = (ti * TT) // TL, (ti * TT) % TL
        cx.dma("sp", R.hT[:], hseq[rk, :, c0t:c0t + TT].rearrange("(kc p) t -> p kc t", p=128),
               reads=[dr["_h2T_seq_b"]], writes=R.hTb)
        for bi, (wc0, zr0, ncol, cw) in enumerate(blocks):
            banks = [(bi % 2) * 2, (bi % 2) * 2 + 1]
            for half in range(2):
                wt, wb = R.wring.next()
                wv = wt[:].rearrange("p (k c) -> p k c", c=256)
                cx.dma("pool", wv[:, :, 0:ncol], Wv[:, half * 16:(half + 1) * 16, wc0:wc0 + ncol], writes=[wb])
                for j in range(2):
                    bank, bb = R.banks[banks[j]], R.bankb[banks[j]]
                    for kcl in range(16):
                        kc = half * 16 + kcl
                        cx.pe(lambda e, bank=bank, wv=wv, kcl=kcl, j=j, kc=kc, cw=cw: e.matmul(
                            bank[0:cw, :], wv[:, kcl, j * cw:(j + 1) * cw], R.hT[:, kc, :],
                            start=(kc == 0), stop=(kc == KC - 1)),
                            reads=[wb] + R.hTb, writes=[bb], sig=(kcl == 15))
            for j in range(2):
                bank, bb = R.banks[banks[j]], R.bankb[banks[j]]
                st, stb = R.stage.next()
                if j == 0:
                    cx.act(lambda e, st=st, bank=bank, cw=cw: e.activation(out=st[0:cw, :], in_=bank[0:cw, :], func=AF.Copy),
                           reads=[bb], writes=[stb])
                else:
                    cx.dve(lambda e, st=st, bank=bank, cw=cw: e.tensor_copy(out=st[0:cw, :], in_=bank[0:cw, :]),
                           reads=[bb], writes=[stb])
                cx.dma("sp", dr["zT"][zr0 + j * cw:zr0 + (j + 1) * cw, ti * TT:(ti + 1) * TT], st[0:cw, :],
                       reads=[stb], pwrites=[dr["_zT_b"]])

        def v_evac(cb, ts, bank, bb, ti=ti):
            st, stb = R.tmp.next()
            stv = st[:].bitcast(BF16)[:, 0:512]
            cx.act(lambda e: e.activation(out=stv, in_=bank[:], func=AF.Copy), reads=[bb], writes=[stb])
            cx.dma("sp", dr["vat"][ti * TT + ts * 128:ti * TT + (ts + 1) * 128, cb * 512:(cb + 1) * 512], stv,
                   reads=[stb], pwrites=[dr["_vat_b"]])
        lin_tok_stage(cx, R, R.hT, (lambda k, ts: R.hTb[ts]), KC, Wm[:, WCOL["v"]:WCOL["v"] + 1024], 1024, v_evac)


def attn_setup(cx, R, dr):
    A = Res()
    cv = Carver(R.big)
    T = dr["zT"].shape[1]
    A.T = T
    A.qr = cv.take(T, BF16).rearrange("p (c t) -> p c t", c=2)
    A.kr = cv.take(T, BF16).rearrange("p (c t) -> p c t", c=2)
    A.V = cv.take(T, BF16).rearrange("p (t v) -> p t v", v=256)
    A.masks = cv.take(1024, BF16).rearrange("p (m q) -> p m q", m=4)
    A.ones_bf = cv.take(64, BF16)
    A.ones_f = cv.take(128)
    A.perm = cv.take(128)
    A.P = Ring([cv.take(256, BF16) for _ in range(3)], "P")
    A.od = [cv.take(512) for _ in range(2)]
    A.odb = [Buf("od0"), Buf("od1")]
    A.sq = [cv.take(512) for _ in range(2)]
    A.sqb = [Buf("sq0"), Buf("sq1")]
    A.fin = Ring([cv.take(256, BF16) for _ in range(2)], "fin")
    A.qrb, A.krb, A.Vb, A.cb = Buf("qr"), Buf("kr"), Buf("V"), Buf("aconst")
    cx.dma("sp", A.masks, dr["amask"].rearrange("m p q -> p m q"), writes=[A.cb])
    cx.dma("sp", A.perm, dr["perm"], pwrites=[A.cb])
    cx.pool(lambda e: e.memset(A.ones_bf, 1.0), pwrites=[A.cb])
    cx.pool(lambda e: e.memset(A.ones_f, 1.0), pwrites=[A.cb])
    sm = R.small
    A.smb = Buf("small")
    l4, l4b = R.qring.next()
    cx.dma("sp", l4[:], dr["lam4"][0:1, :].partition_broadcast(128), writes=[l4b])
    cx.dve(lambda e: e.tensor_tensor(out=l4[:, 0:128], in0=l4[:, 0:128], in1=l4[:, 128:256], op=ALU.mult),
           reads=[l4b], writes=[l4b])
    cx.dve(lambda e: e.tensor_tensor(out=l4[:, 256:384], in0=l4[:, 256:384], in1=l4[:, 384:512], op=ALU.mult),
           reads=[l4b], writes=[l4b])
    cx.dve(lambda e: e.tensor_reduce(out=sm[:, 8:9], in_=l4[:, 0:128], axis=AX.X, op=ALU.add), reads=[l4b], writes=[A.smb])
    cx.dve(lambda e: e.tensor_reduce(out=sm[:, 9:10], in_=l4[:, 256:384], axis=AX.X, op=ALU.add), reads=[l4b], writes=[A.smb])
    cx.act(lambda e: e.activation(out=sm[:, 10:12], in_=sm[:, 8:10], func=AF.Exp), reads=[A.smb], writes=[A.smb])
    cx.dve(lambda e: e.tensor_tensor(out=sm[:, 0:1], in0=sm[:, 11:12], in1=sm[:, 10:11], op=ALU.subtract),
           reads=[A.smb], writes=[A.smb])
    cx.dve(lambda e: e.tensor_scalar(out=sm[:, 0:1], in0=sm[:, 0:1], scalar1=float(-LAMBDA_INIT), scalar2=None, op0=ALU.add),
           reads=[A.smb], writes=[A.smb])
    cx.dma("sp", sm[:, 1:3], dr["subln_g"].rearrange("o (m p) -> p (o m)", p=128), writes=[A.smb],
           allow_slow_non_contiguous=True)
    cx.dve(lambda e: e.tensor_scalar(out=sm[:, 1:3], in0=sm[:, 1:3], scalar1=float(1.0 - LAMBDA_INIT), scalar2=None,
                                     op0=ALU.mult), reads=[A.smb], writes=[A.smb])
    A.neg_lam = sm[:, 0:1]
    A.gsub = sm[:, 1:3]
    return A


def attn_head(cx, R, A, dr, h):
    T = A.T
    NTI = T // TT
    zT = dr["zT"]
    for grp, dst, dstb in (("q", A.qr, A.qrb), ("k", A.kr, A.krb)):
        first = True
        for c in range(2):
            r0 = ZROW[grp] + h * 256 + c * 128
            for i in range(NTI):
                ts_ = slice(i * TT, (i + 1) * TT)
                raw, rb = R.qring.next()
                cx.dma("sp", raw[:], zT[r0:r0 + 128, ts_], reads=[dr["_zT_b"]], writes=[rb])
                cs, csb = R.qring.next()
                cx.dma("sp", cs[0:32, :], dr["cosT"][:, ts_], writes=[csb])
                sn, snb = R.qring.next()
                cx.dma("sp", sn[0:32, :], dr["sinT"][:, ts_], writes=[snb])
                bi = i % 2
                bank, bb = R.banks[bi], R.bankb[bi]
                cx.pe(lambda e, bank=bank, raw=raw: e.matmul(bank[:], A.perm, raw[:], start=True, stop=True),
                      reads=[rb, A.cb], writes=[bb])
                wr = dict(writes=[dstb]) if first else dict(pwrites=[dstb])
                first = False
                ordb = Buf("ord")
                wr = dict(writes=[dstb, ordb]) if "writes" in wr else dict(pwrites=[dstb], writes=[ordb])
                cx.act(lambda e, raw=raw, dst=dst, c=c, ts_=ts_: e.activation(out=dst[:, c, ts_], in_=raw[:, :],
                                                                        func=AF.Copy), reads=[rb], **wr)
                cx.dve(lambda e, raw=raw, cs=cs: e.tensor_tensor(out=cs[0:32, :], in0=raw[0:32, :], in1=cs[0:32, :],
                                                             op=ALU.mult), reads=[rb, csb], writes=[csb])
                cx.dve(lambda e, bank=bank, sn=sn: e.tensor_tensor(out=sn[0:32, :], in0=sn[0:32, :], in1=bank[0:32, :],
                                                               op=ALU.mult), reads=[bb, snb], writes=[snb])
                cx.dve(lambda e, cs=cs, sn=sn, dst=dst, c=c, ts_=ts_: e.tensor_tensor(
                    out=dst[0:32, c, ts_], in0=cs[0:32, :], in1=sn[0:32, :], op=ALU.add),
                    reads=[csb, snb, ordb], pwrites=[dstb])
    cx.dma("sp", A.V, dr["vat"][:, h * 256:(h + 1) * 256].rearrange("(t p) v -> p t v", p=128),
           reads=[dr["_vat_b"]], writes=[A.Vb])
    for j in range(NTI):
        qs = slice(j * TT, (j + 1) * TT)
        for c in range(2):
            nkt = 4 * j + 4
            ob = [2, 3, 4] if c == 0 else [5, 6, 7]
            for kt in range(nkt):
                sbi = kt % 2
                sbank, sbb = R.banks[sbi], R.bankb[sbi]
                cx.pe(lambda e, sbank=sbank, c=c, kt=kt, qs=qs: e.matmul(
                    sbank[:], A.kr[:, c, kt * 128:(kt + 1) * 128], A.qr[:, c, qs], start=True, stop=True),
                    reads=[A.krb, A.qrb], writes=[sbb])
                pt, ptb = A.P.next()
                cx.act(lambda e, pt=pt, sbank=sbank: e.activation(out=pt, in_=sbank[:], func=AF.Exp, scale=float(ATT_SCALE)),
                       reads=[sbb], writes=[ptb])
                if kt >= 4 * j:
                    m = kt - 4 * j
                    cx.pool(lambda e, pt=pt, m=m: e.tensor_tensor(out=pt, in0=pt, in1=A.masks[:, m, :], op=ALU.mult),
                            reads=[ptb, A.cb], writes=[ptb])
                for mi, lhs in enumerate((A.V[:, kt, 0:128], A.V[:, kt, 128:256], A.ones_bf)):
                    cx.pe(lambda e, mi=mi, lhs=lhs, pt=pt, kt=kt, nkt=nkt, ob=ob: e.matmul(
                        R.banks[ob[mi]][:], lhs, pt, start=(kt == 0), stop=(kt == nkt - 1)),
                        reads=[A.Vb, A.cb, ptb], writes=[R.bankb[ob[mi]]], sig=(mi == 2))
            rec, recb = R.qring.next()
            cx.dve(lambda e, rec=rec, ob=ob: e.reciprocal(out=rec[:], in_=R.banks[ob[2]][:]),
                   reads=[R.bankb[ob[2]]], writes=[recb])
            for m in range(2):
                if c == 0:
                    cx.dve(lambda e, m=m, rec=rec, ob=ob: e.tensor_tensor(out=A.od[m], in0=R.banks[ob[m]][:], in1=rec[:],
                                                                      op=ALU.mult),
                           reads=[R.bankb[ob[m]], recb], writes=[A.odb[m]])
                else:
                    tmp, tmpb = R.qring.next()
                    cx.dve(lambda e, m=m, rec=rec, ob=ob, tmp=tmp: e.scalar_tensor_tensor(
                        out=tmp[:], in0=R.banks[ob[m]][:], scalar=A.neg_lam, in1=rec[:], op0=ALU.mult, op1=ALU.mult),
                        reads=[R.bankb[ob[m]], recb, A.smb], writes=[tmpb])
                    cx.pool(lambda e, m=m, tmp=tmp: e.tensor_tensor(out=A.od[m], in0=A.od[m], in1=tmp[:], op=ALU.add),
                            reads=[tmpb, A.odb[m]], writes=[A.odb[m]])
        for m in range(2):
            cx.act(lambda e, m=m: e.activation(out=A.sq[m], in_=A.od[m], func=AF.Square), reads=[A.odb[m]], writes=[A.sqb[m]])
        ssb_i = 0
        for m in range(2):
            cx.pe(lambda e, m=m: e.matmul(R.banks[ssb_i][:], A.ones_f, A.sq[m], start=(m == 0), stop=(m == 1)),
                  reads=[A.sqb[m], A.cb], writes=[R.bankb[ssb_i]], sig=(m == 1))
        rs, rsb = R.qring.next()
        cx.act(lambda e, rs=rs: e.activation(out=rs[:], in_=R.banks[ssb_i][:], func=AF.Sqrt, bias=float(RMS_EPS),
                                             scale=1.0 / 256), reads=[R.bankb[ssb_i]], writes=[rsb])
        cx.dve(lambda e, rs=rs: e.reciprocal(out=rs[:], in_=rs[:]), reads=[rsb], writes=[rsb])
        for m in range(2):
            fin, finb = A.fin.next()
            cx.dve(lambda e, m=m, rs=rs, fin=fin: e.scalar_tensor_tensor(
                out=fin, in0=A.od[m], scalar=A.gsub[:, m:m + 1], in1=rs[:], op0=ALU.mult, op1=ALU.mult),
                reads=[A.odb[m], rsb, A.smb], writes=[finb])
            cx.dma("sp", dr["oT_loc"][h * 256 + m * 128:h * 256 + (m + 1) * 128, qs], fin, reads=[finb],
                   pwrites=[dr["_oT_b"]])


def host_consts(T):
    import ml_dtypes
    c = {}
    c["ident"] = np.eye(128, dtype=np.float32)
    inv = (500000.0 ** (-np.arange(0, 32, 2, dtype=np.float32) / 32)).astype(np.float32)
    ang = np.arange(T, dtype=np.float32)[None, :] * np.concatenate([inv, inv])[:, None]
    c["cosT"] = np.cos(ang).astype(np.float32)
    c["sinT"] = np.sin(ang).astype(np.float32)
    perm = np.zeros((128, 128), np.float32)
    for d in range(16):
        perm[d + 16, d] = -1.0
        perm[d, d + 16] = 1.0
    c["perm"] = perm
    k = np.arange(128)[None, :, None]
    q = np.arange(512)[None, None, :]
    m = np.arange(4)[:, None, None]
    c["amask"] = (((m * 128 + k) // 64) <= (q // 64)).astype(np.float32).astype(ml_dtypes.bfloat16)
    p = np.arange(128)[:, None] % 64
    f = np.arange(512)[None, :] % 64
    rm = np.zeros((5, 128, 512), np.float32)
    rm[0] = np.broadcast_to((f != 0), (128, 512))
    rm[1] = (p < f)
    rm[2] = (p <= f)
    rm[3] = (f < p)
    rm[4] = (p == f)
    c["rmask"] = rm
    pp = np.arange(128)
    c["bones"] = (pp[:, None] // 64 == pp[None, :] // 64).astype(np.float32)
    return c


def declare_b_inputs(nc, dr, T, with_w=False):
    ext = lambda name, shape, dt=F32: nc.dram_tensor(name, shape, dt, kind="ExternalInput").ap()
    if with_w:
        dr["w_in_mine"] = ext("w_in_mine", [D, NWC])
    dr["lam4"] = ext("lam4", [1, 512])
    dr["subln_g"] = ext("subln_g", [1, 256])
    dr["cosT"] = ext("cosT", [32, T])
    dr["sinT"] = ext("sinT", [32, T])
    dr["perm"] = ext("perm", [128, 128])
    dr["amask"] = ext("amask", [4, 128, 512], BF16)
    dr["rmask"] = ext("rmask", [5, 128, 512])
    dr["bones"] = ext("bones", [128, 128])
    dr["pcols"] = ext("pcols", [8, 1024])
    dr["pcl"] = ext("pcl", [4, 128])
    dr["w2m"] = ext("w2m", [96, 1024])
    dr["a2m"] = ext("a2m", [96, 1024])
    dr["g2m"] = ext("g2m", [256, 1024])
    dr["lnw_t"] = ext("lnw_t", [2, 512])
    dr["lnb_t"] = ext("lnb_t", [2, 512])


def build_b_test(T, upto=9):
    nc = bass.Bass("TRN2", target_bir_lowering=False)
    dr = {}
    dr["ident"] = nc.dram_tensor("ident", [128, 128], F32, kind="ExternalInput").ap()
    dr["h2T_seq"] = nc.dram_tensor("h2T_seq", [1, D, T], BF16, kind="ExternalInput").ap()
    declare_b_inputs(nc, dr, T, with_w=True)
    dr["zT"] = nc.dram_tensor("zT", [NZ, T], F32, kind="ExternalOutput").ap()
    dr["vat"] = nc.dram_tensor("vat", [T, 1024], BF16, kind="Internal").ap()
    dr["oT_loc"] = nc.dram_tensor("oT_loc", [2048, T], BF16, kind="ExternalOutput").ap()
    for k in ("h2T_seq", "zT", "vat", "oT"):
        dr["_%s_b" % k] = Buf(k)
    cx = Cx(nc)
    with ExitStack() as es:
        R = alloc_common(nc, es, cx)
        load_consts(cx, R, dr)
        proj_stage(cx, R, dr, T)
        if upto >= 2 and upto != 3:
            A = attn_setup(cx, R, dr)
            for h in range(4):
                attn_head(cx, R, A, dr, h)
        if upto >= 3:
            cx.barrier()
            Wk = rwkv_setup(cx, R, dr)
            for ti in range(T // TT):
                rwkv_tile(cx, R, Wk, dr, ti)
        cx.final_wait("sp", [dr["_oT_b"], dr["_zT_b"]])
        cx.emit(es)
    return nc, cx


RW_LN_EPS = 64e-5
PC_MU_R, PC_MU_K, PC_MU_V, PC_W0, PC_A0, PC_KK, PC_KA, PC_RK = range(8)


def rwkv_setup(cx, R, dr):
    W = Res()
    cv = Carver(R.big)
    T = dr["zT"].shape[1]
    W.T = T
    f32t = lambda: cv.take(512)
    bft = lambda: cv.take(256, BF16)
    W.names = {}
    for n in ("xr", "xk", "xv", "t1", "EW", "A", "kk", "kmod", "lg", "eg", "egi", "egm", "y1", "y2", "y3"):
        setattr(W, n, f32t())
        setattr(W, n + "b", Buf(n))
    W.raw = Ring([cv.take(516) for _ in range(2)], "raw")
    for n in ("Rt", "Kt", "Bt", "At", "RKt", "Q", "N", "Q2", "N2", "P", "Pp", "MakT", "MrbT", "MrkT",
              "Ktm", "Btm", "Vtm", "otm", "oT", "twl", "tal"):
        setattr(W, n, bft())
        setattr(W, n + "b", Buf(n))
    W.sgl = cv.take(512, BF16).rearrange("p (k t) -> p k t", k=2)
    W.sglb = Buf("sgl")
    W.x1 = Ring([cv.take(32, BF16) for _ in range(2)], "x1")
    W.u = Ring([cv.take(32, BF16) for _ in range(2)], "u")
    W.ht = Ring([cv.take(64) for _ in range(2)], "ht")
    W.Hf = cv.take(512)
    W.Hb = cv.take(256, BF16)
    W.Hst = cv.take(288, BF16).rearrange("p (c v) -> p c v", v=64)
    W.Hstb = [Buf("Hst%d" % i) for i in range(9)]
    W.Hbuf = [Buf("H%d" % i) for i in range(8)]
    W.PC = cv.take(64).rearrange("p (i c) -> p i c", c=8)
    W.PL = cv.take(8)
    W.negw0 = cv.take(8)
    W.w2 = cv.take(512, BF16)
    W.a2 = cv.take(512, BF16)
    W.g2 = cv.take(1024, BF16).rearrange("p (k c) -> p k c", k=2)
    W.lnw2 = cv.take(512)
    W.lnb2 = cv.take(512)
    W.lnw = W.lnw2.rearrange("p (a v) -> p a v", v=64)
    W.lnb = W.lnb2.rearrange("p (a v) -> p a v", v=64)
    W.masks = cv.take(2560).rearrange("p (m f) -> p m f", m=5)
    W.bones = cv.take(64, BF16)
    W.identb = cv.take(64, BF16)
    W.onesc = cv.take(4, BF16)
    W.s8 = Ring([cv.take(8) for _ in range(6)], "s8")
    W.cb = Buf("rconst")
    cx.dma("sp", W.masks, dr["rmask"].rearrange("m p f -> p m f"), writes=[W.cb])
    cx.dma("pool", W.bones, dr["bones"], pwrites=[W.cb])
    cx.dma("pool", W.identb, dr["ident"], pwrites=[W.cb])
    cx.dma("pool", W.w2[0:96, :], dr["w2m"], pwrites=[W.cb])
    cx.dma("pool", W.a2[0:96, :], dr["a2m"], pwrites=[W.cb])
    cx.dma("pool", W.g2, dr["g2m"].rearrange("(k p) c -> p k c", p=128), pwrites=[W.cb])
    cx.pool(lambda e: e.memset(W.onesc, 1.0), pwrites=[W.cb])
    for hp in range(2):
        cx.dma("sp", W.lnw2[hp * 64:(hp + 1) * 64, :], dr["lnw_t"][hp:hp + 1, :].partition_broadcast(64), pwrites=[W.cb])
        cx.dma("sp", W.lnb2[hp * 64:(hp + 1) * 64, :], dr["lnb_t"][hp:hp + 1, :].partition_broadcast(64), pwrites=[W.cb])
    t, tb = R.qring.next()
    cx.dma("sp", t[0:64, 0:128], dr["pcols"].rearrange("i (c p) -> (i c) p", p=128), writes=[tb])
    cx.pe(lambda e: e.transpose(out=R.banks[0][:, 0:64], in_=t[0:64, 0:128], identity=R.ident[0:64, 0:64]),
          reads=[tb, R.identb], writes=[R.bankb[0]])
    cx.dve(lambda e: e.tensor_copy(out=W.PC, in_=R.banks[0][:, 0:64].rearrange("p (i c) -> p i c", c=8)),
           reads=[R.bankb[0]], pwrites=[W.cb])
    cx.dve(lambda e: e.tensor_scalar(out=W.negw0, in0=R.banks[0][:, PC_W0 * 8:PC_W0 * 8 + 8], scalar1=-1.0, scalar2=None,
                                     op0=ALU.mult), reads=[R.bankb[0]], pwrites=[W.cb])
    t2, t2b = R.qring.next()
    cx.dma("sp", t2[0:4, 0:128], dr["pcl"], writes=[t2b])
    cx.pe(lambda e: e.transpose(out=R.banks[1][:, 0:4], in_=t2[0:4, 0:128], identity=R.ident[0:4, 0:4]),
          reads=[t2b, R.identb], writes=[R.bankb[1]])
    cx.dve(lambda e: e.tensor_copy(out=W.PL[:, 0:4], in_=R.banks[1][:, 0:4]), reads=[R.bankb[1]], pwrites=[W.cb])
    cx.dve(lambda e: e.memset(W.Hf, 0.0), writes=W.Hbuf)
    cx.dve(lambda e: e.memset(W.Hb, 0.0), pwrites=W.Hbuf)
    return W


def rwkv_tile(cx, R, W, dr, ti):
    zT = dr["zT"]
    t0 = ti * TT
    MSK_RESET, MSK_U, MSK_UD, MSK_L = 0, 1, 2, 3
    cb = W.cb

    def shifted(row0, nrow, mu_col, out, outb, extra_reads=()):
        raw, rb = W.raw.next()
        if ti == 0:
            cx.dve(lambda e: e.memset(raw[0:nrow, 0:1], 0.0), writes=[rb])
            cx.dma("sp", raw[0:nrow, 1:513], zT[row0:row0 + nrow, 0:TT], reads=[dr["_zT_b"], rb], pwrites=[rb])
        else:
            cx.dma("sp", raw[0:nrow, 0:513], zT[row0:row0 + nrow, t0 - 1:t0 + TT], reads=[dr["_zT_b"]], writes=[rb])
        d, db = R.qring.next()
        cx.dve(lambda e: e.tensor_tensor(out=d[0:nrow, :], in0=raw[0:nrow, 0:512], in1=raw[0:nrow, 1:513], op=ALU.subtract),
               reads=[rb], writes=[db])
        cx.dve(lambda e: e.scalar_tensor_tensor(out=out, in0=d[0:nrow, :], scalar=mu_col, in1=raw[0:nrow, 1:513],
                                                op0=ALU.mult, op1=ALU.add),
               reads=[db, rb, cb] + list(extra_reads), writes=[outb])

    tl, tlb = R.qring.next()
    shifted(ZROW["wl"], 96, W.PL[0:96, 0:1], tl[0:96, :], tlb)
    cx.act(lambda e: e.activation(out=W.twl[0:96, :], in_=tl[0:96, :], func=AF.Tanh), reads=[tlb], writes=[W.twlb])
    tl2, tl2b = R.qring.next()
    shifted(ZROW["al"], 96, W.PL[0:96, 1:2], tl2[0:96, :], tl2b)
    cx.act(lambda e: e.activation(out=W.tal[0:96, :], in_=tl2[0:96, :], func=AF.Copy), reads=[tl2b], writes=[W.talb])
    for k in range(2):
        tg, tgb = R.qring.next()
        shifted(ZROW["gl"] + k * 128, 128, W.PL[:, 2 + k:3 + k], tg[:], tgb)
        kw = dict(writes=[W.sglb]) if k == 0 else dict(pwrites=[W.sglb])
        cx.act(lambda e, k=k, tg=tg: e.activation(out=W.sgl[:, k, :], in_=tg[:], func=AF.Sigmoid), reads=[tgb], **kw)

    for cc in range(8):
        col = lambda i: W.PC[:, i, cc:cc + 1]
        shifted(ZROW["r"] + cc * 128, 128, col(PC_MU_R), W.xr, W.xrb)
        shifted(ZROW["rk"] + cc * 128, 128, col(PC_MU_K), W.xk, W.xkb)
        shifted(ZROW["rv"] + cc * 128, 128, col(PC_MU_V), W.xv, W.xvb)
        b0, b1 = R.banks[0], R.banks[1]
        cx.pe(lambda e, cc=cc: e.matmul(b0[:], W.w2[0:96, cc * 128:(cc + 1) * 128], W.twl[0:96, :], start=True, stop=True),
              reads=[cb, W.twlb], writes=[R.bankb[0]])
        cx.dve(lambda e, cc=cc: e.tensor_scalar(out=W.t1, in0=b0[:], scalar1=W.PC[:, PC_W0, cc:cc + 1], scalar2=None,
                                                op0=ALU.add), reads=[R.bankb[0], cb], writes=[W.t1b])
        cx.act(lambda e: e.activation(out=W.t1, in_=W.t1, func=AF.Exp, scale=-1.0), reads=[W.t1b], writes=[W.t1b])
        cx.act(lambda e: e.activation(out=W.t1, in_=W.t1, func=AF.Ln, bias=1.0), reads=[W.t1b], writes=[W.t1b])
        cx.act(lambda e: e.activation(out=W.EW, in_=W.t1, func=AF.Exp, scale=-1.0, bias=-0.5), reads=[W.t1b], writes=[W.EWb])
        cx.pe(lambda e, cc=cc: e.matmul(b1[:], W.a2[0:96, cc * 128:(cc + 1) * 128], W.tal[0:96, :], start=True, stop=True),
              reads=[cb, W.talb], writes=[R.bankb[1]])
        cx.dve(lambda e, cc=cc: e.tensor_scalar(out=W.A, in0=b1[:], scalar1=W.PC[:, PC_A0, cc:cc + 1], scalar2=None,
                                                op0=ALU.add), reads=[R.bankb[1], cb], writes=[W.Ab])
        cx.act(lambda e: e.activation(out=W.A, in_=W.A, func=AF.Sigmoid), reads=[W.Ab], writes=[W.Ab])
        cx.dve(lambda e, cc=cc: e.tensor_scalar(out=W.kk, in0=W.xk, scalar1=W.PC[:, PC_KK, cc:cc + 1], scalar2=None,
                                                op0=ALU.mult), reads=[W.xkb, cb], writes=[W.kkb])
        cx.act(lambda e: e.activation(out=W.Rt, in_=W.kk, func=AF.Square), reads=[W.kkb], writes=[W.Rtb])
        cx.pe(lambda e: e.matmul(b0[:], W.bones, W.Rt, start=True, stop=True), reads=[cb, W.Rtb], writes=[R.bankb[0]])
        cx.act(lambda e: e.activation(out=W.t1, in_=b0[:], func=AF.Sqrt), reads=[R.bankb[0]], writes=[W.t1b])
        cx.dve(lambda e: e.tensor_scalar(out=W.t1, in0=W.t1, scalar1=1e-12, scalar2=None, op0=ALU.max),
               reads=[W.t1b], writes=[W.t1b])
        cx.dve(lambda e: e.reciprocal(out=W.t1, in_=W.t1), reads=[W.t1b], writes=[W.t1b])
        cx.dve(lambda e: e.tensor_tensor(out=W.kk, in0=W.kk, in1=W.t1, op=ALU.mult), reads=[W.kkb, W.t1b], writes=[W.kkb])
        cx.dve(lambda e, cc=cc: e.tensor_scalar(out=W.kmod, in0=W.A, scalar1=-1.0, scalar2=W.PC[:, PC_KA, cc:cc + 1],
                                                op0=ALU.add, op1=ALU.mult), reads=[W.Ab, cb], writes=[W.kmodb])
        cx.dve(lambda e: e.scalar_tensor_tensor(out=W.kmod, in0=W.kmod, scalar=1.0, in1=W.xk, op0=ALU.add, op1=ALU.mult),
               reads=[W.kmodb, W.xkb], writes=[W.kmodb])
        cx.dve(lambda e: e.tensor_tensor_scan(out=W.lg, data0=W.masks[:, MSK_RESET, :], data1=W.EW, initial=0.0,
                                              op0=ALU.mult, op1=ALU.subtract), reads=[W.EWb, cb], writes=[W.lgb])
        cx.act(lambda e: e.activation(out=W.eg, in_=W.lg, func=AF.Exp), reads=[W.lgb], writes=[W.egb])
        cx.act(lambda e: e.activation(out=W.egi, in_=W.lg, func=AF.Exp, scale=-1.0), reads=[W.lgb], writes=[W.egib])
        cx.dve(lambda e: e.tensor_tensor(out=W.egm, in0=W.lg, in1=W.EW, op=ALU.add), reads=[W.lgb, W.EWb], writes=[W.egmb])
        cx.act(lambda e: e.activation(out=W.egm, in_=W.egm, func=AF.Exp), reads=[W.egmb], writes=[W.egmb])
        cx.dve(lambda e: e.tensor_tensor(out=W.Rt, in0=W.xr, in1=W.eg, op=ALU.mult), reads=[W.xrb, W.egb], writes=[W.Rtb])
        cx.dve(lambda e: e.tensor_tensor(out=W.Kt, in0=W.kmod, in1=W.egi, op=ALU.mult), reads=[W.kmodb, W.egib], writes=[W.Ktb])
        cx.dve(lambda e: e.tensor_tensor(out=W.t1, in0=W.kk, in1=W.A, op=ALU.mult), reads=[W.kkb, W.Ab], writes=[W.t1b])
        cx.dve(lambda e: e.tensor_tensor(out=W.Bt, in0=W.t1, in1=W.egi, op=ALU.mult), reads=[W.t1b, W.egib], writes=[W.Btb])
        cx.dve(lambda e: e.scalar_tensor_tensor(out=W.At, in0=W.kk, scalar=-1.0, in1=W.egm, op0=ALU.mult, op1=ALU.mult),
               reads=[W.kkb, W.egmb], writes=[W.Atb])
        cx.dve(lambda e, cc=cc: e.scalar_tensor_tensor(out=W.RKt, in0=W.xr, scalar=W.PC[:, PC_RK, cc:cc + 1], in1=W.kmod,
                                                       op0=ALU.mult, op1=ALU.mult),
               reads=[W.xrb, W.kmodb, cb], writes=[W.RKtb])
        cx.act(lambda e: e.activation(out=W.oT, in_=W.xv, func=AF.Copy), reads=[W.xvb], writes=[W.oTb])

        blk = lambda ap, hp, c: ap[hp * 64:(hp + 1) * 64, c * 64:(c + 1) * 64]

        def blockmm(bank_i, lhs, lhsb, rhs, rhsb):
            bank, bb = R.banks[bank_i], R.bankb[bank_i]
            n = 0
            for hp in range(2):
                for c in range(8):
                    n += 1
                    cx.pe(lambda e, hp=hp, c=c: e.matmul(blk(bank, hp, c), blk(lhs, hp, c), blk(rhs, hp, c),
                                                         start=True, stop=True),
                          reads=[lhsb, rhsb], writes=[bb], sig=(n == 16))
            return bank, bb

        def scoremm(bank_i, lhs, lhsb, rhs, rhsb, mask_i, out, outb, eng):
            bank, bb = blockmm(bank_i, lhs, lhsb, rhs, rhsb)
            cx.dve(lambda e: e.tensor_tensor(out=out, in0=bank[:], in1=W.masks[:, mask_i, :], op=ALU.mult),
                   reads=[bb, cb], writes=[outb])

        scoremm(2, W.Bt, W.Btb, W.At, W.Atb, MSK_U, W.Q, W.Qb, "dve")
        scoremm(3, W.At, W.Atb, W.Bt, W.Btb, MSK_L, W.N, W.Nb, "dve")
        scoremm(4, W.Kt, W.Ktb, W.At, W.Atb, MSK_U, W.MakT, W.MakTb, "dve")
        scoremm(5, W.Bt, W.Btb, W.Rt, W.Rtb, MSK_UD, W.MrbT, W.MrbTb, "dve")
        scoremm(6, W.Kt, W.Ktb, W.Rt, W.Rtb, MSK_UD, W.MrkT, W.MrkTb, "dve")
        for (dst, dstb, src, srcb) in ((W.P, W.Pb, W.Q, W.Qb), (W.Pp, W.Ppb, W.N, W.Nb)):
            cx.dve(lambda e, dst=dst, src=src: e.tensor_tensor(out=dst, in0=src, in1=W.masks[:, 4, :], op=ALU.add),
                   reads=[srcb, cb], writes=[dstb])
        Qc, Qcb, Nc, Ncb = W.Q, W.Qb, W.N, W.Nb
        Qn, Qnb, Nn, Nnb = W.Q2, W.Q2b, W.N2, W.N2b
        for lvl in range(5):
            last = lvl == 4
            bk, bkb = blockmm(2, Nc, Ncb, Qc, Qcb)
            cx.act(lambda e, bk=bk, Qn=Qn: e.activation(out=Qn, in_=bk[:], func=AF.Copy), reads=[bkb], writes=[Qnb])
            if not last:
                bk2, bk2b = blockmm(3, Qc, Qcb, Nc, Ncb)
                cx.act(lambda e, bk2=bk2, Nn=Nn: e.activation(out=Nn, in_=bk2[:], func=AF.Copy), reads=[bk2b], writes=[Nnb])
            bp, bpb = blockmm(4, W.Pp, W.Ppb, Qn, Qnb)
            if not last:
                bq, bqb = blockmm(5, W.P, W.Pb, Nn, Nnb)
            cx.dve(lambda e, bp=bp: e.tensor_tensor(out=W.P, in0=bp[:], in1=W.P, op=ALU.add), reads=[bpb, W.Pb], writes=[W.Pb])
            if not last:
                cx.dve(lambda e, bq=bq: e.tensor_tensor(out=W.Pp, in0=bq[:], in1=W.Pp, op=ALU.add),
                       reads=[bqb, W.Ppb], writes=[W.Ppb])
            Qc, Qcb, Nc, Ncb, Qn, Qnb, Nn, Nnb = Qn, Qnb, Nn, Nnb, Qc, Qcb, Nc, Ncb
        c3 = lambda ap: ap.rearrange("p (c j) -> p c j", j=64)
        egc = c3(W.eg)[:, :, 63:64]
        for tl_, tlb_ in ((W.Bt, W.Btb), (W.Kt, W.Ktb)):
            cx.dve(lambda e, tl_=tl_: e.tensor_tensor(out=c3(tl_), in0=c3(tl_), in1=egc.to_broadcast([128, 8, 64]), op=ALU.mult),
                   reads=[tlb_, W.egb], writes=[tlb_])
        Atm, Atmb = W.Q, W.Qb
        for src, srcb, dst, dstb, bi in ((W.Kt, W.Ktb, W.Ktm, W.Ktmb, 6), (W.Bt, W.Btb, W.Btm, W.Btmb, 7),
                                         (W.oT, W.oTb, W.Vtm, W.Vtmb, 6), (W.At, W.Atb, Atm, Atmb, 7)):
            bank, bb = R.banks[bi], R.bankb[bi]
            bv = bank[:].bitcast(BF16)
            n = 0
            for hp in range(2):
                for c in range(8):
                    n += 1
                    cx.pe(lambda e, hp=hp, c=c, bv=bv, src=src: e.transpose(
                        out=blk(bv, hp, c), in_=blk(src, hp, c), identity=W.identb[hp * 64:(hp + 1) * 64, hp * 64:(hp + 1) * 64]),
                        reads=[srcb, cb], writes=[bb], sig=(n == 16))
            cx.act(lambda e, bv=bv, dst=dst: e.activation(out=dst, in_=bv[:, 0:512], func=AF.Copy), reads=[bb], writes=[dstb])
        Wa, Wab = W.N, W.Nb
        McT, McTb = W.Pp, W.Ppb
        X0, X0b = W.Q2, W.Q2b
        Uv, Uvb = W.N2, W.N2b
        bk, bkb = blockmm(2, W.P, W.Pb, Atm, Atmb)
        cx.act(lambda e, bk=bk: e.activation(out=Wa, in_=bk[:], func=AF.Copy), reads=[bkb], writes=[Wab])
        bk, bkb = blockmm(3, W.MakT, W.MakTb, W.Vtm, W.Vtmb)
        cx.dve(lambda e, bk=bk: e.tensor_copy(out=X0, in_=bk[:]), reads=[bkb], writes=[X0b])
        bk, bkb = blockmm(4, Wa, Wab, W.Btm, W.Btmb)
        cx.dve(lambda e: e.tensor_tensor(out=c3(W.y3), in0=c3(W.masks[:, 4, :]), in1=egc.to_broadcast([128, 8, 64]), op=ALU.mult),
               reads=[cb, W.egb], writes=[W.y3b])
        cx.dve(lambda e, bk=bk: e.tensor_tensor(out=McT, in0=bk[:], in1=W.y3, op=ALU.add), reads=[bkb, W.y3b], writes=[McTb])
        bk, bkb = blockmm(5, W.P, W.Pb, X0, X0b)
        cx.act(lambda e, bk=bk: e.activation(out=Uv, in_=bk[:], func=AF.Copy), reads=[bkb], writes=[Uvb])
        bk, bkb = blockmm(2, Wa, Wab, W.MrbT, W.MrbTb)
        cx.dve(lambda e, bk=bk: e.tensor_tensor(out=W.Rt, in0=bk[:], in1=W.Rt, op=ALU.add), reads=[bkb, W.Rtb], writes=[W.Rtb])
        bD, bDb = R.banks[3], R.bankb[3]
        n = 0
        for hp in range(2):
            for c in range(8):
                n += 1
                cx.pe(lambda e, hp=hp, c=c: e.matmul(blk(bD, hp, c), blk(W.Btm, hp, c), blk(Uv, hp, c), start=True, stop=False),
                      reads=[W.Btmb, Uvb], writes=[bDb], sig=False)
                cx.pe(lambda e, hp=hp, c=c: e.matmul(blk(bD, hp, c), blk(W.Ktm, hp, c), blk(W.Vtm, hp, c), start=False, stop=True),
                      reads=[W.Ktmb, W.Vtmb], writes=[bDb], sig=(n == 16))
        cx.act(lambda e: e.activation(out=W.y3, in_=bD[:], func=AF.Copy), reads=[bDb], writes=[W.y3b])
        bS, bSb = R.banks[7], R.bankb[7]
        n = 0
        for hp in range(2):
            for c in range(8):
                n += 1
                cx.pe(lambda e, hp=hp, c=c: e.matmul(bS[hp * 64:(hp + 1) * 64, 256 + c:257 + c], blk(W.RKt, hp, c),
                                                     W.onesc[hp * 64:(hp + 1) * 64, 0:1], start=True, stop=True),
                      reads=[W.RKtb, cb], writes=[bSb], sig=(n == 16))
        rk8, rk8b = W.s8.next()
        cx.dve(lambda e, rk8=rk8: e.tensor_copy(out=rk8, in_=bS[:, 256:264]), reads=[bSb], writes=[rk8b])
        bG, bGb = R.banks[1], R.bankb[1]
        n = 0
        for hp in range(2):
            for c in range(8):
                for k in range(2):
                    n += 1
                    cx.pe(lambda e, hp=hp, c=c, k=k, cc=cc: e.matmul(
                        blk(bG, hp, c), W.sgl[:, k, c * 64:(c + 1) * 64],
                        W.g2[:, k, cc * 128 + hp * 64:cc * 128 + (hp + 1) * 64], start=(k == 0), stop=(k == 1)),
                        reads=[W.sglb, cb], writes=[bGb], sig=(n == 32))
        bY, bYb = R.banks[0], R.bankb[0]
        Hbf = W.Hbuf[cc]
        Hcol = slice(cc * 64, (cc + 1) * 64)
        cx.dve(lambda e, Hcol=Hcol: e.tensor_copy(out=W.Hst[:, 0, :], in_=W.Hb[:, Hcol]), reads=[Hbf], writes=[W.Hstb[0]])
        bH, bHb = R.banks[4], R.bankb[4]
        for c in range(8):
            for hp in range(2):
                rows = slice(hp * 64, (hp + 1) * 64)
                cx.pe(lambda e, rows=rows, hp=hp, c=c: e.matmul(blk(bH, hp, c), blk(McT, hp, c), W.Hst[rows, c, :],
                                                             start=True, stop=True),
                      reads=[McTb, W.Hstb[c]], writes=[bHb], sig=(hp == 1))
            cx.dve(lambda e, c=c: e.tensor_tensor(out=W.Hst[:, c + 1, :], in0=bH[:, c * 64:(c + 1) * 64],
                                                  in1=W.y3[:, c * 64:(c + 1) * 64], op=ALU.add),
                   reads=[bHb, W.y3b], writes=[W.Hstb[c + 1]])
        cx.act(lambda e, Hcol=Hcol: e.activation(out=W.Hb[:, Hcol], in_=W.Hst[:, 8, :], func=AF.Copy),
               reads=[W.Hstb[8]], writes=[Hbf])
        n = 0
        for c in range(8):
            for hp in range(2):
                rows = slice(hp * 64, (hp + 1) * 64)
                n += 1
                cx.pe(lambda e, rows=rows, hp=hp, c=c: e.matmul(blk(bY, hp, c), blk(W.Rt, hp, c), W.Hst[rows, c, :],
                                                             start=True, stop=False), reads=[W.Rtb, W.Hstb[c]], writes=[bYb], sig=False)
                cx.pe(lambda e, hp=hp, c=c: e.matmul(blk(bY, hp, c), blk(W.MrbT, hp, c), blk(Uv, hp, c),
                                                     start=False, stop=False), reads=[W.MrbTb, Uvb], writes=[bYb], sig=False)
                cx.pe(lambda e, hp=hp, c=c: e.matmul(blk(bY, hp, c), blk(W.MrkT, hp, c), blk(W.Vtm, hp, c),
                                                     start=False, stop=True), reads=[W.MrkTb, W.Vtmb], writes=[bYb], sig=(n == 16))
        v3 = lambda ap: ap.rearrange("p (c v) -> p c v", v=64)
        m8, m8b = W.s8.next()
        cx.dve(lambda e, m8=m8: e.tensor_reduce(out=m8, in_=v3(bY[:]), axis=AX.X, op=ALU.add), reads=[bYb], writes=[m8b])
        cx.dve(lambda e, m8=m8: e.tensor_scalar(out=m8, in0=m8, scalar1=-1.0 / 64, scalar2=None, op0=ALU.mult), reads=[m8b], writes=[m8b])
        cx.dve(lambda e, m8=m8: e.tensor_tensor(out=v3(W.y1), in0=v3(bY[:]), in1=m8.unsqueeze(2).to_broadcast([128, 8, 64]),
                                         op=ALU.add), reads=[bYb, m8b], writes=[W.y1b])
        cx.act(lambda e: e.activation(out=W.y2, in_=W.y1, func=AF.Square), reads=[W.y1b], writes=[W.y2b])
        v8, v8b = W.s8.next()
        cx.dve(lambda e, v8=v8: e.tensor_reduce(out=v8, in_=v3(W.y2), axis=AX.X, op=ALU.add), reads=[W.y2b], writes=[v8b])
        cx.act(lambda e, v8=v8: e.activation(out=v8, in_=v8, func=AF.Sqrt, scale=1.0 / 64, bias=float(RW_LN_EPS)), reads=[v8b], writes=[v8b])
        cx.dve(lambda e, v8=v8: e.reciprocal(out=v8, in_=v8), reads=[v8b], writes=[v8b])
        cx.dve(lambda e, v8=v8: e.tensor_tensor(out=v3(W.y1), in0=v3(W.y1), in1=v8.unsqueeze(2).to_broadcast([128, 8, 64]), op=ALU.mult),
               reads=[W.y1b, v8b], writes=[W.y1b])
        cx.dve(lambda e, cc=cc: e.tensor_tensor(out=v3(W.y1), in0=v3(W.y1),
                                                in1=W.lnw[:, cc:cc + 1, :].to_broadcast([128, 8, 64]), op=ALU.mult),
               reads=[W.y1b, cb], writes=[W.y1b])
        cx.dve(lambda e, cc=cc: e.tensor_tensor(out=v3(W.y1), in0=v3(W.y1),
                                                in1=W.lnb[:, cc:cc + 1, :].to_broadcast([128, 8, 64]), op=ALU.add),
               reads=[W.y1b, cb], writes=[W.y1b])
        cx.dve(lambda e, rk8=rk8: e.tensor_tensor(out=v3(W.y2), in0=v3(W.Vtm), in1=rk8.unsqueeze(2).to_broadcast([128, 8, 64]),
                                         op=ALU.mult), reads=[W.Vtmb, rk8b], writes=[W.y2b])
        cx.dve(lambda e: e.tensor_tensor(out=W.y1, in0=W.y1, in1=W.y2, op=ALU.add), reads=[W.y1b, W.y2b], writes=[W.y1b])
        cx.dve(lambda e: e.tensor_tensor(out=W.otm, in0=W.y1, in1=bG[:], op=ALU.mult), reads=[W.y1b, bGb], writes=[W.otmb])
        bank, bb = R.banks[5], R.bankb[5]
        bv = bank[:].bitcast(BF16)
        n = 0
        for hp in range(2):
            for c in range(8):
                n += 1
                cx.pe(lambda e, hp=hp, c=c, bv=bv: e.transpose(
                    out=blk(bv, hp, c), in_=blk(W.otm, hp, c), identity=W.identb[hp * 64:(hp + 1) * 64, hp * 64:(hp + 1) * 64]),
                    reads=[W.otmb, cb], writes=[bb], sig=(n == 16))
        cx.act(lambda e, bv=bv: e.activation(out=W.N2, in_=bv[:, 0:512], func=AF.Copy), reads=[bb], writes=[W.N2b])
        cx.dma("sp", dr["oT_loc"][1024 + cc * 128:1024 + (cc + 1) * 128, t0:t0 + TT], W.N2, reads=[W.N2b],
               pwrites=[dr["_oT_b"]])


from concourse.bass import ds

SEQ = 4096
IN_PROJ = 12736
WALL = ["ffn1_w_gate", "ffn1_w_up", "ffn1_w_down", "w_out", "ffn2_w_gate", "ffn2_w_up", "ffn2_w_down",
        "ple_w_gate", "ple_w_proj"]
WSHAPE["w_out"] = (D, D)
GALL = ["ffn1_pre_g", "ffn1_post_g", "mix_pre_g", "mix_post_g", "ffn2_pre_g", "ffn2_post_g", "ple_pre_g", "ple_post_g"]
WIN_GROUPS = [(0, WCOL["q"]), (2048, WCOL["k"]), (4096, WCOL["v"]), (6144, WCOL["r"]), (8192, WCOL["rk"]),
              (10240, WCOL["rv"])]
WIN_LORA0 = 12288


def wout_rowmap(k0):
    return {0: 0, 8: 16, 16: 8, 24: 24}[k0]


def build_mega_cc(ncores=NCORES):
    ntok = NTOK
    T = SEQ
    nc = bass.Bass("TRN2", target_bir_lowering=False)
    ext = lambda name, shape, dt=F32: nc.dram_tensor(name, shape, dt, kind="ExternalInput").ap()
    loc = lambda name, shape, dt=F32: nc.dram_tensor(name, shape, dt, kind="Internal").ap()
    dr = {}
    dr["x"] = ext("x", [ntok, D])
    dr["p"] = ext("p", [ntok, 256])
    dr["ident"] = ext("ident", [128, 128])
    for g in GALL:
        dr[g] = ext(g, [1, D])
    shards = {}
    for w in WALL:
        r, c = WSHAPE[w]
        shards[w] = ext(w + "_sh", [r // ncores, c])
    win_sh = [ext("w_in_sh%d" % i, [2048 // ncores, IN_PROJ]) for i in range(2)]
    declare_b_inputs(nc, dr, T)
    out = nc.dram_tensor("out", [ntok, D], F32, kind="ExternalOutput").ap()
    x1, x2, x3 = loc("x1", [ntok, D]), loc("x2", [ntok, D]), loc("x3", [ntok, D])
    fscr = loc("fscr", [TT, D])
    h2T_loc = loc("h2T_loc", [D, ntok], BF16)
    dr["h2T_seq"] = loc("h2T_seq", [2, D, ntok], BF16)
    dr["zT"] = loc("zT", [NZ, T])
    dr["vat"] = loc("vat", [T, 1024], BF16)
    dr["oT_loc"] = loc("oT_loc", [2048, T], BF16)
    oT_mine = loc("oT_mine", [2, 2048, ntok], BF16)
    for k in ("h2T_seq", "zT", "vat", "oT"):
        dr["_%s_b" % k] = Buf(k)
    shared = nc.dram_tensor("wshared", [D * DFF], F32, kind="Internal", addr_space="Shared").ap()
    shared_bf = shared.bitcast(BF16)
    shb = Buf("wshared")
    cx = Cx(nc)
    cx.want_pid = True
    wbufs = {}
    groups = [list(range(ncores))]
    with ExitStack() as es:
        R = alloc_common(nc, es, cx)
        R.pT = es.enter_context(nc.sbuf_tensor("sb_pT", [128, 2, TT], BF16))
        R.pTb = [Buf("pT0"), Buf("pT1")]
        R.gstage = [es.enter_context(nc.sbuf_tensor("sb_gst%d" % i, [128, 512], F32)) for i in range(4)]
        R.gstageb = [Buf("gst%d" % i) for i in range(4)]
        load_consts(cx, R, dr)

        def gather_weight(w):
            r, c = WSHAPE[w]
            bounce = loc(w + "_bn", [r // ncores, c])
            full = loc(w, [r, c])
            shv = shared[0:r * c].rearrange("(r c) -> r c", c=c)
            bb, wb = Buf(w + "_bn"), Buf(w)
            cx.dma("pool", bounce, shards[w], writes=[bb])
            cx.cc(lambda e: e.collective_compute("AllGather", ALU.bypass, replica_groups=groups, ins=[bounce], outs=[shv]),
                  reads=[bb], writes=[shb])
            cx.dma("sp", full, shv, reads=[shb], writes=[wb])
            dr[w] = full
            wbufs[w] = wb

        for w in WALL[:3]:
            gather_weight(w)
        w_in_mine = loc("w_in_mine_l", [D, NWC])
        dr["w_in_mine"] = w_in_mine
        winb = Buf("w_in_mine")
        for i in range(2):
            bounce = loc("w_in_bn%d" % i, [2048 // ncores, IN_PROJ])
            shv = shared[0:2048 * IN_PROJ].rearrange("(r c) -> r c", c=IN_PROJ)
            bb = Buf("w_in_bn%d" % i)
            cx.dma("pool", bounce, win_sh[i], writes=[bb])
            cx.cc(lambda e, bounce=bounce, shv=shv: e.collective_compute("AllGather", ALU.bypass, replica_groups=groups,
                                                                         ins=[bounce], outs=[shv]), reads=[bb], writes=[shb])
            rows = slice(i * 2048, (i + 1) * 2048)
            for gbase, mybase in WIN_GROUPS:
                def fn(e, gbase=gbase, mybase=mybase, rows=rows, shv=shv):
                    half = cx.pid % 2
                    return e.dma_start(out=w_in_mine[rows, mybase:mybase + 1024], in_=shv[:, ds(half * 1024 + gbase, 1024)])
                cx.op("pool", fn, reads=[shb], pwrites=[winb], dma=True)
            cx.dma("pool", w_in_mine[rows, WCOL["wl"]:WCOL["wl"] + 448], shv[:, WIN_LORA0:WIN_LORA0 + 448],
                   reads=[shb], pwrites=[winb])
        for w in WALL[3:]:
            gather_weight(w)

        xb, x1b, x2b, x3b, ob, fb, hlb, omb = (Buf(n) for n in ("x", "x1", "x2", "x3", "out", "f", "h2T_loc", "oT_mine"))
        ntile = ntok // TT
        for tt in range(ntile):
            ffn_block(cx, R, dr, "ffn1", dr["x"], xb, x1, x1b, tt * TT, fscr, fb, wbuf=wbufs["ffn1_w_down"])
            norm_stage(cx, R, x1, x1b, tt * TT, GIDX["mix_pre_g"])
            cx.dma("sp", h2T_loc[:, tt * TT:(tt + 1) * TT].rearrange("(kc p) t -> p kc t", p=128), R.hT[:],
                   reads=R.hTb, pwrites=[hlb])
        shv = shared_bf[0:ncores * D * ntok].rearrange("(r t) -> r t", t=ntok)
        cx.cc(lambda e: e.collective_compute("AllGather", ALU.bypass, replica_groups=groups, ins=[h2T_loc], outs=[shv]),
              reads=[hlb], writes=[shb])

        def fn_h(e):
            pair = cx.pid // 2
            src = shv.rearrange("(q r d) t -> q r d t", q=ncores // 2, r=2)[ds(pair, 1), :, :, :]
            return e.dma_start(out=dr["h2T_seq"], in_=src.rearrange("q r d t -> (q r) d t"))
        cx.op("pool", fn_h, reads=[shb], writes=[dr["_h2T_seq_b"]], dma=True)
        cx.barrier()
        proj_stage(cx, R, dr, T)
        A = attn_setup(cx, R, dr)
        for h in range(4):
            attn_head(cx, R, A, dr, h)
        cx.barrier()
        Wk = rwkv_setup(cx, R, dr)
        for ti in range(T // TT):
            rwkv_tile(cx, R, Wk, dr, ti)
        shv2 = shared_bf[0:ncores * 2048 * T].rearrange("(r t) -> r t", t=T)
        cx.cc(lambda e: e.collective_compute("AllGather", ALU.bypass, replica_groups=groups, ins=[dr["oT_loc"]], outs=[shv2]),
              reads=[dr["_oT_b"]], writes=[shb])

        def fn_o(e):
            pair, half = cx.pid // 2, cx.pid % 2
            src = shv2.rearrange("(q r c) t -> q r c t", q=ncores // 2, r=2)[ds(pair, 1), :, :, ds(half * ntok, ntok)]
            return e.dma_start(out=oT_mine, in_=src.rearrange("q r c t -> (q r) c t"))
        cx.op("pool", fn_o, reads=[shb], writes=[omb], dma=True)
        cx.barrier()
        for tt in range(ntile):
            for r in range(2):
                cx.dma("sp", R.hT[:, r * 16:(r + 1) * 16, :],
                       oT_mine[r, :, tt * TT:(tt + 1) * TT].rearrange("(j p) t -> p j t", p=128),
                       reads=[omb], writes=(R.hTb if r == 0 else []), pwrites=([] if r == 0 else R.hTb))
            lin_tok_stage(cx, R, R.hT, (lambda k, ts: R.hTb[ts]), KC, dr["w_out"], D, make_f_evac(cx, R, fscr, fb),
                          wbuf=wbufs["w_out"], rowmap=wout_rowmap)
            finalize_stage(cx, R, fscr, fb, dr["mix_post_g"], x1, x1b, tt * TT, x2, x2b, tt * TT, 1.0)
            ffn_block(cx, R, dr, "ffn2", x2, x2b, x3, x3b, tt * TT, fscr, fb, wbuf=wbufs["ffn2_w_down"])
            ple_block(cx, R, dr, x3, x3b, out, ob, tt * TT, fscr, fb, wbuf=wbufs["ple_w_proj"])
        cx.final_wait("sp", [ob])
        cx.emit(es)
    return nc, cx


def _core_params(inputs, hf):
    f = lambda k: np.asarray(inputs[k], dtype=np.float32)
    sl = slice(hf * 1024, (hf + 1) * 1024)
    mu = f("rwkv_mu").reshape(-1)
    m = {}
    m["lam4"] = np.concatenate([f("diff_lambda_q1").reshape(-1), f("diff_lambda_k1").reshape(-1),
                                f("diff_lambda_q2").reshape(-1), f("diff_lambda_k2").reshape(-1)]).reshape(1, 512)
    m["subln_g"] = f("diff_subln_g").reshape(1, 256)
    m["pcols"] = np.ascontiguousarray(np.stack([
        mu[0:2048][sl], mu[2048:4096][sl], mu[4096:6144][sl], f("rwkv_w0").reshape(-1)[sl], f("rwkv_a0").reshape(-1)[sl],
        f("rwkv_k_k").reshape(-1)[sl], f("rwkv_k_a").reshape(-1)[sl], f("rwkv_r_k").reshape(-1)[sl]]))
    pcl = np.zeros((4, 128), np.float32)
    pcl[0, :96] = mu[6144:6240]
    pcl[1, :96] = mu[6240:6336]
    pcl[2] = mu[6336:6464]
    pcl[3] = mu[6464:6592]
    m["pcl"] = pcl
    m["w2m"] = np.ascontiguousarray(f("rwkv_w2").reshape(96, 2048)[:, sl])
    m["a2m"] = np.ascontiguousarray(f("rwkv_a2").reshape(96, 2048)[:, sl])
    m["g2m"] = np.ascontiguousarray(f("rwkv_g2").reshape(256, 2048)[:, sl])
    tm = lambda v: np.ascontiguousarray(v.reshape(8, 2, 64).transpose(1, 0, 2).reshape(2, 512))
    m["lnw_t"] = tm(f("rwkv_ln_w").reshape(-1)[sl])
    m["lnb_t"] = tm(f("rwkv_ln_b").reshape(-1)[sl])
    return m


def kernel_cc(**inputs):
    ncores = NCORES
    x = np.ascontiguousarray(inputs["x"], dtype=np.float32).reshape(-1, D)
    p = np.ascontiguousarray(inputs["p"], dtype=np.float32).reshape(-1, 256)
    ntok = x.shape[0] // ncores
    assert ntok == NTOK
    nc, cx = build_mega_cc(ncores)
    consts = host_consts(SEQ)
    w_in = np.asarray(inputs["w_in"]).reshape(D, IN_PROJ)
    in_maps = []
    for c in range(ncores):
        hf = c % 2
        m = {"x": x[c * ntok:(c + 1) * ntok], "p": p[c * ntok:(c + 1) * ntok]}
        m.update(consts)
        for g in GALL:
            m[g] = np.ascontiguousarray(inputs[g], dtype=np.float32).reshape(1, D)
        for w in WALL:
            r, cdim = WSHAPE[w]
            wf = np.asarray(inputs[w]).reshape(r, cdim)
            m[w + "_sh"] = np.ascontiguousarray(wf[c * (r // ncores):(c + 1) * (r // ncores)], dtype=np.float32)
        for i in range(2):
            r0 = i * 2048 + c * 256
            m["w_in_sh%d" % i] = np.ascontiguousarray(w_in[r0:r0 + 256], dtype=np.float32)
        m.update(_core_params(inputs, hf))
        in_maps.append(m)
    res = run_bass_kernel_spmd(nc, in_maps, core_ids=list(range(ncores)))
    out = np.concatenate([r["out"] for r in res.results], axis=0)
    return out.reshape(inputs["x"].shape).astype(np.float32)


BPAR = ["lam4", "subln_g", "pcols", "pcl", "w2m", "a2m", "g2m", "lnw_t", "lnb_t"]
BPSHAPE = {"lam4": [1, 512], "subln_g": [1, 256], "pcols": [8, 1024], "pcl": [4, 128], "w2m": [96, 1024],
           "a2m": [96, 1024], "g2m": [256, 1024], "lnw_t": [2, 512], "lnb_t": [2, 512]}
WFULL = WALL + ["w_in"]
WSHAPE["w_in"] = (D, IN_PROJ)


def build_mega(ncores=NCORES):
    ntok = NTOK
    T = SEQ
    nc = bass.Bass("TRN2", target_bir_lowering=False)
    ext = lambda name, shape, dt=F32: nc.dram_tensor(name, shape, dt, kind="ExternalInput").ap()
    loc = lambda name, shape, dt=F32: nc.dram_tensor(name, shape, dt, kind="Internal").ap()
    dr = {}
    dr["xseq"] = ext("xseq", [T, D])
    dr["p"] = ext("p", [ntok, 256])
    dr["ident"] = ext("ident", [128, 128])
    for g in GALL:
        dr[g] = ext(g, [1, D])
    for w in WFULL:
        dr[w] = ext(w, list(WSHAPE[w]))
    for k, shp, dt in (("cosT", [32, T], F32), ("sinT", [32, T], F32), ("perm", [128, 128], F32),
                       ("amask", [4, 128, 512], BF16), ("rmask", [5, 128, 512], F32), ("bones", [128, 128], F32)):
        dr[k] = ext(k, shp, dt)
    drh = []
    for hfp in range(2):
        d2 = {}
        for k in BPAR:
            d2[k] = ext("%s_%d" % (k, hfp), BPSHAPE[k])
        drh.append(d2)
    out = nc.dram_tensor("out", [ntok, D], F32, kind="ExternalOutput").ap()
    x1f = loc("x1f", [T, D])
    x1, x2, x3 = loc("x1", [ntok, D]), loc("x2", [ntok, D]), loc("x3", [ntok, D])
    fscr = loc("fscr", [TT, D])
    dr["h2T_seq"] = loc("h2T_seq", [1, D, T], BF16)
    dr["zT"] = loc("zT", [NZ, T])
    dr["vat"] = loc("vat", [T, 1024], BF16)
    oT_loc = [loc("oT_loc%d" % i, [2048, T], BF16) for i in range(2)]
    oT_mine = loc("oT_mine", [2, 2048, ntok], BF16)
    for k in ("h2T_seq", "zT", "vat"):
        dr["_%s_b" % k] = Buf(k)
    oTb = [Buf("oT0"), Buf("oT1")]
    cx = Cx(nc)
    cx.want_pid = True
    with ExitStack() as es:
        R = alloc_common(nc, es, cx)
        R.pT = es.enter_context(nc.sbuf_tensor("sb_pT", [128, 2, TT], BF16))
        R.pTb = [Buf("pT0"), Buf("pT1")]
        R.gstage = [es.enter_context(nc.sbuf_tensor("sb_gst%d" % i, [128, 512], F32)) for i in range(4)]
        R.gstageb = [Buf("gst%d" % i) for i in range(4)]
        load_consts(cx, R, dr)
        xb, x1fb, x1b, x2b, x3b, ob, fb, omb = (Buf(n) for n in ("x", "x1f", "x1", "x2", "x3", "out", "f", "oT_mine"))
        for tt in range(T // TT):
            ffn_block(cx, R, dr, "ffn1", dr["xseq"], xb, x1f, x1fb, tt * TT, fscr, fb)
            norm_stage(cx, R, x1f, x1fb, tt * TT, GIDX["mix_pre_g"])
            cx.dma("sp", dr["h2T_seq"][0, :, tt * TT:(tt + 1) * TT].rearrange("(kc p) t -> p kc t", p=128), R.hT[:],
                   reads=R.hTb, pwrites=[dr["_h2T_seq_b"]])

        def fn_x(e):
            half = cx.pid % 2
            return e.dma_start(out=x1, in_=x1f[ds(half * ntok, ntok), :])
        cx.op("pool", fn_x, reads=[x1fb], writes=[x1b], dma=True)
        cx.barrier()
        for hfp in range(2):
            d2 = dict(dr)
            d2.update(drh[hfp])
            d2["w_in_mine"] = dr["w_in"]
            d2["oT_loc"] = oT_loc[hfp]
            d2["_oT_b"] = oTb[hfp]
            wc = {"q": hfp * 1024, "k": 2048 + hfp * 1024, "v": 4096 + hfp * 1024, "r": 6144 + hfp * 1024,
                  "rk": 8192 + hfp * 1024, "rv": 10240 + hfp * 1024, "wl": 12288, "al": 12384, "gl": 12480}
            proj_stage(cx, R, d2, T, WCOL=wc)
            A = attn_setup(cx, R, d2)
            for h in range(4):
                attn_head(cx, R, A, d2, h)
            cx.barrier()
            Wk = rwkv_setup(cx, R, d2)
            for ti in range(T // TT):
                rwkv_tile(cx, R, Wk, d2, ti)
            cx.barrier()

        def fn_o(e, r):
            half = cx.pid % 2
            return e.dma_start(out=oT_mine[r], in_=oT_loc[r][:, ds(half * ntok, ntok)])
        for r in range(2):
            cx.op("pool", (lambda e, r=r: fn_o(e, r)), reads=[oTb[r]], pwrites=[omb], dma=True)
        cx.barrier()
        for tt in range(ntok // TT):
            for r in range(2):
                cx.dma("sp", R.hT[:, r * 16:(r + 1) * 16, :],
                       oT_mine[r, :, tt * TT:(tt + 1) * TT].rearrange("(j p) t -> p j t", p=128),
                       reads=[omb], writes=(R.hTb if r == 0 else []), pwrites=([] if r == 0 else R.hTb))
            lin_tok_stage(cx, R, R.hT, (lambda k, ts: R.hTb[ts]), KC, dr["w_out"], D, make_f_evac(cx, R, fscr, fb),
                          rowmap=wout_rowmap)
            finalize_stage(cx, R, fscr, fb, dr["mix_post_g"], x1, x1b, tt * TT, x2, x2b, tt * TT, 1.0)
            ffn_block(cx, R, dr, "ffn2", x2, x2b, x3, x3b, tt * TT, fscr, fb)
            ple_block(cx, R, dr, x3, x3b, out, ob, tt * TT, fscr, fb)
        cx.final_wait("sp", [ob])
        cx.emit(es)
    return nc, cx


def kernel(**inputs):
    ncores = NCORES
    x = np.ascontiguousarray(inputs["x"], dtype=np.float32).reshape(-1, SEQ, D)
    p = np.ascontiguousarray(inputs["p"], dtype=np.float32).reshape(-1, 256)
    nc, cx = build_mega(ncores)
    consts = host_consts(SEQ)
    shared = dict(consts)
    for g in GALL:
        shared[g] = np.ascontiguousarray(inputs[g], dtype=np.float32).reshape(1, D)
    for w in WFULL:
        shared[w] = np.ascontiguousarray(np.asarray(inputs[w]).reshape(WSHAPE[w]), dtype=np.float32)
    for hfp in range(2):
        for k, v in _core_params(inputs, hfp).items():
            shared["%s_%d" % (k, hfp)] = v
    in_maps = []
    for c in range(ncores):
        m = dict(shared)
        m["xseq"] = x[c // 2]
        m["p"] = p[c * NTOK:(c + 1) * NTOK]
        in_maps.append(m)
    res = run_bass_kernel_spmd(nc, in_maps, core_ids=list(range(ncores)))
    out = np.concatenate([r["out"] for r in res.results], axis=0)
    return out.reshape(inputs["x"].shape).astype(np.float32)
```
